# Optimizing a Trainium2 kernel written in Bass

```python
import jax, jax.numpy as jnp
from jax import lax
import numpy as np

D_MODEL = 2048
BATCH = 16
SEQ = 2048
DEPTH = 4

N_BRANCH = 4
BRANCH_W = D_MODEL // 4
N_GROUPS = 4
GROUP_W = BRANCH_W // N_GROUPS
CHUNK = 128
SCONV_W = 3
N_HEADS = N_GROUPS
HEAD_DIM = GROUP_W
MOBA_BLOCK = 256
MOBA_TOPK = 3
Q_BLOCK = 128
ROPE_THETA = 10000.0
NEG = -1e30
LRU_CONV_W = 4
LRU_C = 8.0
D_FF = (((8 * D_MODEL + 2) // 3 + 255) // 256) * 256
EPS = 1e-6

A_OFF = 0
B_OFF = 2 * BRANCH_W
C_OFF = 5 * BRANCH_W
D_OFF = 8 * BRANCH_W
IN_W = 10 * BRANCH_W

kernel_name = "hybrid_gated_parallel_mixers"


def rmsnorm(x, g):
    xf = x.astype(jnp.float32)
    var = jnp.mean(xf * xf, axis=-1, keepdims=True)
    return (xf * lax.rsqrt(var + EPS)).astype(x.dtype) * g


def causal_depthwise_conv(x, w):
    K = w.shape[0]
    S = x.shape[1]
    xp = jnp.pad(x, ((0, 0), (K - 1, 0), (0, 0)))
    return sum(xp[:, k:k + S] * w[k] for k in range(K))


def rope(x, pos):
    half = x.shape[-1] // 2
    inv = ROPE_THETA ** (-jnp.arange(half, dtype=jnp.float32) / half)
    ang = pos.astype(jnp.float32)[:, None] * inv[None, :]
    cos = jnp.cos(ang)[None, :, None, :]
    sin = jnp.sin(ang)[None, :, None, :]
    xf = x.astype(jnp.float32)
    x1, x2 = xf[..., :half], xf[..., half:]
    return jnp.concatenate([x1 * cos - x2 * sin, x2 * cos + x1 * sin], axis=-1).astype(x.dtype)


def spatial_gating(u, v, w_s, b_s, g_v):
    Bn, S, _ = u.shape
    v = rmsnorm(v, g_v)
    mask = jnp.tril(jnp.ones((CHUNK, CHUNK), dtype=bool))
    w = jnp.where(mask[None], w_s, 0)
    vc = v.reshape(Bn, S // CHUNK, CHUNK, N_GROUPS, GROUP_W)
    mix = jnp.einsum('gts,bnsgc->bntgc', w, vc) + b_s[None, None, :, :, None]
    return u * mix.reshape(Bn, S, BRANCH_W)


def short_conv_mixer(b_gate, c_gate, xc, w_conv):
    return b_gate * causal_depthwise_conv(c_gate * xc, w_conv)


def moba_attention(q, k, v):
    Bn, S, H, Dh = q.shape
    nkb = -(-S // MOBA_BLOCK)
    pad = nkb * MOBA_BLOCK - S
    kp = jnp.pad(k, ((0, 0), (0, pad), (0, 0), (0, 0)))
    vp = jnp.pad(v, ((0, 0), (0, pad), (0, 0), (0, 0)))
    kb = kp.reshape(Bn, nkb, MOBA_BLOCK, H, Dh).transpose(0, 3, 1, 2, 4)
    vb = vp.reshape(Bn, nkb, MOBA_BLOCK, H, Dh).transpose(0, 3, 1, 2, 4)
    kmean = jnp.mean(kb.astype(jnp.float32), axis=3)
    own = jnp.arange(S) // MOBA_BLOCK
    gate = jnp.einsum('bshd,bhnd->bhsn', q.astype(jnp.float32), kmean)
    past = jnp.arange(nkb)[None, :] < own[:, None]
    gate = jnp.where(past[None, None], gate, NEG)
    topk = min(MOBA_TOPK, nkb)
    _, idx = lax.top_k(gate, topk)
    valid = idx < own[None, None, :, None]

    nqb = S // Q_BLOCK
    qs = q.reshape(Bn, nqb, Q_BLOCK, H, Dh).transpose(0, 1, 3, 2, 4).reshape(Bn * nqb, H, Q_BLOCK, Dh)
    idx_s = idx.reshape(Bn, H, nqb, Q_BLOCK, topk).transpose(0, 2, 1, 3, 4).reshape(Bn * nqb, H, Q_BLOCK, topk)
    val_s = valid.reshape(Bn, H, nqb, Q_BLOCK, topk).transpose(0, 2, 1, 3, 4).reshape(Bn * nqb, H, Q_BLOCK, topk)
    b_ids = jnp.repeat(jnp.arange(Bn), nqb)
    qb_ids = jnp.tile(jnp.arange(nqb), Bn)
    scale = Dh ** -0.5
    L = MOBA_BLOCK

    def one_block(args):
        qblk, idb, vld, b, qb = args
        kbb = kb[b]
        vbb = vb[b]
        kg = jax.vmap(lambda kh, ih: kh[ih])(kbb, idb)
        vg = jax.vmap(lambda vh, ih: vh[ih])(vbb, idb)
        s_sel = jnp.einsum('hqd,hqjld->hqjl', qblk, kg).astype(jnp.float32) * scale
        s_sel = jnp.where(vld[..., None], s_sel, NEG)
        qpos = qb * Q_BLOCK + jnp.arange(Q_BLOCK)
        own_blk = (qb * Q_BLOCK) // MOBA_BLOCK
        k_own = lax.dynamic_index_in_dim(kbb, own_blk, axis=1, keepdims=False)
        v_own = lax.dynamic_index_in_dim(vbb, own_blk, axis=1, keepdims=False)
        s_own = jnp.einsum('hqd,hld->hql', qblk, k_own).astype(jnp.float32) * scale
        kpos = own_blk * MOBA_BLOCK + jnp.arange(MOBA_BLOCK)
        s_own = jnp.where(kpos[None, None, :] <= qpos[None, :, None], s_own, NEG)
        s_all = jnp.concatenate([s_sel.reshape(H, Q_BLOCK, topk * L), s_own], axis=-1)
        p = jax.nn.softmax(s_all, axis=-1).astype(v.dtype)
        p_sel = p[..., :topk * L].reshape(H, Q_BLOCK, topk, L)
        p_own = p[..., topk * L:]
        return (jnp.einsum('hqjl,hqjld->hqd', p_sel, vg)
                + jnp.einsum('hql,hld->hqd', p_own, v_own))

    out = lax.map(one_block, (qs, idx_s, val_s, b_ids, qb_ids))
    return out.reshape(Bn, nqb, H, Q_BLOCK, Dh).transpose(0, 1, 3, 2, 4).reshape(Bn, S, H * Dh)


def rg_lru_mixer(xr, gate_in, conv_w, conv_b, w_a, b_a, w_x, b_x, lam):
    Bn, S, _ = xr.shape
    xr = causal_depthwise_conv(xr, conv_w) + conv_b
    xg = xr.reshape(Bn, S, N_GROUPS, GROUP_W)
    r = jax.nn.sigmoid(jnp.einsum('bsgi,gij->bsgj', xg, w_a).reshape(Bn, S, BRANCH_W) + b_a)
    i = jax.nn.sigmoid(jnp.einsum('bsgi,gij->bsgj', xg, w_x).reshape(Bn, S, BRANCH_W) + b_x)
    log_a = -LRU_C * r.astype(jnp.float32) * jax.nn.softplus(-lam.astype(jnp.float32))
    a = jnp.exp(log_a)
    bterm = jnp.sqrt(-jnp.expm1(2.0 * log_a)) * (i * xr).astype(jnp.float32)

    def comb(c1, c2):
        a1, b1 = c1
        a2, b2 = c2
        return a1 * a2, a2 * b1 + b2

    _, h = lax.associative_scan(comb, (a, bterm), axis=1)
    return jax.nn.gelu(gate_in) * h.astype(xr.dtype)


def setup_inputs(seed: int = 0) -> dict:
    key = jax.random.key(seed)
    ks = iter(jax.random.split(key, 32))

    def nrm(shape, scale):
        return jax.random.normal(next(ks), shape, jnp.float32) * scale

    def gain(shape):
        return 1.0 + 0.05 * jax.random.normal(next(ks), shape, jnp.float32)

    out_scale = (2 * DEPTH) ** -0.5
    x = jax.random.normal(next(ks), (BATCH, SEQ, D_MODEL), jnp.float32)
    u = jax.random.uniform(next(ks), (DEPTH, BRANCH_W), jnp.float32, 0.9, 0.999)
    a0 = u ** (1.0 / LRU_C)
    lru_lambda = jnp.log(a0) - jnp.log1p(-a0)
    return {
        "x": x,
        "g_mix": gain((DEPTH, D_MODEL)),
        "w_in": nrm((DEPTH, D_MODEL, IN_W), D_MODEL ** -0.5),
        "w_sgu": nrm((DEPTH, N_GROUPS, CHUNK, CHUNK), CHUNK ** -0.5),
        "b_sgu": gain((DEPTH, CHUNK, N_GROUPS)),
        "g_sgu": gain((DEPTH, BRANCH_W)),
        "w_sconv": nrm((DEPTH, SCONV_W, BRANCH_W), SCONV_W ** -0.5),
        "w_lru_conv": nrm((DEPTH, LRU_CONV_W, BRANCH_W), LRU_CONV_W ** -0.5),
        "b_lru_conv": nrm((DEPTH, BRANCH_W), 0.01),
        "w_lru_a": nrm((DEPTH, N_GROUPS, GROUP_W, GROUP_W), GROUP_W ** -0.5),
        "b_lru_a": nrm((DEPTH, BRANCH_W), 0.01),
        "w_lru_x": nrm((DEPTH, N_GROUPS, GROUP_W, GROUP_W), GROUP_W ** -0.5),
        "b_lru_x": nrm((DEPTH, BRANCH_W), 0.01),
        "lru_lambda": lru_lambda,
        "w_gate": nrm((DEPTH, N_BRANCH, D_MODEL, D_MODEL), D_MODEL ** -0.5),
        "b_gate": nrm((DEPTH, N_BRANCH, D_MODEL), 0.01),
        "w_branch": nrm((DEPTH, N_BRANCH, BRANCH_W, D_MODEL), BRANCH_W ** -0.5),
        "w_out": nrm((DEPTH, D_MODEL, D_MODEL), D_MODEL ** -0.5 * out_scale),
        "g_ffn": gain((DEPTH, D_MODEL)),
        "w_ffn1": nrm((DEPTH, D_MODEL, D_FF), D_MODEL ** -0.5),
        "w_ffn3": nrm((DEPTH, D_MODEL, D_FF), D_MODEL ** -0.5),
        "w_ffn2": nrm((DEPTH, D_FF, D_MODEL), D_FF ** -0.5 * out_scale),
        "g_final": gain((D_MODEL,)),
    }


def reference(x, g_mix, w_in, w_sgu, b_sgu, g_sgu, w_sconv, w_lru_conv, b_lru_conv,
              w_lru_a, b_lru_a, w_lru_x, b_lru_x, lru_lambda, w_gate, b_gate, w_branch,
              w_out, g_ffn, w_ffn1, w_ffn3, w_ffn2, g_final):
    Bn, S, _ = x.shape
    pos = jnp.arange(S)
    BW = BRANCH_W
    for l in range(DEPTH):
        xn = rmsnorm(x, g_mix[l])
        p = xn @ w_in[l]
        ua = jax.nn.gelu(p[..., A_OFF:A_OFF + BW])
        va = jax.nn.gelu(p[..., A_OFF + BW:A_OFF + 2 * BW])
        o_a = spatial_gating(ua, va, w_sgu[l], b_sgu[l], g_sgu[l])
        o_b = short_conv_mixer(p[..., B_OFF:B_OFF + BW], p[..., B_OFF + BW:B_OFF + 2 * BW],
                               p[..., B_OFF + 2 * BW:B_OFF + 3 * BW], w_sconv[l])
        q = rope(p[..., C_OFF:C_OFF + BW].reshape(Bn, S, N_HEADS, HEAD_DIM), pos)
        k = rope(p[..., C_OFF + BW:C_OFF + 2 * BW].reshape(Bn, S, N_HEADS, HEAD_DIM), pos)
        v = p[..., C_OFF + 2 * BW:C_OFF + 3 * BW].reshape(Bn, S, N_HEADS, HEAD_DIM)
        o_c = moba_attention(q, k, v)
        o_d = rg_lru_mixer(p[..., D_OFF:D_OFF + BW], p[..., D_OFF + BW:D_OFF + 2 * BW],
                           w_lru_conv[l], b_lru_conv[l], w_lru_a[l], b_lru_a[l],
                           w_lru_x[l], b_lru_x[l], lru_lambda[l])
        y = 0.0
        for bi, o in enumerate((o_a, o_b, o_c, o_d)):
            g = jax.nn.sigmoid(xn @ w_gate[l, bi] + b_gate[l, bi])
            y = y + g * (o @ w_branch[l, bi])
        x = x + y @ w_out[l]
        xn = rmsnorm(x, g_ffn[l])
        h = jax.nn.silu(xn @ w_ffn1[l]) * (xn @ w_ffn3[l])
        x = x + h @ w_ffn2[l]
    return rmsnorm(x, g_final)
```

```python
import numpy as np
from contextlib import ExitStack
import concourse.bass as bass
import concourse.mybir as mybir
from concourse.bass_utils import run_bass_kernel_spmd

F32 = mybir.dt.float32
BF16 = mybir.dt.bfloat16
AF = mybir.ActivationFunctionType
ALU = mybir.AluOpType
AX = mybir.AxisListType

D = 2048
S = 2048
T = 512
KC = 16
INW = 5120
DFF = 5632
BW = 512
NEG = -1e30
BIG = 1e30
EPS = 1e-6
NCORES = 8
SEQ_PER_CORE = 2
DEPTH = 4
SCALE = 128 ** -0.5


class Buf:
    __slots__ = ("name", "w", "r", "dsem", "dcnt")

    def __init__(self, name):
        self.name = name
        self.w = None
        self.r = {}
        self.dsem = None
        self.dcnt = 0


class Eng:
    def __init__(self, name, h, sem):
        self.name = name
        self.h = h
        self.sem = sem
        self.cnt = 0
        self.seen = {}


class FW:
    def __init__(self, nc, stack):
        self.nc = nc
        self.stack = stack
        self.engs = {}
        for n in ("pe", "act", "dve", "pool", "sp"):
            h = {"pe": nc.tensor, "act": nc.scalar, "dve": nc.vector, "pool": nc.gpsimd, "sp": nc.sync}[n]
            sem = stack.enter_context(nc.semaphore("sem_" + n))
            self.engs[n] = Eng(n, h, sem)
        self.nwait = 0
        self.nins = 0

    def _waits(self, E, reads, writes):
        need = {}

        def add(tok):
            sem, val = tok
            k = id(sem)
            if k not in need or need[k][1] < val:
                need[k] = (sem, val)
        for b in reads:
            if b.w is not None:
                add(b.w)
        for b in writes:
            if b.w is not None and b.w[0] is not E.sem:
                add(b.w)
            for tok in b.r.values():
                if tok[0] is not E.sem:
                    add(tok)
        for k, (sem, val) in need.items():
            if E.seen.get(k, 0) >= val:
                continue
            E.h.wait_ge(sem, val)
            E.seen[k] = val
            self.nwait += 1

    def _record(self, tok, reads, writes):
        k = id(tok[0])
        for b in reads:
            b.r[k] = tok
        for b in writes:
            b.w = tok
            b.r = {}

    def op(self, eng, fn, reads=(), writes=(), mark=True):
        E = self.engs[eng]
        self._waits(E, reads, writes)
        ins = fn(E.h)
        self.nins += 1
        if mark:
            E.cnt += 1
            ins.then_inc(E.sem, 1)
            self._record((E.sem, E.cnt), reads, writes)
        return ins

    def dma(self, q, out_ap, in_ap, src, dst, sembuf=None):
        E = self.engs[q]
        sb = sembuf if sembuf is not None else dst
        if sb.dsem is None:
            sb.dsem = self.stack.enter_context(self.nc.semaphore("d_" + sb.name))
        same_fill = (dst.w is not None and dst.w[0] is sb.dsem and not dst.r)
        if same_fill:
            saved = dst.w
            dst.w = None
            self._waits(E, [src], [dst])
            dst.w = saved
        else:
            self._waits(E, [src], [dst])
        sb.dcnt += 16
        ins = E.h.dma_start(out=out_ap, in_=in_ap)
        ins.then_inc(sb.dsem, 16)
        self.nins += 1
        self._record((sb.dsem, sb.dcnt), [src], [dst])
        return ins

    def finish(self, eng, bufs):
        E = self.engs[eng]
        self._waits(E, list(bufs), list(bufs))


def pp_layout(L):
    off = {}
    n = 0
    for name, w in [("gmix", L * 16), ("gffn", L * 16), ("gfin", 16), ("bgate", L * 64), ("wsconv", L * 12),
                    ("wlconv", L * 16), ("blconv", L * 4), ("bla", L * 4), ("blx", L * 4), ("lam", L * 4)]:
        off[name] = n
        n += w
    return off, n


def _cols(a):
    return np.ascontiguousarray(np.asarray(a, np.float32).reshape(-1, 128).T)


def host_consts():
    half = 64
    inv = (10000.0 ** (-(np.arange(half, dtype=np.float32)) / np.float32(half))).astype(np.float32)
    ang = np.arange(S, dtype=np.float32)[None, :] * inv[:, None]
    cos = np.cos(ang).astype(np.float32)
    sin = np.sin(ang).astype(np.float32)
    rope = np.concatenate([np.concatenate([cos, cos], 0), np.concatenate([-sin, sin], 0)], axis=1)
    perm = np.zeros((128, 128), np.float32)
    for m in range(128):
        perm[(m + 64) % 128, m] = 1.0
    triu = (np.arange(128)[:, None] <= np.arange(128)[None, :]).astype(np.float32)
    q = np.arange(128)[:, None]
    key = np.arange(256)[None, :]
    causal = np.concatenate([np.where(key <= q, 0.0, NEG), np.where(key <= 128 + q, 0.0, NEG)], axis=1).astype(np.float32)
    pastneg = np.zeros((8, 8), np.float32)
    for ob in range(8):
        for n in range(8):
            pastneg[ob, n] = 0.0 if n < ob else NEG
    pastneg = np.broadcast_to(pastneg.reshape(1, 64), (128, 64)).astype(np.float32)
    ident = np.eye(128, dtype=np.float32)
    cst = np.concatenate([perm, causal, pastneg, triu, ident], axis=1)
    return np.ascontiguousarray(rope.astype(np.float32)), np.ascontiguousarray(cst)


def host_params(inp, L):
    off, npp = pp_layout(L)
    pp = np.zeros((128, npp), np.float32)

    def put(name, a):
        c = _cols(a)
        pp[:, off[name]:off[name] + c.shape[1]] = c
    put("gmix", inp["g_mix"][:L])
    put("gffn", inp["g_ffn"][:L])
    put("gfin", inp["g_final"])
    put("bgate", inp["b_gate"][:L])
    put("wsconv", inp["w_sconv"][:L])
    put("wlconv", inp["w_lru_conv"][:L])
    put("blconv", inp["b_lru_conv"][:L])
    put("bla", inp["b_lru_a"][:L])
    put("blx", inp["b_lru_x"][:L])
    put("lam", inp["lru_lambda"][:L])
    wla = np.asarray(inp["w_lru_a"][:L], np.float32).transpose(2, 0, 1, 3).reshape(128, -1)
    wlx = np.asarray(inp["w_lru_x"][:L], np.float32).transpose(2, 0, 1, 3).reshape(128, -1)
    wsg = np.asarray(inp["w_sgu"][:L], np.float32).transpose(3, 0, 1, 2).reshape(128, -1)
    mats = np.ascontiguousarray(np.concatenate([wla, wlx, wsg], axis=1))
    bsgu = np.asarray(inp["b_sgu"][:L], np.float32).transpose(0, 2, 1).reshape(1, -1)
    gsgu = np.asarray(inp["g_sgu"][:L], np.float32).reshape(1, -1)
    rows = np.ascontiguousarray(np.concatenate([bsgu, gsgu], axis=1))
    return pp, mats, rows


WNAMES = ["w_in", "w_gate", "w_branch", "w_out", "w_ffn1", "w_ffn3", "w_ffn2"]


def wshape(name, L):
    return {"w_in": [L, D, INW], "w_gate": [L, 4, D, D], "w_branch": [L, 4, BW, D], "w_out": [L, D, D],
            "w_ffn1": [L, D, DFF], "w_ffn3": [L, D, DFF], "w_ffn2": [L, DFF, D]}[name]


def build(L=DEPTH, NSEQ=SEQ_PER_CORE, NT=S // T, debug=False):
    nc = bass.Bass("TRN2", target_bir_lowering=False)
    off, npp = pp_layout(L)
    CHK = 2048

    xT_d = nc.dram_tensor("xT", [NSEQ, D, S], F32, kind="ExternalInput").ap()
    oT_d = nc.dram_tensor("oT", [NSEQ, D, S], F32, kind="ExternalOutput").ap()
    pp_d = nc.dram_tensor("pp", [128, npp], F32, kind="ExternalInput").ap()
    mats_d = nc.dram_tensor("mats", [128, 3 * L * 512], F32, kind="ExternalInput").ap()
    rows_d = nc.dram_tensor("rows", [1, 2 * L * 512], F32, kind="ExternalInput").ap()
    rope_d = nc.dram_tensor("rope", [128, 2 * S], F32, kind="ExternalInput").ap()
    cst_d = nc.dram_tensor("cst", [128, 960], F32, kind="ExternalInput").ap()
    wsrc = {}
    wbf = {}
    for n in WNAMES:
        shp = wshape(n, L)
        E = int(np.prod(shp))
        wsrc[n] = nc.dram_tensor(n, [L, 128, E // L // 128], F32, kind="ExternalInput").ap()
        wbf[n] = nc.dram_tensor(n + "_bf", [E], BF16).ap()
    xscr = (nc.dram_tensor("xscr", [NSEQ, D, S], F32, kind="ExternalOutput") if debug else nc.dram_tensor("xscr", [NSEQ, D, S], F32)).ap()
    dbg_d = nc.dram_tensor("dbg", [128, 16 * 512], BF16, kind="ExternalOutput").ap() if debug else None

    def wview(n):
        shp = wshape(n, L)
        if len(shp) == 3:
            return wbf[n].rearrange("(l r c) -> l r c", l=shp[0], r=shp[1])
        return wbf[n].rearrange("(l k r c) -> l k r c", l=shp[0], k=shp[1], r=shp[2])

    with ExitStack() as st:
        fw = FW(nc, st)

        def sb(name, shape, dt=F32):
            return st.enter_context(nc.sbuf_tensor(name, shape, dt))

        X = sb("X", [128, KC, T])
        XB = [Buf("X%d" % i) for i in range(KC)]
        XN = sb("XN", [128, KC, T], BF16)
        XNBs = [Buf("XN%d" % i) for i in range(KC)]

        def xnr(kc, last=False):
            return list(XNBs) if last else [XNBs[kc]]
        NSLOT = 4
        WS = [sb("WS%d" % i, [128, 16, 256], BF16) for i in range(NSLOT)]
        WSB = [Buf("WS%d" % i) for i in range(NSLOT)]
        WBS2 = [sb("WBS%d" % i, [128, 16, 256], BF16) for i in range(2)]
        WBSB2 = [Buf("WBS%d" % i) for i in range(2)]
        KCt = sb("KC", [128, 4, S], BF16)
        KCB = Buf("KC")
        VCt = sb("VC", [128, 16, 512], BF16)
        VCB = Buf("VC")
        PP = sb("PP", [128, npp])
        PPB = Buf("PP")
        CST = sb("CST", [128, 704])
        CSTB = Buf("CST")
        IDB = sb("IDB", [128, 128], BF16)
        IDBB = Buf("IDB")
        ONEB = sb("ONEB", [128, 128], BF16)
        ONE32 = sb("ONE32", [128, 128])
        ONESB = Buf("ONES")
        WLA = sb("WLA", [128, 512], BF16)
        WLX = sb("WLX", [128, 512], BF16)
        WSG = sb("WSG", [128, 512], BF16)
        LWB = Buf("LW")
        BSG = sb("BSG", [1, 1024], BF16)
        GSG = sb("GSG", [128, 512])
        LRB = Buf("LR")
        SPt = sb("SPt", [128, 16])
        SPB = Buf("SP")
        CS = sb("CS", [128, 2, T])
        CSB = Buf("CS")
        ZH = sb("ZH", [128, 4, 2])
        ZHB = [Buf("ZH%d" % i) for i in range(4)]
        XRH = sb("XRH", [128, 4, 3])
        XRHB = [Buf("XRH%d" % i) for i in range(4)]
        HC = sb("HC", [128, 4])
        HCB = [Buf("HC%d" % i) for i in range(4)]
        KM = sb("KM", [128, 4, 8])
        KMB = Buf("KM")
        SM = sb("SM", [128, 160])
        GMB, TOPB, SELB, MBB = Buf("GM"), Buf("TOP"), Buf("SEL"), Buf("MBm")
        MXB = [Buf("MX%d" % i) for i in range(4)]
        RSB = [Buf("RS%d" % i) for i in range(4)]
        RINV = sb("RINV", [128, 2, 128])
        RINVB = [Buf("RINV0"), Buf("RINV1")]

        NAR = 28
        AR = sb("AR", [128, NAR, 516])
        ARB = [Buf("AR%d" % i) for i in range(NAR)]

        def a32(i, n=512, o=0):
            return AR[:, i, o:o + n]

        def abf(i):
            return AR[:, i, :].bitcast(BF16)

        def abf2(i):
            return AR[:, i:i + 2, :].rearrange("p a b -> p (a b)").bitcast(BF16)

        TM0 = 22

        def tmp(k):
            return TM0 + k

        def o_ap(k, c):
            return abf(k * 2 + c // 2)[:, (c % 2) * 512:(c % 2) * 512 + 512]

        def o_buf(k, c):
            return ARB[k * 2 + c // 2]

        PS = st.enter_context(nc.psum_tensor("PS", [128, 4096], F32))
        PSB = [Buf("PS%d" % i) for i in range(8)]
        rot = {"i": 0}

        def bank(allowed=(0, 1, 2, 3, 4, 5, 6, 7)):
            while True:
                b = rot["i"] % 8
                rot["i"] += 1
                if b in allowed:
                    return b

        def psb(b, n=512, o=0):
            return PS[:, b * 512 + o:b * 512 + o + n]

        wrot = {"i": 0}
        DW = Buf("DW")

        def load_w(src_fn_list, dep):
            i = wrot["i"] % NSLOT
            wrot["i"] += 1
            for ap, kc0, n in src_fn_list:
                fw.dma("sp", WS[i][:, kc0:kc0 + n, :], ap, dep, WSB[i])
            return WS[i], WSB[i]

        def wpieces(ap2d, nkc):
            v = ap2d.rearrange("(kc p) c -> p kc c", p=128)
            out = []
            k = 0
            while k < nkc:
                n = min(8, nkc - k)
                out.append((v[:, k:k + n, :], k, n))
                k += n
            return out

        def mm(out, lhsT, rhs, start, stop, reads, writes, mark):
            return fw.op("pe", lambda h: h.matmul(out, lhsT=lhsT, rhs=rhs, start=start, stop=stop), reads, writes, mark)

        DIN = Buf("DIN")
        fw.dma("pool", PP[:], pp_d[:, :], DIN, PPB)
        fw.dma("pool", CST[:], cst_d[:, 0:704], DIN, CSTB)
        fw.dma("pool", a32(0, 256), cst_d[:, 704:960], DIN, ARB[0])
        fw.op("dve", lambda h: h.tensor_copy(out=IDB[:], in_=a32(0, 128, 128)), [ARB[0]], [IDBB])
        fw.op("dve", lambda h: h.memset(ONEB[:], 1.0), [], [ONESB])
        fw.op("dve", lambda h: h.memset(ONE32[:], 1.0), [], [ONESB])
        TRI = sb("TRI", [128, 128])
        TRIB = Buf("TRI")
        fw.op("dve", lambda h: h.tensor_copy(out=TRI[:], in_=a32(0, 128, 0)), [ARB[0]], [TRIB])

        CVS = {n: Buf("cvs_" + n) for n in WNAMES}
        CVW = {n: [Buf("cvw_%s_%d" % (n, l)) for l in range(L)] for n in WNAMES}
        CVF = 8192
        cvjobs = []
        for l in range(L):
            for n in ["w_in", "w_branch", "w_gate", "w_out", "w_ffn1", "w_ffn3", "w_ffn2"]:
                El = int(np.prod(wshape(n, L))) // L
                per = El // 128
                assert per % CVF == 0
                for c in range(per // CVF):
                    cvjobs.append((l, n, c, El))
        cvpos = {"i": 0}

        def cv_issue(upto_layer=None, count=None):
            done = 0
            while cvpos["i"] < len(cvjobs):
                l, n, c, El = cvjobs[cvpos["i"]]
                if upto_layer is not None and l > upto_layer:
                    break
                if count is not None and done >= count:
                    break
                dst = wbf[n][l * El:(l + 1) * El].rearrange("(p f) -> p f", p=128)[:, c * CVF:(c + 1) * CVF]
                fw.dma("pool", dst, wsrc[n][l, :, c * CVF:(c + 1) * CVF], DIN, CVW[n][l], sembuf=CVS[n])
                cvpos["i"] += 1
                done += 1
        cv_issue(upto_layer=0, count=10)

        def ppc(name, idx):
            return PP[:, off[name] + idx:off[name] + idx + 1]

        def rms_stats(rs_slot):
            b = bank()
            for kc in range(KC):
                sq = tmp(kc % 2)
                fw.op("act", lambda h: h.activation(out=a32(sq), in_=X[:, kc, :], func=AF.Square), [XB[kc]], [ARB[sq]])
                mm(psb(b), ONE32[:], a32(sq), kc == 0, kc == KC - 1, [ONESB, ARB[sq]], [PSB[b]], True)
            fw.op("act", lambda h: h.activation(out=a32(rs_slot), in_=psb(b), func=AF.Sqrt, scale=1.0 / D, bias=EPSC[:, 0:1]),
                  [PSB[b], EPSB], [ARB[rs_slot]])
            fw.op("dve", lambda h: h.reciprocal(out=a32(rs_slot), in_=a32(rs_slot)), [ARB[rs_slot]], [ARB[rs_slot]])

        EPSC = sb("EPSC", [128, 2])
        EPSB = Buf("EPS")
        fw.op("dve", lambda h: h.memset(EPSC[:, 0:1], EPS), [], [EPSB])
        fw.op("dve", lambda h: h.memset(EPSC[:, 1:2], 1.0), [], [EPSB])

        def norm_to_xn(gname, l):
            rs = tmp(2)
            rms_stats(rs)
            for kc in range(KC):
                fw.op("dve", lambda h: h.scalar_tensor_tensor(out=XN[:, kc, :], in0=X[:, kc, :], scalar=ppc(gname, l * 16 + kc),
                                                                in1=a32(rs), op0=ALU.mult, op1=ALU.mult),
                      [XB[kc], PPB, ARB[rs]], [XNBs[kc]])

        def dense_fm(src2d, nkc, rhs_fn, rhs_bufs, dep, allowed=(0, 1, 2, 3, 4, 5, 6, 7)):
            slot, sB = load_w(wpieces(src2d, nkc), dep)
            outs = []
            for mi in range(2):
                b = bank(allowed)
                for kc in range(nkc):
                    rb = rhs_bufs(kc, kc == nkc - 1) if callable(rhs_bufs) else rhs_bufs
                    mm(psb(b), slot[:, kc, mi * 128:(mi + 1) * 128], rhs_fn(kc), kc == 0, kc == nkc - 1,
                       [sB] + rb, [PSB[b]], kc == nkc - 1)
                outs.append(b)
            return outs

        def xn_rhs(kc):
            return XN[:, kc, :]

        XRG = [[[Buf("XRG%d_%d_%d" % (s, j, kc)) for kc in range(KC)] for j in range(NT)] for s in range(NSEQ)]
        ORG = [[[Buf("ORG%d_%d_%d" % (s, j, kc)) for kc in range(KC)] for j in range(NT)] for s in range(NSEQ)]
        tiles = [(l, s, j) for l in range(L) for s in range(NSEQ) for j in range(NT)]

        def load_x_chunk(l, s, j, kc):
            if l == 0:
                fw.dma("act", X[:, kc, :], xT_d[s, kc * 128:(kc + 1) * 128, j * T:(j + 1) * T], DIN, XB[kc])
            else:
                fw.dma("act", X[:, kc, :], xscr[s, kc * 128:(kc + 1) * 128, j * T:(j + 1) * T], XRG[s][j][kc], XB[kc])

        def store_x_chunk(l, s, j, kc):
            if l == L - 1:
                fw.dma("act", oT_d[s, kc * 128:(kc + 1) * 128, j * T:(j + 1) * T], X[:, kc, :], XB[kc], ORG[s][j][kc], sembuf=XB[kc])
            else:
                fw.dma("act", xscr[s, kc * 128:(kc + 1) * 128, j * T:(j + 1) * T], X[:, kc, :], XB[kc], XRG[s][j][kc], sembuf=XB[kc])

        def layer_prep(l):
            s0 = tmp(3)
            for idx, dstt in enumerate((WLA, WLX)):
                fw.dma("pool", a32(s0), mats_d[:, (idx * L + l) * 512:(idx * L + l + 1) * 512], DIN, ARB[s0])
                fw.op("dve", lambda h: h.tensor_copy(out=dstt[:], in_=a32(s0)), [ARB[s0]], [LWB])
            fw.dma("pool", a32(s0), mats_d[:, (2 * L + l) * 512:(2 * L + l + 1) * 512], DIN, ARB[s0])
            fw.op("dve", lambda h: h.tensor_tensor(out=WSG[:].rearrange("p (g t) -> p g t", g=4),
                                                    in0=a32(s0).rearrange("p (g t) -> p g t", g=4),
                                                    in1=TRI[:].unsqueeze(1).to_broadcast([128, 4, 128]), op=ALU.mult),
                  [ARB[s0], TRIB], [LWB])
            s1 = tmp(4)
            fw.dma("pool", AR[0:1, s1, 0:512], rows_d[0:1, l * 512:(l + 1) * 512], DIN, ARB[s1])
            fw.op("dve", lambda h: h.tensor_copy(out=BSG[0:1, 0:512], in_=AR[0:1, s1, 0:512]), [ARB[s1]], [LRB])
            s2 = tmp(5)
            fw.op("dve", lambda h: h.tensor_copy(out=AR[0:1, s2, 0:512], in_=BSG[0:1, 0:512]), [LRB], [ARB[s2]])
            fw.op("dve", lambda h: h.tensor_tensor(out=BSG[0:1, 512:1024], in0=AR[0:1, s1, 0:512], in1=AR[0:1, s2, 0:512], op=ALU.subtract),
                  [ARB[s1], ARB[s2]], [LRB])
            fw.dma("pool", GSG[:], rows_d[0:1, (L + l) * 512:(L + l + 1) * 512].partition_broadcast(128), DIN, LRB)
            lam = PP[:, off["lam"] + l * 4:off["lam"] + l * 4 + 4]
            z = SPt[:, 0:4]
            sp = SPt[:, 4:8]
            t8 = SPt[:, 8:12]
            t16 = SPt[:, 12:16]
            fw.op("act", lambda h: h.activation(out=z, in_=lam, func=AF.Exp, scale=-1.0), [PPB], [SPB])
            fw.op("dve", lambda h: h.tensor_scalar(out=sp, in0=z, scalar1=-0.2, scalar2=0.25, op0=ALU.mult, op1=ALU.add), [SPB], [SPB])
            for cst in (1.0 / 3.0, 0.5, 1.0):
                fw.op("dve", lambda h: h.tensor_tensor(out=sp, in0=sp, in1=z, op=ALU.mult), [SPB], [SPB])
                fw.op("dve", lambda h: h.tensor_scalar(out=sp, in0=sp, scalar1=-1.0, scalar2=cst, op0=ALU.mult, op1=ALU.add), [SPB], [SPB])
            fw.op("dve", lambda h: h.tensor_tensor(out=sp, in0=sp, in1=z, op=ALU.mult), [SPB], [SPB])
            fw.op("act", lambda h: h.activation(out=t8, in_=z, func=AF.Ln, bias=EPSC[:, 1:2], scale=1.0), [SPB, EPSB], [SPB])
            fw.op("dve", lambda h: h.tensor_scalar(out=t16, in0=z, scalar1=0.05, scalar2=None, op0=ALU.is_ge), [SPB], [SPB])
            fw.op("dve", lambda h: h.tensor_tensor(out=t8, in0=t8, in1=sp, op=ALU.subtract), [SPB], [SPB])
            fw.op("dve", lambda h: h.tensor_tensor(out=t8, in0=t8, in1=t16, op=ALU.mult), [SPB], [SPB])
            fw.op("dve", lambda h: h.tensor_tensor(out=sp, in0=sp, in1=t8, op=ALU.add), [SPB], [SPB])
            fw.op("dve", lambda h: h.tensor_scalar(out=t8, in0=sp, scalar1=-8.0, scalar2=None, op0=ALU.mult), [SPB], [SPB])
            fw.op("dve", lambda h: h.tensor_scalar(out=t16, in0=sp, scalar1=-16.0, scalar2=None, op0=ALU.mult), [SPB], [SPB])

        def seq_reset():
            for c in range(4):
                fw.op("pool", lambda h: h.memset(ZH[:, c, :], 0.0), [], [ZHB[c]])
                fw.op("pool", lambda h: h.memset(XRH[:, c, :], 0.0), [], [XRHB[c]])
                fw.op("pool", lambda h: h.memset(HC[:, c:c + 1], 0.0), [], [HCB[c]])
            fw.op("pool", lambda h: h.memset(KM[:], 0.0), [], [KMB])

        def tile(ti):
            l, s, j = tiles[ti]
            WIN = wview("w_in")
            WG = wview("w_gate")
            WBR = wview("w_branch")
            WO = wview("w_out")
            W1 = wview("w_ffn1")
            W3 = wview("w_ffn3")
            W2 = wview("w_ffn2")
            tok0 = j * T

            fw.dma("pool", CS[:, 0, :], rope_d[:, tok0:tok0 + T], DIN, CSB)
            fw.dma("pool", CS[:, 1, :], rope_d[:, S + tok0:S + tok0 + T], DIN, CSB)

            norm_to_xn("gmix", l)
            if ti == 0:
                cv_issue(upto_layer=0, count=20)

            def win_slot(col0, allowed=(0, 1, 2, 3, 4, 5, 6, 7)):
                return dense_fm(WIN[l, :, col0:col0 + 256], KC, xn_rhs, xnr, CVW["w_in"][l], allowed)

            def d_cv(c):
                return 8 + c

            def d_cvb(c):
                return abf(12 + c // 2)[:, (c % 2) * 512:(c % 2) * 512 + 512], ARB[12 + c // 2]
            for i2 in range(2):
                bx = win_slot(4096 + i2 * 256)
                for mi in range(2):
                    c = i2 * 2 + mi
                    cv = d_cv(c)
                    cvb_ap, cvbB = d_cvb(c)
                    xin = 14 + (c % 2)
                    b = bx[mi]
                    fw.op("pool", lambda h: h.tensor_copy(out=AR[:, xin, 0:3], in_=XRH[:, c, :]), [XRHB[c]], [ARB[xin]])
                    fw.op("act", lambda h: h.copy(out=AR[:, xin, 3:515], in_=psb(b)), [PSB[b]], [ARB[xin]])
                    wl = lambda tap: ppc("wlconv", (l * 4 + tap) * 4 + c)
                    fw.op("dve", lambda h: h.tensor_scalar(out=a32(cv), in0=AR[:, xin, 0:512], scalar1=wl(0), scalar2=ppc("blconv", l * 4 + c),
                                                            op0=ALU.mult, op1=ALU.add), [ARB[xin], PPB], [ARB[cv]])
                    for tap in (1, 2, 3):
                        fw.op("dve", lambda h: h.scalar_tensor_tensor(out=a32(cv), in0=AR[:, xin, tap:tap + 512], scalar=wl(tap), in1=a32(cv),
                                                                       op0=ALU.mult, op1=ALU.add), [ARB[xin], PPB, ARB[cv]], [ARB[cv]])
                    fw.op("pool", lambda h: h.tensor_copy(out=XRH[:, c, :], in_=AR[:, xin, 512:515]), [ARB[xin]], [XRHB[c]])
                    fw.op("pool", lambda h: h.tensor_copy(out=cvb_ap, in_=a32(cv)), [ARB[cv]], [cvbB])

            for i2 in range(2):
                bxc = win_slot(2048 + i2 * 256)
                bcg = win_slot(1536 + i2 * 256)
                bbg = win_slot(1024 + i2 * 256)
                for mi in range(2):
                    c = i2 * 2 + mi
                    xc32 = tmp(0 + 3 * (c % 2))
                    acc = tmp(1 + 3 * (c % 2))
                    zz = tmp(2 + 3 * (c % 2))
                    fw.op("pool", lambda h: h.tensor_copy(out=AR[:, zz, 0:2], in_=ZH[:, c, :]), [ZHB[c]], [ARB[zz]])
                    b = bxc[mi]
                    fw.op("act", lambda h: h.copy(out=a32(xc32), in_=psb(b)), [PSB[b]], [ARB[xc32]])
                    b = bcg[mi]
                    fw.op("dve", lambda h: h.tensor_tensor(out=AR[:, zz, 2:514], in0=psb(b), in1=a32(xc32), op=ALU.mult), [PSB[b], ARB[xc32]], [ARB[zz]])
                    ws = lambda tap: ppc("wsconv", (l * 3 + tap) * 4 + c)
                    fw.op("dve", lambda h: h.tensor_scalar(out=a32(acc), in0=AR[:, zz, 0:512], scalar1=ws(0), scalar2=None, op0=ALU.mult), [ARB[zz], PPB], [ARB[acc]])
                    for tap in (1, 2):
                        fw.op("dve", lambda h: h.scalar_tensor_tensor(out=a32(acc), in0=AR[:, zz, tap:tap + 512], scalar=ws(tap), in1=a32(acc),
                                                                       op0=ALU.mult, op1=ALU.add), [ARB[zz], PPB, ARB[acc]], [ARB[acc]])
                    fw.op("pool", lambda h: h.tensor_copy(out=ZH[:, c, :], in_=AR[:, zz, 512:514]), [ARB[zz]], [ZHB[c]])
                    b = bbg[mi]
                    fw.op("dve", lambda h: h.tensor_tensor(out=o_ap(1, c), in0=psb(b), in1=a32(acc), op=ALU.mult), [PSB[b], ARB[acc]], [o_buf(1, c)])

            if ti == 0:
                cv_issue(upto_layer=0, count=26)
            bgs = []
            for i2 in range(2):
                bgs += win_slot(4608 + i2 * 256, (4, 5, 6, 7))
            d_ra = lambda c: 16 + c
            d_a2 = lambda c: 20 + c
            d_ib = lambda c: 24 + c
            brs, bis = [], []
            for c in range(4):
                cvb_ap, cvbB = d_cvb(c)
                br = bank((0, 1, 2, 3))
                mm(psb(br), WLA[:, c * 128:(c + 1) * 128], cvb_ap, True, True, [LWB, cvbB], [PSB[br]], True)
                brs.append(br)
            for c in range(4):
                ra = d_ra(c)
                fw.op("act", lambda h: h.activation(out=a32(ra), in_=psb(brs[c]), func=AF.Sigmoid, bias=ppc("bla", l * 4 + c)), [PSB[brs[c]], PPB], [ARB[ra]])
            for c in range(4):
                cvb_ap, cvbB = d_cvb(c)
                bi = bank((0, 1, 2, 3))
                mm(psb(bi), WLX[:, c * 128:(c + 1) * 128], cvb_ap, True, True, [LWB, cvbB], [PSB[bi]], True)
                bis.append(bi)
            for c in range(4):
                ibt = d_ib(c)
                fw.op("act", lambda h: h.activation(out=a32(ibt), in_=psb(bis[c]), func=AF.Sigmoid, bias=ppc("blx", l * 4 + c)), [PSB[bis[c]], PPB], [ARB[ibt]])
            for c in range(4):
                ra, a2s = d_ra(c), d_a2(c)
                fw.op("act", lambda h: h.activation(out=a32(a2s), in_=a32(ra), func=AF.Exp, scale=SPt[:, 12 + c:13 + c]), [ARB[ra], SPB], [ARB[a2s]])
                fw.op("act", lambda h: h.activation(out=a32(ra), in_=a32(ra), func=AF.Exp, scale=SPt[:, 8 + c:9 + c]), [ARB[ra], SPB], [ARB[ra]])
                fw.op("dve", lambda h: h.tensor_tensor(out=a32(d_ib(c)), in0=a32(d_ib(c)), in1=a32(d_cv(c)), op=ALU.mult), [ARB[d_ib(c)], ARB[d_cv(c)]], [ARB[d_ib(c)]])
            for c in range(4):
                a2s, ibt = d_a2(c), d_ib(c)
                fw.op("act", lambda h: h.activation(out=a32(a2s), in_=a32(a2s), func=AF.Sqrt, scale=-1.0, bias=EPSC[:, 1:2]), [ARB[a2s], EPSB], [ARB[a2s]])
                fw.op("dve", lambda h: h.tensor_tensor(out=a32(ibt), in0=a32(ibt), in1=a32(a2s), op=ALU.mult), [ARB[ibt], ARB[a2s]], [ARB[ibt]])
            for c in range(4):
                ra, hh, ibt = d_ra(c), d_a2(c), d_ib(c)
                fw.op("dve", lambda h: h.tensor_tensor_scan(out=a32(hh), data0=a32(ra), data1=a32(ibt), initial=HC[:, c:c + 1],
                                                             op0=ALU.mult, op1=ALU.add), [ARB[ra], ARB[ibt], HCB[c]], [ARB[hh]])
                fw.op("pool", lambda h: h.tensor_copy(out=HC[:, c:c + 1], in_=a32(hh, 1, 511)), [ARB[hh]], [HCB[c]])
            for c in range(4):
                gg, hh = d_ra(c), d_a2(c)
                b2 = bgs[c]
                fw.op("act", lambda h: h.activation(out=a32(gg), in_=psb(b2), func=AF.Gelu_apprx_tanh), [PSB[b2]], [ARB[gg]])
                fw.op("dve", lambda h: h.tensor_tensor(out=o_ap(3, c), in0=a32(gg), in1=a32(hh), op=ALU.mult), [ARB[gg], ARB[hh]], [o_buf(3, c)])

            if ti == 0:
                cv_issue(upto_layer=0)
            for i2 in range(2):
                bu = win_slot(0 + i2 * 256)
                for mi in range(2):
                    c = i2 * 2 + mi
                    b = bu[mi]
                    fw.op("act", lambda h: h.activation(out=a32(8 + c), in_=psb(b), func=AF.Gelu_apprx_tanh), [PSB[b]], [ARB[8 + c]])
            sv0, sv0B = load_w(wpieces(WIN[l, :, 512:768], KC), CVW["w_in"][l])
            sv1, sv1B = load_w(wpieces(WIN[l, :, 768:1024], KC), CVW["w_in"][l])
            def sgu_chain(tt):
                b = bank((4, 5, 6, 7))
                for hf, (sv, svB) in enumerate(((sv0, sv0B), (sv1, sv1B))):
                    for kc in range(KC):
                        mm(psb(b, 256, hf * 256), XN[:, kc, tt * 128:(tt + 1) * 128], sv[:, kc, :], kc == 0, kc == KC - 1,
                           xnr(kc, kc == KC - 1) + [svB], [PSB[b]], kc == KC - 1)
                vg = tmp(0 + 3 * (tt % 2))
                junk = tmp(1 + 3 * (tt % 2))
                vn = tmp(2 + 3 * (tt % 2))
                rsb = RSB[tt]
                rcol = SM[:, 136 + tt:137 + tt]
                fw.op("act", lambda h: h.activation(out=a32(vg), in_=psb(b), func=AF.Gelu_apprx_tanh), [PSB[b]], [ARB[vg]])
                fw.op("act", lambda h: h.activation(out=a32(junk), in_=a32(vg), func=AF.Square, accum_out=rcol), [ARB[vg]], [ARB[junk], rsb])
                fw.op("act", lambda h: h.activation(out=rcol, in_=rcol, func=AF.Sqrt, scale=1.0 / BW, bias=EPSC[:, 0:1]), [rsb, EPSB], [rsb])
                fw.op("dve", lambda h: h.reciprocal(out=rcol, in_=rcol), [rsb], [rsb])
                fw.op("dve", lambda h: h.scalar_tensor_tensor(out=abf(vn)[:, 0:512], in0=a32(vg), scalar=rcol, in1=GSG[:], op0=ALU.mult, op1=ALU.mult),
                      [ARB[vg], rsb, LRB], [ARB[vn]])
                return vn

            def sgu_mix(tt, vn):
                for g in range(4):
                    o = psb(g, 128, tt * 128)
                    mm(o, abf(vn)[:, g * 128:(g + 1) * 128], WSG[:, g * 128:(g + 1) * 128], True, False, [ARB[vn], LWB], [PSB[g]], False)
                    mm(o, ONEB[0:1, 0:128], BSG[0:1, g * 128:(g + 1) * 128], False, False, [ONESB, LRB], [PSB[g]], False)
                    mm(o, ONEB[0:1, 0:128], BSG[0:1, 512 + g * 128:512 + (g + 1) * 128], False, True, [ONESB, LRB, ARB[vn]], [PSB[g]], True)
            prev = None
            for tt in range(4):
                vn = sgu_chain(tt)
                if prev is not None:
                    sgu_mix(*prev)
                prev = (tt, vn)
            sgu_mix(*prev)
            for g in range(4):
                fw.op("dve", lambda h: h.tensor_tensor(out=o_ap(0, g), in0=psb(g), in1=a32(8 + g), op=ALU.mult), [PSB[g], ARB[8 + g]], [o_buf(0, g)])

            for which in range(2):
                for i2 in range(2):
                    bq = win_slot((2560 if which == 0 else 3072) + i2 * 256)
                    for mi in range(2):
                        hd = i2 * 2 + mi
                        b = bq[mi]
                        q32 = (12 + hd) if which == 0 else tmp(0 + 2 * (hd % 2))
                        t2 = tmp(1 + 2 * (hd % 2)) if which == 1 else tmp(4 + (hd % 2))
                        fw.op("act", lambda h: h.copy(out=a32(q32), in_=psb(b)), [PSB[b]], [ARB[q32]])
                        b2 = bank()
                        mm(psb(b2), CST[:, 0:128], a32(q32), True, True, [CSTB, ARB[q32]], [PSB[b2]], True)
                        fw.op("dve", lambda h: h.tensor_tensor(out=a32(t2), in0=psb(b2), in1=CS[:, 1, :], op=ALU.mult), [PSB[b2], CSB], [ARB[t2]])
                        fw.op("dve", lambda h: h.tensor_tensor(out=a32(q32), in0=a32(q32), in1=CS[:, 0, :], op=ALU.mult), [ARB[q32], CSB], [ARB[q32]])
                        fw.op("dve", lambda h: h.tensor_tensor(out=a32(q32), in0=a32(q32), in1=a32(t2), op=ALU.add), [ARB[q32], ARB[t2]], [ARB[q32]])
                        if which == 0:
                            qb = abf(16 + hd // 2)[:, (hd % 2) * 512:(hd % 2) * 512 + 512]
                            fw.op("pool", lambda h: h.tensor_copy(out=qb, in_=a32(q32)), [ARB[q32]], [ARB[16 + hd // 2]])
                        else:
                            fw.op("pool", lambda h: h.tensor_copy(out=KCt[:, hd, tok0:tok0 + T], in_=a32(q32)), [ARB[q32]], [KCB])
                            kmt = SM[:, 144 + 2 * hd:146 + 2 * hd]
                            fw.op("dve", lambda h: h.tensor_reduce(out=kmt, in_=a32(q32).rearrange("p (b k) -> p b k", b=2), axis=AX.X, op=ALU.add),
                                  [ARB[q32]], [MXB[hd]])
                            fw.op("dve", lambda h: h.tensor_scalar(out=KM[:, hd, 2 * j:2 * j + 2], in0=kmt, scalar1=1.0 / 256.0, scalar2=None, op0=ALU.mult),
                                  [MXB[hd]], [KMB])
            sv0, sv0B = load_w(wpieces(WIN[l, :, 3584:3840], KC), CVW["w_in"][l])
            sv1, sv1B = load_w(wpieces(WIN[l, :, 3840:4096], KC), CVW["w_in"][l])
            for tt in range(4):
                b = bank()
                for hf, (sv, svB) in enumerate(((sv0, sv0B), (sv1, sv1B))):
                    for kc in range(KC):
                        mm(psb(b, 256, hf * 256), XN[:, kc, tt * 128:(tt + 1) * 128], sv[:, kc, :], kc == 0, kc == KC - 1,
                           xnr(kc, kc == KC - 1) + [svB], [PSB[b]], kc == KC - 1)
                fw.op("act", lambda h: h.copy(out=VCt[:, j * 4 + tt, :], in_=psb(b)), [PSB[b]], [VCB])

            PT = PS[:, 2048:3072].bitcast(BF16)
            GM = SM[:, 0:32]
            TOP = SM[:, 32:64]
            SEL = SM[:, 64:96]
            MBm = SM[:, 96:128]
            ai = 0
            for qt in range(4):
                QT = 4 * j + qt
                ob = QT // 2
                half = QT % 2
                W = (ob + 1) * 256
                qs = slice(qt * 128, (qt + 1) * 128)
                if ob > 0:
                    for hd in range(4):
                        mm(psb(7, 8, hd * 8), a32(12 + hd)[:, qs], KM[:, hd, :], True, True, [ARB[12 + hd], KMB], [PSB[7]], hd == 3)
                    pn = CST[:, 640 + ob * 8:640 + ob * 8 + 8].unsqueeze(1).to_broadcast([128, 4, 8])
                    g3 = lambda ap: ap.rearrange("p (h n) -> p h n", h=4)
                    fw.op("dve", lambda h: h.tensor_tensor(out=g3(GM), in0=g3(psb(7, 32)), in1=pn, op=ALU.add), [PSB[7], CSTB], [GMB])
                    for hd in range(4):
                        fw.op("dve", lambda h: h.max(out=TOP[:, hd * 8:(hd + 1) * 8], in_=GM[:, hd * 8:(hd + 1) * 8]), [GMB], [TOPB])
                    fw.op("dve", lambda h: h.tensor_tensor(out=g3(SEL), in0=g3(GM), in1=g3(TOP)[:, :, 2:3].to_broadcast([128, 4, 8]), op=ALU.is_ge),
                          [GMB, TOPB], [SELB])
                    fw.op("dve", lambda h: h.tensor_scalar(out=MBm, in0=SEL, scalar1=-1.0, scalar2=BIG, op0=ALU.add, op1=ALU.mult), [SELB], [MBB])
                    fw.op("dve", lambda h: h.tensor_tensor(out=g3(MBm), in0=g3(MBm), in1=pn, op=ALU.add), [MBB, CSTB], [MBB])
                two = W <= 1024
                nb = (W + 511) // 512
                nkt = W // 128
                ctx = {}

                def geom(hd):
                    par = hd % 2
                    so = (par * 1024) if two else 0
                    sbo = (par * 1032) if two else 0
                    sbanks = [PSB[(so + c * 512) // 512] for c in range(nb)]
                    pbB = [ARB[18 + par]] if two else [ARB[18], ARB[19]]
                    ptB = [PSB[4 + par]] if two else [PSB[4], PSB[5]]
                    ptsB = [ARB[20 + par]] if two else [ARB[20], ARB[21]]
                    return par, so, sbo, sbanks, pbB, ptB, ptsB

                def st_A(hd):
                    par, so, sbo, sbanks, pbB, ptB, ptsB = geom(hd)
                    for c in range(nb):
                        n = min(512, W - c * 512)
                        mm(PS[:, so + c * 512:so + c * 512 + n], abf(16 + hd // 2)[:, (hd % 2) * 512 + qt * 128:(hd % 2) * 512 + (qt + 1) * 128],
                           KCt[:, hd, c * 512:c * 512 + n], True, True, [ARB[16 + hd // 2], KCB], [sbanks[c]], True)

                def st_B(hd):
                    par, so, sbo, sbanks, pbB, ptB, ptsB = geom(hd)
                    Sps = PS[:, so:so + W]
                    if ob > 0:
                        v3 = PS[:, so:so + ob * 256].rearrange("p (n k) -> p n k", k=256)
                        fw.op("dve", lambda h: h.scalar_tensor_tensor(out=v3, in0=v3, scalar=SCALE,
                                                                       in1=MBm[:, hd * 8:hd * 8 + ob].unsqueeze(2).to_broadcast([128, ob, 256]),
                                                                       op0=ALU.mult, op1=ALU.add), sbanks + [MBB], sbanks)
                    vo = PS[:, so + ob * 256:so + W]
                    fw.op("dve", lambda h: h.scalar_tensor_tensor(out=vo, in0=vo, scalar=SCALE, in1=CST[:, 128 + half * 256:128 + (half + 1) * 256],
                                                                   op0=ALU.mult, op1=ALU.add), sbanks + [CSTB], sbanks)
                    mx = SM[:, 128 + par:129 + par]
                    fw.op("dve", lambda h: h.reduce_max(out=mx, in_=Sps, axis=AX.X), sbanks, [MXB[par]])
                    fw.op("dve", lambda h: h.tensor_scalar(out=mx, in0=mx, scalar1=-1.0, scalar2=None, op0=ALU.mult), [MXB[par]], [MXB[par]])
                    PBv = abf2(18)[:, sbo:sbo + W]
                    fw.op("act", lambda h: h.activation(out=PBv, in_=Sps, func=AF.Exp, bias=mx, scale=1.0), sbanks + [MXB[par]], pbB)

                def st_C(hd):
                    par, so, sbo, sbanks, pbB, ptB, ptsB = geom(hd)
                    PBv = abf2(18)[:, sbo:sbo + W]
                    for kt in range(nkt):
                        fw.op("pe", lambda h: h.transpose(out=PT[:, so + kt * 128:so + (kt + 1) * 128], in_=PBv[:, kt * 128:(kt + 1) * 128], identity=IDB[:]),
                              pbB + [IDBB], ptB, kt == nkt - 1)

                def st_D(hd):
                    par, so, sbo, sbanks, pbB, ptB, ptsB = geom(hd)
                    PTSv = abf2(20)[:, sbo:sbo + W]
                    fw.op("act", lambda h: h.copy(out=PTSv, in_=PT[:, so:so + W]), ptB, ptsB)

                def st_E(hd):
                    par, so, sbo, sbanks, pbB, ptB, ptsB = geom(hd)
                    PTSv = abf2(20)[:, sbo:sbo + W]
                    ob_ = 6
                    oo = par * 256
                    for kt in range(nkt):
                        mm(psb(ob_, 128, oo), VCt[:, kt, hd * 128:(hd + 1) * 128], PTSv[:, kt * 128:(kt + 1) * 128], kt == 0, kt == nkt - 1,
                           [VCB] + ptsB, [PSB[ob_]], False)
                    for kt in range(nkt):
                        mm(psb(ob_, 128, oo + 128), ONEB[:], PTSv[:, kt * 128:(kt + 1) * 128], kt == 0, kt == nkt - 1,
                           [ONESB, VCB] + ptsB, [PSB[ob_]], kt == nkt - 1)
                    fw.op("dve", lambda h: h.reciprocal(out=RINV[:, par, :], in_=psb(ob_, 128, oo + 128)), [PSB[ob_]], [RINVB[par]])
                    fw.op("dve", lambda h: h.tensor_tensor(out=o_ap(2, hd)[:, qs], in0=psb(ob_, 128, oo), in1=RINV[:, par, :], op=ALU.mult),
                          [PSB[ob_], RINVB[par]], [o_buf(2, hd)])

                if two:
                    st_A(0)
                    st_B(0)
                    for hd in range(4):
                        if hd + 1 < 4:
                            st_A(hd + 1)
                            st_B(hd + 1)
                        st_C(hd)
                        st_D(hd)
                        st_E(hd)
                else:
                    st_A(0)
                    st_B(0)
                    st_C(0)
                    st_D(0)
                    for hd in range(4):
                        if hd + 1 < 4:
                            st_A(hd + 1)
                            st_B(hd + 1)
                        st_E(hd)
                        if hd + 1 < 4:
                            st_C(hd + 1)
                            st_D(hd + 1)

            if debug and (l, s, j) == (L - 1, 0, 0):
                for k in range(4):
                    for c in range(4):
                        fw.dma("pool", dbg_d[:, (k * 4 + c) * 512:(k * 4 + c + 1) * 512], o_ap(k, c), o_buf(k, c), Buf("dbg"), sembuf=o_buf(k, c))
            for mg in range(8):
                cv_issue(upto_layer=l + 1, count=1)
                bsl, bsB = WBS2[mg % 2], WBSB2[mg % 2]
                for k in range(4):
                    fw.dma("sp", bsl[:, k * 4:k * 4 + 4, :], WBR[l, k, :, mg * 256:(mg + 1) * 256].rearrange("(kc p) c -> p kc c", p=128), CVW["w_branch"][l], bsB)
                for k in range(4):
                    gsl, gsB = load_w(wpieces(WG[l, k, :, mg * 256:(mg + 1) * 256], KC), CVW["w_gate"][l])
                    for mi in range(2):
                        m = mg * 2 + mi
                        bgk = bank()
                        for kc in range(KC):
                            mm(psb(bgk), gsl[:, kc, mi * 128:(mi + 1) * 128], XN[:, kc, :], kc == 0, kc == KC - 1, [gsB] + xnr(kc, kc == KC - 1), [PSB[bgk]], kc == KC - 1)
                        bbk = bank()
                        for kc in range(4):
                            mm(psb(bbk), bsl[:, k * 4 + kc, mi * 128:(mi + 1) * 128], o_ap(k, kc), kc == 0, kc == 3, [bsB, o_buf(k, kc)], [PSB[bbk]], kc == 3)
                        sig = tmp((k * 2 + mi) % 4)
                        yacc = tmp(4 + mi)
                        fw.op("act", lambda h: h.activation(out=a32(sig), in_=psb(bgk), func=AF.Sigmoid, bias=ppc("bgate", (l * 4 + k) * 16 + m)),
                              [PSB[bgk], PPB], [ARB[sig]])
                        if k == 0:
                            fw.op("dve", lambda h: h.tensor_tensor(out=a32(yacc), in0=psb(bbk), in1=a32(sig), op=ALU.mult), [PSB[bbk], ARB[sig]], [ARB[yacc]])
                        else:
                            fw.op("dve", lambda h: h.tensor_tensor(out=a32(sig), in0=psb(bbk), in1=a32(sig), op=ALU.mult), [PSB[bbk], ARB[sig]], [ARB[sig]])
                            if k < 3:
                                fw.op("dve", lambda h: h.tensor_tensor(out=a32(yacc), in0=a32(yacc), in1=a32(sig), op=ALU.add), [ARB[yacc], ARB[sig]], [ARB[yacc]])
                            else:
                                ysl = 8 + m // 2
                                yv = abf(ysl)[:, (m % 2) * 512:(m % 2) * 512 + 512]
                                fw.op("dve", lambda h: h.tensor_tensor(out=yv, in0=a32(yacc), in1=a32(sig), op=ALU.add), [ARB[yacc], ARB[sig]], [ARB[ysl]])

            def y_rhs(kc):
                return abf(8 + kc // 2)[:, (kc % 2) * 512:(kc % 2) * 512 + 512]
            ybufs = [ARB[8 + i] for i in range(8)]
            for mg in range(8):
                bs = dense_fm(WO[l, :, mg * 256:(mg + 1) * 256], KC, y_rhs, ybufs, CVW["w_out"][l])
                for mi in range(2):
                    m = mg * 2 + mi
                    b = bs[mi]
                    fw.op("dve", lambda h: h.tensor_tensor(out=X[:, m, :], in0=psb(b), in1=X[:, m, :], op=ALU.add), [PSB[b], XB[m]], [XB[m]])

            norm_to_xn("gffn", l)

            def h_ap(fc):
                return abf(fc // 2)[:, (fc % 2) * 512:(fc % 2) * 512 + 512]

            def h_buf(fc):
                return ARB[fc // 2]
            nxt = tiles[ti + 1] if ti + 1 < len(tiles) else None
            for hf in range(2):
                for fg in range(11):
                    if hf == 0 and fg in (0, 5):
                        cv_issue(upto_layer=l + 1, count=1)
                    col0 = (hf * 11 + fg) * 256
                    s1, s1B = load_w(wpieces(W1[l, :, col0:col0 + 256], KC), CVW["w_ffn1"][l])
                    s3, s3B = load_w(wpieces(W3[l, :, col0:col0 + 256], KC), CVW["w_ffn3"][l])
                    for mi in range(2):
                        fc = fg * 2 + mi
                        b1 = bank()
                        for kc in range(KC):
                            mm(psb(b1), s1[:, kc, mi * 128:(mi + 1) * 128], XN[:, kc, :], kc == 0, kc == KC - 1, [s1B] + xnr(kc, kc == KC - 1), [PSB[b1]], kc == KC - 1)
                        b3 = bank()
                        for kc in range(KC):
                            mm(psb(b3), s3[:, kc, mi * 128:(mi + 1) * 128], XN[:, kc, :], kc == 0, kc == KC - 1, [s3B] + xnr(kc, kc == KC - 1), [PSB[b3]], kc == KC - 1)
                        sl = tmp(fc % 4)
                        fw.op("act", lambda h: h.activation(out=a32(sl), in_=psb(b1), func=AF.Silu), [PSB[b1]], [ARB[sl]])
                        fw.op("dve", lambda h: h.tensor_tensor(out=h_ap(fc), in0=psb(b3), in1=a32(sl), op=ALU.mult), [PSB[b3], ARB[sl]], [h_buf(fc)])
                hbufs = [ARB[i] for i in range(11)]
                for mg in range(8):
                    r0 = hf * 22 * 128
                    sa, saB = load_w(wpieces(W2[l, r0:r0 + 11 * 128, mg * 256:(mg + 1) * 256], 11), CVW["w_ffn2"][l])
                    sb_, sbB = load_w(wpieces(W2[l, r0 + 11 * 128:r0 + 22 * 128, mg * 256:(mg + 1) * 256], 11), CVW["w_ffn2"][l])
                    bks = [bank(), bank()]
                    for si, (sl_, slB) in enumerate(((sa, saB), (sb_, sbB))):
                        for mi in range(2):
                            for kc in range(11):
                                fc = si * 11 + kc
                                last = (si == 1 and kc == 10)
                                mm(psb(bks[mi]), sl_[:, kc, mi * 128:(mi + 1) * 128], h_ap(fc), si == 0 and kc == 0, last,
                                   [slB] + hbufs, [PSB[bks[mi]]] if (last or (si == 0 and kc == 0)) else [], last or kc == 10)
                    for mi in range(2):
                        m = mg * 2 + mi
                        b = bks[mi]
                        fw.op("dve", lambda h: h.tensor_tensor(out=X[:, m, :], in0=psb(b), in1=X[:, m, :], op=ALU.add), [PSB[b], XB[m]], [XB[m]])
                        if hf == 1 and l < L - 1:
                            store_x_chunk(l, s, j, m)
                            if nxt is not None:
                                load_x_chunk(nxt[0], nxt[1], nxt[2], m)
            if l == L - 1:
                rs = tmp(2)
                rms_stats(rs)
                for kc in range(KC):
                    fw.op("dve", lambda h: h.scalar_tensor_tensor(out=X[:, kc, :], in0=X[:, kc, :], scalar=ppc("gfin", kc), in1=a32(rs),
                                                                    op0=ALU.mult, op1=ALU.mult), [XB[kc], PPB, ARB[rs]], [XB[kc]])
                    store_x_chunk(l, s, j, kc)
                    if nxt is not None:
                        load_x_chunk(nxt[0], nxt[1], nxt[2], kc)

        for kc in range(KC):
            load_x_chunk(0, 0, 0, kc)
        ti = 0
        for l in range(L):
            layer_prep(l)
            for s in range(NSEQ):
                seq_reset()
                for j in range(NT):
                    tile(ti)
                    ti += 1
            cv_issue(upto_layer=l + 1)
        outb = [ORG[s][j][kc] for s in range(NSEQ) for j in range(NT) for kc in range(KC)]
        fw.finish("pool", outb)
        fw.finish("sp", outb)
        build.stats = (fw.nins, fw.nwait)
        build.sbuf_left = nc.sbuf_bytes_remaining
    return nc


def make_in_maps(inputs, L, nseq, ncores):
    rope, cst = host_consts()
    pp, mats, rows = host_params(inputs, L)
    x = np.asarray(inputs["x"], np.float32)
    common = {"pp": pp, "mats": mats, "rows": rows, "rope": rope, "cst": cst}
    for n in WNAMES:
        common[n] = np.ascontiguousarray(np.asarray(inputs[n], np.float32)[:L]).reshape(L, 128, -1)
    maps = []
    for c in range(ncores):
        m = dict(common)
        m["xT"] = np.ascontiguousarray(x[c * nseq:(c + 1) * nseq].transpose(0, 2, 1))
        maps.append(m)
    return maps


def kernel(**inputs):
    nc = build(DEPTH, SEQ_PER_CORE, S // T)
    maps = make_in_maps(inputs, DEPTH, SEQ_PER_CORE, NCORES)
    res = run_bass_kernel_spmd(nc, maps, core_ids=list(range(NCORES)))
    outs = [np.asarray(r["oT"], np.float32).transpose(0, 2, 1) for r in res.results]
    return np.ascontiguousarray(np.concatenate(outs, axis=0))
```

```python
import numpy as np
from contextlib import ExitStack
import concourse.bass as bass
import concourse.mybir as mybir
from concourse.bass_utils import run_bass_kernel_spmd

F32 = mybir.dt.float32
BF16 = mybir.dt.bfloat16
AF = mybir.ActivationFunctionType
ALU = mybir.AluOpType
AX = mybir.AxisListType

D = 2048
S = 2048
T = 512
KC = 16
INW = 5120
DFF = 5632
BW = 512
NEG = -1e30
BIG = 1e30
EPS = 1e-6
NCORES = 8
SEQ_PER_CORE = 2
DEPTH = 4
SCALE = 128 ** -0.5


class Buf:
    __slots__ = ("name", "w", "r", "dsem", "dcnt")

    def __init__(self, name):
        self.name = name
        self.w = None
        self.r = {}
        self.dsem = None
        self.dcnt = 0


class Eng:
    def __init__(self, name, h, sem):
        self.name = name
        self.h = h
        self.sem = sem
        self.cnt = 0
        self.seen = {}


class FW:
    def __init__(self, nc, stack):
        self.nc = nc
        self.stack = stack
        self.engs = {}
        for n in ("pe", "act", "dve", "pool", "sp"):
            h = {"pe": nc.tensor, "act": nc.scalar, "dve": nc.vector, "pool": nc.gpsimd, "sp": nc.sync}[n]
            sem = stack.enter_context(nc.semaphore("sem_" + n))
            self.engs[n] = Eng(n, h, sem)
        self.nwait = 0
        self.nins = 0

    def _waits(self, E, reads, writes):
        need = {}

        def add(tok):
            sem, val = tok
            k = id(sem)
            if k not in need or need[k][1] < val:
                need[k] = (sem, val)
        for b in reads:
            if b.w is not None:
                add(b.w)
        for b in writes:
            if b.w is not None and b.w[0] is not E.sem:
                add(b.w)
            for tok in b.r.values():
                if tok[0] is not E.sem:
                    add(tok)
        for k, (sem, val) in need.items():
            if E.seen.get(k, 0) >= val:
                continue
            E.h.wait_ge(sem, val)
            E.seen[k] = val
            self.nwait += 1

    def _record(self, tok, reads, writes):
        k = id(tok[0])
        for b in reads:
            b.r[k] = tok
        for b in writes:
            b.w = tok
            b.r = {}

    def op(self, eng, fn, reads=(), writes=(), mark=True):
        E = self.engs[eng]
        self._waits(E, reads, writes)
        ins = fn(E.h)
        self.nins += 1
        if mark:
            E.cnt += 1
            ins.then_inc(E.sem, 1)
            self._record((E.sem, E.cnt), reads, writes)
        return ins

    def dma(self, q, out_ap, in_ap, src, dst, sembuf=None):
        E = self.engs[q]
        sb = sembuf if sembuf is not None else dst
        if sb.dsem is None:
            sb.dsem = self.stack.enter_context(self.nc.semaphore("d_" + sb.name))
        same_fill = (dst.w is not None and dst.w[0] is sb.dsem and not dst.r)
        if same_fill:
            saved = dst.w
            dst.w = None
            self._waits(E, [src], [dst])
            dst.w = saved
        else:
            self._waits(E, [src], [dst])
        sb.dcnt += 16
        ins = E.h.dma_start(out=out_ap, in_=in_ap)
        ins.then_inc(sb.dsem, 16)
        self.nins += 1
        self._record((sb.dsem, sb.dcnt), [src], [dst])
        return ins

    def finish(self, eng, bufs):
        E = self.engs[eng]
        self._waits(E, list(bufs), list(bufs))


def pp_layout(L):
    off = {}
    n = 0
    for name, w in [("gmix", L * 16), ("gffn", L * 16), ("gfin", 16), ("bgate", L * 64), ("wsconv", L * 12),
                    ("wlconv", L * 16), ("blconv", L * 4), ("bla", L * 4), ("blx", L * 4), ("lam", L * 4)]:
        off[name] = n
        n += w
    return off, n


def _cols(a):
    return np.ascontiguousarray(np.asarray(a, np.float32).reshape(-1, 128).T)


def host_consts():
    half = 64
    inv = (10000.0 ** (-(np.arange(half, dtype=np.float32)) / np.float32(half))).astype(np.float32)
    ang = np.arange(S, dtype=np.float32)[None, :] * inv[:, None]
    cos = np.cos(ang).astype(np.float32)
    sin = np.sin(ang).astype(np.float32)
    rope = np.concatenate([np.concatenate([cos, cos], 0), np.concatenate([-sin, sin], 0)], axis=1)
    perm = np.zeros((128, 128), np.float32)
    for m in range(128):
        perm[(m + 64) % 128, m] = 1.0
    triu = (np.arange(128)[:, None] <= np.arange(128)[None, :]).astype(np.float32)
    q = np.arange(128)[:, None]
    key = np.arange(256)[None, :]
    causal = np.concatenate([np.where(key <= q, 0.0, NEG), np.where(key <= 128 + q, 0.0, NEG)], axis=1).astype(np.float32)
    pastneg = np.zeros((8, 8), np.float32)
    for ob in range(8):
        for n in range(8):
            pastneg[ob, n] = 0.0 if n < ob else NEG
    pastneg = np.broadcast_to(pastneg.reshape(1, 64), (128, 64)).astype(np.float32)
    ident = np.eye(128, dtype=np.float32)
    cst = np.concatenate([perm, causal, pastneg, triu, ident], axis=1)
    return np.ascontiguousarray(rope.astype(np.float32)), np.ascontiguousarray(cst)


def host_params(inp, L):
    off, npp = pp_layout(L)
    pp = np.zeros((128, npp), np.float32)

    def put(name, a):
        c = _cols(a)
        pp[:, off[name]:off[name] + c.shape[1]] = c
    put("gmix", inp["g_mix"][:L])
    put("gffn", inp["g_ffn"][:L])
    put("gfin", inp["g_final"])
    put("bgate", inp["b_gate"][:L])
    put("wsconv", inp["w_sconv"][:L])
    put("wlconv", inp["w_lru_conv"][:L])
    put("blconv", inp["b_lru_conv"][:L])
    put("bla", inp["b_lru_a"][:L])
    put("blx", inp["b_lru_x"][:L])
    put("lam", inp["lru_lambda"][:L])
    wla = np.asarray(inp["w_lru_a"][:L], np.float32).transpose(2, 0, 1, 3).reshape(128, -1)
    wlx = np.asarray(inp["w_lru_x"][:L], np.float32).transpose(2, 0, 1, 3).reshape(128, -1)
    wsg = np.asarray(inp["w_sgu"][:L], np.float32).transpose(3, 0, 1, 2).reshape(128, -1)
    mats = np.ascontiguousarray(np.concatenate([wla, wlx, wsg], axis=1))
    bsgu = np.asarray(inp["b_sgu"][:L], np.float32).transpose(0, 2, 1).reshape(1, -1)
    gsgu = np.asarray(inp["g_sgu"][:L], np.float32).reshape(1, -1)
    rows = np.ascontiguousarray(np.concatenate([bsgu, gsgu], axis=1))
    return pp, mats, rows


WNAMES = ["w_in", "w_gate", "w_branch", "w_out", "w_ffn1", "w_ffn3", "w_ffn2"]


def wshape(name, L):
    return {"w_in": [L, D, INW], "w_gate": [L, 4, D, D], "w_branch": [L, 4, BW, D], "w_out": [L, D, D],
            "w_ffn1": [L, D, DFF], "w_ffn3": [L, D, DFF], "w_ffn2": [L, DFF, D]}[name]


def build(L=DEPTH, NSEQ=SEQ_PER_CORE, NT=S // T, debug=False):
    nc = bass.Bass("TRN2", target_bir_lowering=False)
    off, npp = pp_layout(L)
    CHK = 2048

    xT_d = nc.dram_tensor("xT", [NSEQ, D, S], F32, kind="ExternalInput").ap()
    oT_d = nc.dram_tensor("oT", [NSEQ, D, S], F32, kind="ExternalOutput").ap()
    pp_d = nc.dram_tensor("pp", [128, npp], F32, kind="ExternalInput").ap()
    mats_d = nc.dram_tensor("mats", [128, 3 * L * 512], F32, kind="ExternalInput").ap()
    rows_d = nc.dram_tensor("rows", [1, 2 * L * 512], F32, kind="ExternalInput").ap()
    rope_d = nc.dram_tensor("rope", [128, 2 * S], F32, kind="ExternalInput").ap()
    cst_d = nc.dram_tensor("cst", [128, 960], F32, kind="ExternalInput").ap()
    wsrc = {}
    wbf = {}
    for n in WNAMES:
        shp = wshape(n, L)
        E = int(np.prod(shp))
        wsrc[n] = nc.dram_tensor(n, [L, 128, E // L // 128], F32, kind="ExternalInput").ap()
        wbf[n] = nc.dram_tensor(n + "_bf", [E], BF16).ap()
    xscr = (nc.dram_tensor("xscr", [NSEQ, D, S], F32, kind="ExternalOutput") if debug else nc.dram_tensor("xscr", [NSEQ, D, S], F32)).ap()
    dbg_d = nc.dram_tensor("dbg", [128, 16 * 512], BF16, kind="ExternalOutput").ap() if debug else None

    def wview(n):
        shp = wshape(n, L)
        if len(shp) == 3:
            return wbf[n].rearrange("(l r c) -> l r c", l=shp[0], r=shp[1])
        return wbf[n].rearrange("(l k r c) -> l k r c", l=shp[0], k=shp[1], r=shp[2])

    with ExitStack() as st:
        fw = FW(nc, st)

        def sb(name, shape, dt=F32):
            return st.enter_context(nc.sbuf_tensor(name, shape, dt))

        X = sb("X", [128, KC + 2, T])
        XB = [Buf("X%d" % i) for i in range(KC + 2)]
        xst = {"map": list(range(KC)), "spare": [KC, KC + 1]}

        def xp(kc):
            return xst["map"][kc]
        XN = sb("XN", [128, KC, T], BF16)
        XNBs = [Buf("XN%d" % i) for i in range(KC)]

        def xnr(kc, last=False):
            return list(XNBs) if last else [XNBs[kc]]
        NSLOT = 4
        WS = [sb("WS%d" % i, [128, 16, 256], BF16) for i in range(NSLOT)]
        WSB = [Buf("WS%d" % i) for i in range(NSLOT)]
        WBS2 = [sb("WBS%d" % i, [128, 16, 256], BF16) for i in range(2)]
        WBSB2 = [Buf("WBS%d" % i) for i in range(2)]
        KCt = sb("KC", [128, 4, S], BF16)
        KCB = Buf("KC")
        VCt = sb("VC", [128, 16, 512], BF16)
        VCB = Buf("VC")
        PP = sb("PP", [128, npp])
        PPB = Buf("PP")
        CST = sb("CST", [128, 704])
        CSTB = Buf("CST")
        IDB = sb("IDB", [128, 128], BF16)
        IDBB = Buf("IDB")
        ONEB = sb("ONEB", [128, 128], BF16)
        ONE32 = sb("ONE32", [128, 128])
        ONESB = Buf("ONES")
        WLA = sb("WLA", [128, 512], BF16)
        WLX = sb("WLX", [128, 512], BF16)
        WSG = sb("WSG", [128, 512], BF16)
        LWB = Buf("LW")
        BSG = sb("BSG", [1, 1024], BF16)
        GSG = sb("GSG", [128, 512])
        LRB = Buf("LR")
        SPt = sb("SPt", [128, 16])
        SPB = Buf("SP")
        CS = sb("CS", [128, 2, T])
        CSB = Buf("CS")
        ZH = sb("ZH", [128, 4, 2])
        ZHB = [Buf("ZH%d" % i) for i in range(4)]
        XRH = sb("XRH", [128, 4, 3])
        XRHB = [Buf("XRH%d" % i) for i in range(4)]
        HC = sb("HC", [128, 4])
        HCB = [Buf("HC%d" % i) for i in range(4)]
        KM = sb("KM", [128, 4, 8])
        KMB = Buf("KM")
        SM = sb("SM", [128, 248])
        MBQB = [Buf("MBQ%d" % i) for i in range(4)]
        GMB, TOPB, SELB, MBB = Buf("GM"), Buf("TOP"), Buf("SEL"), Buf("MBm")
        MXB = [Buf("MX%d" % i) for i in range(4)]
        RSB = [Buf("RS%d" % i) for i in range(4)]
        RINV = sb("RINV", [128, 2, 128])
        RINVB = [Buf("RINV0"), Buf("RINV1")]

        NAR = 28
        AR = sb("AR", [128, NAR, 516])
        ARB = [Buf("AR%d" % i) for i in range(NAR)]

        def a32(i, n=512, o=0):
            return AR[:, i, o:o + n]

        def abf(i):
            return AR[:, i, :].bitcast(BF16)

        def abf2(i):
            return AR[:, i:i + 2, :].rearrange("p a b -> p (a b)").bitcast(BF16)

        TM0 = 22

        def tmp(k):
            return TM0 + k

        def o_ap(k, c):
            return abf(k * 2 + c // 2)[:, (c % 2) * 512:(c % 2) * 512 + 512]

        def o_buf(k, c):
            return ARB[k * 2 + c // 2]

        PS = st.enter_context(nc.psum_tensor("PS", [128, 4096], F32))
        PSB = [Buf("PS%d" % i) for i in range(8)]
        rot = {"i": 0}

        def bank(allowed=(0, 1, 2, 3, 4, 5, 6, 7)):
            while True:
                b = rot["i"] % 8
                rot["i"] += 1
                if b in allowed:
                    return b

        def psb(b, n=512, o=0):
            return PS[:, b * 512 + o:b * 512 + o + n]

        wrot = {"i": 0}
        DW = Buf("DW")

        def load_w(src_fn_list, dep):
            i = wrot["i"] % NSLOT
            wrot["i"] += 1
            for ap, kc0, n in src_fn_list:
                fw.dma("sp", WS[i][:, kc0:kc0 + n, :], ap, dep, WSB[i])
            return WS[i], WSB[i]

        def wpieces(ap2d, nkc):
            v = ap2d.rearrange("(kc p) c -> p kc c", p=128)
            out = []
            k = 0
            while k < nkc:
                n = min(8, nkc - k)
                out.append((v[:, k:k + n, :], k, n))
                k += n
            return out

        def mm(out, lhsT, rhs, start, stop, reads, writes, mark):
            return fw.op("pe", lambda h: h.matmul(out, lhsT=lhsT, rhs=rhs, start=start, stop=stop), reads, writes, mark)

        DIN = Buf("DIN")
        fw.dma("pool", PP[:], pp_d[:, :], DIN, PPB)
        fw.dma("pool", CST[:], cst_d[:, 0:704], DIN, CSTB)
        fw.dma("pool", a32(0, 256), cst_d[:, 704:960], DIN, ARB[0])
        fw.op("dve", lambda h: h.tensor_copy(out=IDB[:], in_=a32(0, 128, 128)), [ARB[0]], [IDBB])
        fw.op("dve", lambda h: h.memset(ONEB[:], 1.0), [], [ONESB])
        fw.op("dve", lambda h: h.memset(ONE32[:], 1.0), [], [ONESB])

        CVS = {n: Buf("cvs_" + n) for n in WNAMES}
        CVW = {n: [Buf("cvw_%s_%d" % (n, l)) for l in range(L)] for n in WNAMES}
        CVF = 8192
        cvjobs = []
        for l in range(L):
            for n in ["w_in", "w_branch", "w_gate", "w_out", "w_ffn1", "w_ffn3", "w_ffn2"]:
                El = int(np.prod(wshape(n, L))) // L
                per = El // 128
                assert per % CVF == 0
                for c in range(per // CVF):
                    cvjobs.append((l, n, c, El))
        cvpos = {"i": 0}

        def cv_issue(upto_layer=None, count=None):
            done = 0
            while cvpos["i"] < len(cvjobs):
                l, n, c, El = cvjobs[cvpos["i"]]
                if upto_layer is not None and l > upto_layer:
                    break
                if count is not None and done >= count:
                    break
                dst = wbf[n][l * El:(l + 1) * El].rearrange("(p f) -> p f", p=128)[:, c * CVF:(c + 1) * CVF]
                fw.dma("pool", dst, wsrc[n][l, :, c * CVF:(c + 1) * CVF], DIN, CVW[n][l], sembuf=CVS[n])
                cvpos["i"] += 1
                done += 1
        cv_issue(upto_layer=0, count=10)

        def ppc(name, idx):
            return PP[:, off[name] + idx:off[name] + idx + 1]

        def rms_stats(rs_slot):
            b = bank()
            for kc in range(KC):
                sq = tmp(kc % 2)
                sqv = abf(sq)[:, (kc // 2 % 2) * 512:(kc // 2 % 2) * 512 + 512]
                fw.op("act", lambda h: h.activation(out=sqv, in_=X[:, xp(kc), :], func=AF.Square), [XB[xp(kc)]], [ARB[sq]])
                mm(psb(b), ONEB[:], sqv, kc == 0, kc == KC - 1, [ONESB, ARB[sq]], [PSB[b]], True)
            fw.op("act", lambda h: h.activation(out=a32(rs_slot), in_=psb(b), func=AF.Sqrt, scale=1.0 / D, bias=EPSC[:, 0:1]),
                  [PSB[b], EPSB], [ARB[rs_slot]])
            fw.op("dve", lambda h: h.reciprocal(out=a32(rs_slot), in_=a32(rs_slot)), [ARB[rs_slot]], [ARB[rs_slot]])

        EPSC = sb("EPSC", [128, 2])
        EPSB = Buf("EPS")
        fw.op("dve", lambda h: h.memset(EPSC[:, 0:1], EPS), [], [EPSB])
        fw.op("dve", lambda h: h.memset(EPSC[:, 1:2], 1.0), [], [EPSB])

        def norm_to_xn(gname, l):
            rs = tmp(2)
            rms_stats(rs)
            for kc in range(KC):
                fw.op("dve", lambda h: h.scalar_tensor_tensor(out=XN[:, kc, :], in0=X[:, xp(kc), :], scalar=ppc(gname, l * 16 + kc),
                                                                in1=a32(rs), op0=ALU.mult, op1=ALU.mult),
                      [XB[xp(kc)], PPB, ARB[rs]], [XNBs[kc]])

        def dense_fm(src2d, nkc, rhs_fn, rhs_bufs, dep, allowed=(0, 1, 2, 3, 4, 5, 6, 7)):
            slot, sB = load_w(wpieces(src2d, nkc), dep)
            outs = []
            for mi in range(2):
                b = bank(allowed)
                for kc in range(nkc):
                    rb = rhs_bufs(kc, kc == nkc - 1) if callable(rhs_bufs) else rhs_bufs
                    mm(psb(b), slot[:, kc, mi * 128:(mi + 1) * 128], rhs_fn(kc), kc == 0, kc == nkc - 1,
                       [sB] + rb, [PSB[b]], kc == nkc - 1)
                outs.append(b)
            return outs

        def xn_rhs(kc):
            return XN[:, kc, :]

        XRG = [[[Buf("XRG%d_%d_%d" % (s, j, kc)) for kc in range(KC)] for j in range(NT)] for s in range(NSEQ)]
        ORG = [[[Buf("ORG%d_%d_%d" % (s, j, kc)) for kc in range(KC)] for j in range(NT)] for s in range(NSEQ)]
        tiles = [(l, s, j) for l in range(L) for s in range(NSEQ) for j in range(NT)]

        def load_x_chunk(l, s, j, kc, ph):
            if l == 0:
                fw.dma("act", X[:, ph, :], xT_d[s, kc * 128:(kc + 1) * 128, j * T:(j + 1) * T], DIN, XB[ph])
            else:
                fw.dma("act", X[:, ph, :], xscr[s, kc * 128:(kc + 1) * 128, j * T:(j + 1) * T], XRG[s][j][kc], XB[ph])

        def store_x_chunk(l, s, j, kc, ph):
            if l == L - 1:
                fw.dma("act", oT_d[s, kc * 128:(kc + 1) * 128, j * T:(j + 1) * T], X[:, ph, :], XB[ph], ORG[s][j][kc], sembuf=XB[ph])
            else:
                fw.dma("act", xscr[s, kc * 128:(kc + 1) * 128, j * T:(j + 1) * T], X[:, ph, :], XB[ph], XRG[s][j][kc], sembuf=XB[ph])

        def next_phys(kc):
            return xst["map"][kc] if kc < KC - 2 else xst["spare"][kc - (KC - 2)]

        def rotate_x():
            m = xst["map"]
            sp = xst["spare"]
            xst["map"] = m[:KC - 2] + sp
            xst["spare"] = m[KC - 2:]

        def layer_prep(l):
            s0 = tmp(3)
            for idx, dstt in enumerate((WLA, WLX)):
                fw.dma("pool", a32(s0), mats_d[:, (idx * L + l) * 512:(idx * L + l + 1) * 512], DIN, ARB[s0])
                fw.op("dve", lambda h: h.tensor_copy(out=dstt[:], in_=a32(s0)), [ARB[s0]], [LWB])
            fw.dma("pool", a32(s0), mats_d[:, (2 * L + l) * 512:(2 * L + l + 1) * 512], DIN, ARB[s0])
            st_ = tmp(5)
            fw.dma("pool", a32(st_, 128), cst_d[:, 704:832], DIN, ARB[st_])
            fw.op("dve", lambda h: h.tensor_tensor(out=WSG[:].rearrange("p (g t) -> p g t", g=4),
                                                    in0=a32(s0).rearrange("p (g t) -> p g t", g=4),
                                                    in1=a32(st_, 128).unsqueeze(1).to_broadcast([128, 4, 128]), op=ALU.mult),
                  [ARB[s0], ARB[st_]], [LWB])
            s1 = tmp(4)
            fw.dma("pool", AR[0:1, s1, 0:512], rows_d[0:1, l * 512:(l + 1) * 512], DIN, ARB[s1])
            fw.op("dve", lambda h: h.tensor_copy(out=BSG[0:1, 0:512], in_=AR[0:1, s1, 0:512]), [ARB[s1]], [LRB])
            s2 = tmp(5)
            fw.op("dve", lambda h: h.tensor_copy(out=AR[0:1, s2, 0:512], in_=BSG[0:1, 0:512]), [LRB], [ARB[s2]])
            fw.op("dve", lambda h: h.tensor_tensor(out=BSG[0:1, 512:1024], in0=AR[0:1, s1, 0:512], in1=AR[0:1, s2, 0:512], op=ALU.subtract),
                  [ARB[s1], ARB[s2]], [LRB])
            fw.dma("pool", GSG[:], rows_d[0:1, (L + l) * 512:(L + l + 1) * 512].partition_broadcast(128), DIN, LRB)
            lam = PP[:, off["lam"] + l * 4:off["lam"] + l * 4 + 4]
            z = SPt[:, 0:4]
            sp = SPt[:, 4:8]
            t8 = SPt[:, 8:12]
            t16 = SPt[:, 12:16]
            fw.op("act", lambda h: h.activation(out=z, in_=lam, func=AF.Exp, scale=-1.0), [PPB], [SPB])
            fw.op("dve", lambda h: h.tensor_scalar(out=sp, in0=z, scalar1=-0.2, scalar2=0.25, op0=ALU.mult, op1=ALU.add), [SPB], [SPB])
            for cst in (1.0 / 3.0, 0.5, 1.0):
                fw.op("dve", lambda h: h.tensor_tensor(out=sp, in0=sp, in1=z, op=ALU.mult), [SPB], [SPB])
                fw.op("dve", lambda h: h.tensor_scalar(out=sp, in0=sp, scalar1=-1.0, scalar2=cst, op0=ALU.mult, op1=ALU.add), [SPB], [SPB])
            fw.op("dve", lambda h: h.tensor_tensor(out=sp, in0=sp, in1=z, op=ALU.mult), [SPB], [SPB])
            fw.op("act", lambda h: h.activation(out=t8, in_=z, func=AF.Ln, bias=EPSC[:, 1:2], scale=1.0), [SPB, EPSB], [SPB])
            fw.op("dve", lambda h: h.tensor_scalar(out=t16, in0=z, scalar1=0.05, scalar2=None, op0=ALU.is_ge), [SPB], [SPB])
            fw.op("dve", lambda h: h.tensor_tensor(out=t8, in0=t8, in1=sp, op=ALU.subtract), [SPB], [SPB])
            fw.op("dve", lambda h: h.tensor_tensor(out=t8, in0=t8, in1=t16, op=ALU.mult), [SPB], [SPB])
            fw.op("dve", lambda h: h.tensor_tensor(out=sp, in0=sp, in1=t8, op=ALU.add), [SPB], [SPB])
            fw.op("dve", lambda h: h.tensor_scalar(out=t8, in0=sp, scalar1=-8.0, scalar2=None, op0=ALU.mult), [SPB], [SPB])
            fw.op("dve", lambda h: h.tensor_scalar(out=t16, in0=sp, scalar1=-16.0, scalar2=None, op0=ALU.mult), [SPB], [SPB])

        def seq_reset():
            for c in range(4):
                fw.op("pool", lambda h: h.memset(ZH[:, c, :], 0.0), [], [ZHB[c]])
                fw.op("pool", lambda h: h.memset(XRH[:, c, :], 0.0), [], [XRHB[c]])
                fw.op("pool", lambda h: h.memset(HC[:, c:c + 1], 0.0), [], [HCB[c]])
            fw.op("pool", lambda h: h.memset(KM[:], 0.0), [], [KMB])

        def tile(ti):
            l, s, j = tiles[ti]
            WIN = wview("w_in")
            WG = wview("w_gate")
            WBR = wview("w_branch")
            WO = wview("w_out")
            W1 = wview("w_ffn1")
            W3 = wview("w_ffn3")
            W2 = wview("w_ffn2")
            tok0 = j * T

            fw.dma("pool", CS[:, 0, :], rope_d[:, tok0:tok0 + T], DIN, CSB)
            fw.dma("pool", CS[:, 1, :], rope_d[:, S + tok0:S + tok0 + T], DIN, CSB)

            norm_to_xn("gmix", l)
            if ti == 0:
                cv_issue(upto_layer=0, count=20)

            def win_slot(col0, allowed=(0, 1, 2, 3, 4, 5, 6, 7)):
                return dense_fm(WIN[l, :, col0:col0 + 256], KC, xn_rhs, xnr, CVW["w_in"][l], allowed)

            def d_cv(c):
                return 8 + c

            def d_cvb(c):
                return abf(12 + c // 2)[:, (c % 2) * 512:(c % 2) * 512 + 512], ARB[12 + c // 2]
            for i2 in range(2):
                bx = win_slot(4096 + i2 * 256)
                for mi in range(2):
                    c = i2 * 2 + mi
                    cv = d_cv(c)
                    cvb_ap, cvbB = d_cvb(c)
                    xin = 14 + (c % 2)
                    b = bx[mi]
                    fw.op("pool", lambda h: h.tensor_copy(out=AR[:, xin, 0:3], in_=XRH[:, c, :]), [XRHB[c]], [ARB[xin]])
                    fw.op("act", lambda h: h.copy(out=AR[:, xin, 3:515], in_=psb(b)), [PSB[b]], [ARB[xin]])
                    wl = lambda tap: ppc("wlconv", (l * 4 + tap) * 4 + c)
                    fw.op("dve", lambda h: h.tensor_scalar(out=a32(cv), in0=AR[:, xin, 0:512], scalar1=wl(0), scalar2=ppc("blconv", l * 4 + c),
                                                            op0=ALU.mult, op1=ALU.add), [ARB[xin], PPB], [ARB[cv]])
                    for tap in (1, 2, 3):
                        fw.op("dve", lambda h: h.scalar_tensor_tensor(out=a32(cv), in0=AR[:, xin, tap:tap + 512], scalar=wl(tap), in1=a32(cv),
                                                                       op0=ALU.mult, op1=ALU.add), [ARB[xin], PPB, ARB[cv]], [ARB[cv]])
                    fw.op("pool", lambda h: h.tensor_copy(out=XRH[:, c, :], in_=AR[:, xin, 512:515]), [ARB[xin]], [XRHB[c]])
                    fw.op("pool", lambda h: h.tensor_copy(out=cvb_ap, in_=a32(cv)), [ARB[cv]], [cvbB])

            for i2 in range(2):
                bxc = win_slot(2048 + i2 * 256)
                bcg = win_slot(1536 + i2 * 256)
                bbg = win_slot(1024 + i2 * 256)
                for mi in range(2):
                    c = i2 * 2 + mi
                    xc32 = tmp(0 + 3 * (c % 2))
                    acc = tmp(1 + 3 * (c % 2))
                    zz = tmp(2 + 3 * (c % 2))
                    fw.op("pool", lambda h: h.tensor_copy(out=AR[:, zz, 0:2], in_=ZH[:, c, :]), [ZHB[c]], [ARB[zz]])
                    b = bxc[mi]
                    fw.op("act", lambda h: h.copy(out=a32(xc32), in_=psb(b)), [PSB[b]], [ARB[xc32]])
                    b = bcg[mi]
                    fw.op("dve", lambda h: h.tensor_tensor(out=AR[:, zz, 2:514], in0=psb(b), in1=a32(xc32), op=ALU.mult), [PSB[b], ARB[xc32]], [ARB[zz]])
                    ws = lambda tap: ppc("wsconv", (l * 3 + tap) * 4 + c)
                    fw.op("dve", lambda h: h.tensor_scalar(out=a32(acc), in0=AR[:, zz, 0:512], scalar1=ws(0), scalar2=None, op0=ALU.mult), [ARB[zz], PPB], [ARB[acc]])
                    for tap in (1, 2):
                        fw.op("dve", lambda h: h.scalar_tensor_tensor(out=a32(acc), in0=AR[:, zz, tap:tap + 512], scalar=ws(tap), in1=a32(acc),
                                                                       op0=ALU.mult, op1=ALU.add), [ARB[zz], PPB, ARB[acc]], [ARB[acc]])
                    fw.op("pool", lambda h: h.tensor_copy(out=ZH[:, c, :], in_=AR[:, zz, 512:514]), [ARB[zz]], [ZHB[c]])
                    b = bbg[mi]
                    fw.op("dve", lambda h: h.tensor_tensor(out=o_ap(1, c), in0=psb(b), in1=a32(acc), op=ALU.mult), [PSB[b], ARB[acc]], [o_buf(1, c)])

            if ti == 0:
                cv_issue(upto_layer=0, count=26)
            for dh in range(2):
                bgs = {}
                for mi, bnk in enumerate(win_slot(4608 + dh * 256, (4, 5, 6, 7))):
                    bgs[dh * 2 + mi] = bnk
                d_ra = lambda c: 16 + c
                d_a2 = lambda c: 20 + c
                d_ib = lambda c: 24 + c
                brs, bis = {}, {}
                for c in (dh * 2, dh * 2 + 1):
                    cvb_ap, cvbB = d_cvb(c)
                    br = bank((0, 1, 2, 3))
                    mm(psb(br), WLA[:, c * 128:(c + 1) * 128], cvb_ap, True, True, [LWB, cvbB], [PSB[br]], True)
                    brs[c] = br
                for c in (dh * 2, dh * 2 + 1):
                    ra = d_ra(c)
                    fw.op("act", lambda h: h.activation(out=a32(ra), in_=psb(brs[c]), func=AF.Sigmoid, bias=ppc("bla", l * 4 + c)), [PSB[brs[c]], PPB], [ARB[ra]])
                for c in (dh * 2, dh * 2 + 1):
                    cvb_ap, cvbB = d_cvb(c)
                    bi = bank((0, 1, 2, 3))
                    mm(psb(bi), WLX[:, c * 128:(c + 1) * 128], cvb_ap, True, True, [LWB, cvbB], [PSB[bi]], True)
                    bis[c] = bi
                for c in (dh * 2, dh * 2 + 1):
                    ibt = d_ib(c)
                    fw.op("act", lambda h: h.activation(out=a32(ibt), in_=psb(bis[c]), func=AF.Sigmoid, bias=ppc("blx", l * 4 + c)), [PSB[bis[c]], PPB], [ARB[ibt]])
                for c in (dh * 2, dh * 2 + 1):
                    ra, a2s = d_ra(c), d_a2(c)
                    fw.op("act", lambda h: h.activation(out=a32(a2s), in_=a32(ra), func=AF.Exp, scale=SPt[:, 12 + c:13 + c]), [ARB[ra], SPB], [ARB[a2s]])
                    fw.op("act", lambda h: h.activation(out=a32(ra), in_=a32(ra), func=AF.Exp, scale=SPt[:, 8 + c:9 + c]), [ARB[ra], SPB], [ARB[ra]])
                    fw.op("dve", lambda h: h.tensor_tensor(out=a32(d_ib(c)), in0=a32(d_ib(c)), in1=a32(d_cv(c)), op=ALU.mult), [ARB[d_ib(c)], ARB[d_cv(c)]], [ARB[d_ib(c)]])
                for c in (dh * 2, dh * 2 + 1):
                    a2s, ibt = d_a2(c), d_ib(c)
                    fw.op("act", lambda h: h.activation(out=a32(a2s), in_=a32(a2s), func=AF.Sqrt, scale=-1.0, bias=EPSC[:, 1:2]), [ARB[a2s], EPSB], [ARB[a2s]])
                    fw.op("dve", lambda h: h.tensor_tensor(out=a32(ibt), in0=a32(ibt), in1=a32(a2s), op=ALU.mult), [ARB[ibt], ARB[a2s]], [ARB[ibt]])
                for c in (dh * 2, dh * 2 + 1):
                    ra, hh, ibt = d_ra(c), d_a2(c), d_ib(c)
                    fw.op("dve", lambda h: h.tensor_tensor_scan(out=a32(hh), data0=a32(ra), data1=a32(ibt), initial=HC[:, c:c + 1],
                                                                 op0=ALU.mult, op1=ALU.add), [ARB[ra], ARB[ibt], HCB[c]], [ARB[hh]])
                    fw.op("pool", lambda h: h.tensor_copy(out=HC[:, c:c + 1], in_=a32(hh, 1, 511)), [ARB[hh]], [HCB[c]])
                for c in (dh * 2, dh * 2 + 1):
                    gg, hh = d_ra(c), d_a2(c)
                    b2 = bgs[c]
                    fw.op("act", lambda h: h.activation(out=a32(gg), in_=psb(b2), func=AF.Gelu_apprx_tanh), [PSB[b2]], [ARB[gg]])
                    fw.op("dve", lambda h: h.tensor_tensor(out=o_ap(3, c), in0=a32(gg), in1=a32(hh), op=ALU.mult), [ARB[gg], ARB[hh]], [o_buf(3, c)])

            if ti == 0:
                cv_issue(upto_layer=0)
            for i2 in range(2):
                bu = win_slot(0 + i2 * 256)
                for mi in range(2):
                    c = i2 * 2 + mi
                    b = bu[mi]
                    fw.op("act", lambda h: h.activation(out=a32(8 + c), in_=psb(b), func=AF.Gelu_apprx_tanh), [PSB[b]], [ARB[8 + c]])
            sv0, sv0B = load_w(wpieces(WIN[l, :, 512:768], KC), CVW["w_in"][l])
            sv1, sv1B = load_w(wpieces(WIN[l, :, 768:1024], KC), CVW["w_in"][l])
            def sgu_chain(tt):
                b = bank((4, 5, 6, 7))
                for hf, (sv, svB) in enumerate(((sv0, sv0B), (sv1, sv1B))):
                    for kc in range(KC):
                        mm(psb(b, 256, hf * 256), XN[:, kc, tt * 128:(tt + 1) * 128], sv[:, kc, :], kc == 0, kc == KC - 1,
                           xnr(kc, kc == KC - 1) + [svB], [PSB[b]], kc == KC - 1)
                vg = tmp(0 + 3 * (tt % 2))
                junk = tmp(1 + 3 * (tt % 2))
                vn = tmp(2 + 3 * (tt % 2))
                rsb = RSB[tt]
                rcol = SM[:, 232 + tt:233 + tt]
                fw.op("act", lambda h: h.activation(out=a32(vg), in_=psb(b), func=AF.Gelu_apprx_tanh), [PSB[b]], [ARB[vg]])
                fw.op("act", lambda h: h.activation(out=a32(junk), in_=a32(vg), func=AF.Square, accum_out=rcol), [ARB[vg]], [ARB[junk], rsb])
                fw.op("act", lambda h: h.activation(out=rcol, in_=rcol, func=AF.Sqrt, scale=1.0 / BW, bias=EPSC[:, 0:1]), [rsb, EPSB], [rsb])
                fw.op("dve", lambda h: h.reciprocal(out=rcol, in_=rcol), [rsb], [rsb])
                fw.op("dve", lambda h: h.scalar_tensor_tensor(out=abf(vn)[:, 0:512], in0=a32(vg), scalar=rcol, in1=GSG[:], op0=ALU.mult, op1=ALU.mult),
                      [ARB[vg], rsb, LRB], [ARB[vn]])
                return vn

            def sgu_mix(tt, vn):
                for g in range(4):
                    o = psb(g, 128, tt * 128)
                    mm(o, abf(vn)[:, g * 128:(g + 1) * 128], WSG[:, g * 128:(g + 1) * 128], True, False, [ARB[vn], LWB], [PSB[g]], False)
                    mm(o, ONEB[0:1, 0:128], BSG[0:1, g * 128:(g + 1) * 128], False, False, [ONESB, LRB], [PSB[g]], False)
                    mm(o, ONEB[0:1, 0:128], BSG[0:1, 512 + g * 128:512 + (g + 1) * 128], False, True, [ONESB, LRB, ARB[vn]], [PSB[g]], True)
            prev = None
            for tt in range(4):
                vn = sgu_chain(tt)
                if prev is not None:
                    sgu_mix(*prev)
                prev = (tt, vn)
            sgu_mix(*prev)
            for g in range(4):
                fw.op("dve", lambda h: h.tensor_tensor(out=o_ap(0, g), in0=psb(g), in1=a32(8 + g), op=ALU.mult), [PSB[g], ARB[8 + g]], [o_buf(0, g)])

            for which in range(2):
                for i2 in range(2):
                    bq = win_slot((2560 if which == 0 else 3072) + i2 * 256)
                    for mi in range(2):
                        hd = i2 * 2 + mi
                        b = bq[mi]
                        q32 = (12 + hd) if which == 0 else tmp(0 + 2 * (hd % 2))
                        t2 = tmp(1 + 2 * (hd % 2)) if which == 1 else tmp(4 + (hd % 2))
                        fw.op("act", lambda h: h.copy(out=a32(q32), in_=psb(b)), [PSB[b]], [ARB[q32]])
                        b2 = bank()
                        mm(psb(b2), CST[:, 0:128], a32(q32), True, True, [CSTB, ARB[q32]], [PSB[b2]], True)
                        fw.op("dve", lambda h: h.tensor_tensor(out=a32(t2), in0=psb(b2), in1=CS[:, 1, :], op=ALU.mult), [PSB[b2], CSB], [ARB[t2]])
                        fw.op("dve", lambda h: h.tensor_tensor(out=a32(q32), in0=a32(q32), in1=CS[:, 0, :], op=ALU.mult), [ARB[q32], CSB], [ARB[q32]])
                        fw.op("dve", lambda h: h.tensor_tensor(out=a32(q32), in0=a32(q32), in1=a32(t2), op=ALU.add), [ARB[q32], ARB[t2]], [ARB[q32]])
                        if which == 0:
                            qb = abf(16 + hd // 2)[:, (hd % 2) * 512:(hd % 2) * 512 + 512]
                            fw.op("pool", lambda h: h.tensor_copy(out=qb, in_=a32(q32)), [ARB[q32]], [ARB[16 + hd // 2]])
                        else:
                            fw.op("pool", lambda h: h.tensor_copy(out=KCt[:, hd, tok0:tok0 + T], in_=a32(q32)), [ARB[q32]], [KCB])
                            kmt = SM[:, 240 + 2 * hd:242 + 2 * hd]
                            fw.op("dve", lambda h: h.tensor_reduce(out=kmt, in_=a32(q32).rearrange("p (b k) -> p b k", b=2), axis=AX.X, op=ALU.add),
                                  [ARB[q32]], [MXB[hd]])
                            fw.op("dve", lambda h: h.tensor_scalar(out=KM[:, hd, 2 * j:2 * j + 2], in0=kmt, scalar1=1.0 / 256.0, scalar2=None, op0=ALU.mult),
                                  [MXB[hd]], [KMB])
            GM = SM[:, 0:32]
            TOP = SM[:, 32:64]
            SEL = SM[:, 64:96]
            g3 = lambda ap: ap.rearrange("p (h n) -> p h n", h=4)
            for qt in range(4):
                ob = (4 * j + qt) // 2
                if ob == 0:
                    continue
                qs = slice(qt * 128, (qt + 1) * 128)
                MBq = SM[:, 96 + qt * 32:96 + qt * 32 + 32]
                for hd in range(4):
                    mm(psb(7, 8, qt * 32 + hd * 8), a32(12 + hd)[:, qs], KM[:, hd, :], True, True, [ARB[12 + hd], KMB], [PSB[7]], hd == 3)
                pn = CST[:, 640 + ob * 8:640 + ob * 8 + 8].unsqueeze(1).to_broadcast([128, 4, 8])
                fw.op("dve", lambda h: h.tensor_tensor(out=g3(GM), in0=g3(psb(7, 32, qt * 32)), in1=pn, op=ALU.add), [PSB[7], CSTB], [GMB])
                for hd in range(4):
                    fw.op("dve", lambda h: h.max(out=TOP[:, hd * 8:(hd + 1) * 8], in_=GM[:, hd * 8:(hd + 1) * 8]), [GMB], [TOPB])
                fw.op("dve", lambda h: h.tensor_tensor(out=g3(SEL), in0=g3(GM), in1=g3(TOP)[:, :, 2:3].to_broadcast([128, 4, 8]), op=ALU.is_ge),
                      [GMB, TOPB], [SELB])
                fw.op("dve", lambda h: h.tensor_scalar(out=MBq, in0=SEL, scalar1=-1.0, scalar2=BIG, op0=ALU.add, op1=ALU.mult), [SELB], [MBQB[qt]])
                fw.op("dve", lambda h: h.tensor_tensor(out=g3(MBq), in0=g3(MBq), in1=pn, op=ALU.add), [MBQB[qt], CSTB], [MBQB[qt]])
            sv0, sv0B = load_w(wpieces(WIN[l, :, 3584:3840], KC), CVW["w_in"][l])
            sv1, sv1B = load_w(wpieces(WIN[l, :, 3840:4096], KC), CVW["w_in"][l])
            for tt in range(4):
                b = bank()
                for hf, (sv, svB) in enumerate(((sv0, sv0B), (sv1, sv1B))):
                    for kc in range(KC):
                        mm(psb(b, 256, hf * 256), XN[:, kc, tt * 128:(tt + 1) * 128], sv[:, kc, :], kc == 0, kc == KC - 1,
                           xnr(kc, kc == KC - 1) + [svB], [PSB[b]], kc == KC - 1)
                fw.op("act", lambda h: h.copy(out=VCt[:, j * 4 + tt, :], in_=psb(b)), [PSB[b]], [VCB])

            PT = PS[:, 2048:3072].bitcast(BF16)
            ai = 0
            for qt in range(4):
                QT = 4 * j + qt
                ob = QT // 2
                half = QT % 2
                W = (ob + 1) * 256
                qs = slice(qt * 128, (qt + 1) * 128)
                two = W <= 1024
                nb = (W + 511) // 512
                nkt = W // 128
                ctx = {}

                def geom(hd):
                    par = hd % 2
                    so = (par * 1024) if two else 0
                    sbo = (par * 1032) if two else 0
                    sbanks = [PSB[(so + c * 512) // 512] for c in range(nb)]
                    pbB = [ARB[18 + par]] if two else [ARB[18], ARB[19]]
                    ptB = [PSB[4 + par]] if two else [PSB[4], PSB[5]]
                    ptsB = [ARB[20 + par]] if two else [ARB[20], ARB[21]]
                    return par, so, sbo, sbanks, pbB, ptB, ptsB

                def st_A(hd):
                    par, so, sbo, sbanks, pbB, ptB, ptsB = geom(hd)
                    for c in range(nb):
                        n = min(512, W - c * 512)
                        mm(PS[:, so + c * 512:so + c * 512 + n], abf(16 + hd // 2)[:, (hd % 2) * 512 + qt * 128:(hd % 2) * 512 + (qt + 1) * 128],
                           KCt[:, hd, c * 512:c * 512 + n], True, True, [ARB[16 + hd // 2], KCB], [sbanks[c]], True)

                def st_B(hd):
                    par, so, sbo, sbanks, pbB, ptB, ptsB = geom(hd)
                    Sps = PS[:, so:so + W]
                    if ob > 0:
                        v3 = PS[:, so:so + ob * 256].rearrange("p (n k) -> p n k", k=256)
                        fw.op("dve", lambda h: h.scalar_tensor_tensor(out=v3, in0=v3, scalar=SCALE,
                                                                       in1=SM[:, 96 + qt * 32 + hd * 8:96 + qt * 32 + hd * 8 + ob].unsqueeze(2).to_broadcast([128, ob, 256]),
                                                                       op0=ALU.mult, op1=ALU.add), sbanks + [MBQB[qt]], sbanks)
                    vo = PS[:, so + ob * 256:so + W]
                    fw.op("dve", lambda h: h.scalar_tensor_tensor(out=vo, in0=vo, scalar=SCALE, in1=CST[:, 128 + half * 256:128 + (half + 1) * 256],
                                                                   op0=ALU.mult, op1=ALU.add), sbanks + [CSTB], sbanks)
                    mx = SM[:, 224 + par:225 + par]
                    fw.op("dve", lambda h: h.reduce_max(out=mx, in_=Sps, axis=AX.X), sbanks, [MXB[par]])
                    fw.op("dve", lambda h: h.tensor_scalar(out=mx, in0=mx, scalar1=-1.0, scalar2=None, op0=ALU.mult), [MXB[par]], [MXB[par]])
                    PBv = abf2(18)[:, sbo:sbo + W]
                    fw.op("act", lambda h: h.activation(out=PBv, in_=Sps, func=AF.Exp, bias=mx, scale=1.0), sbanks + [MXB[par]], pbB)

                def st_C(hd):
                    par, so, sbo, sbanks, pbB, ptB, ptsB = geom(hd)
                    PBv = abf2(18)[:, sbo:sbo + W]
                    for kt in range(nkt):
                        fw.op("pe", lambda h: h.transpose(out=PT[:, so + kt * 128:so + (kt + 1) * 128], in_=PBv[:, kt * 128:(kt + 1) * 128], identity=IDB[:]),
                              pbB + [IDBB], ptB, kt == nkt - 1)

                def st_D(hd):
                    par, so, sbo, sbanks, pbB, ptB, ptsB = geom(hd)
                    PTSv = abf2(20)[:, sbo:sbo + W]
                    fw.op("act", lambda h: h.copy(out=PTSv, in_=PT[:, so:so + W]), ptB, ptsB)

                def st_E(hd):
                    par, so, sbo, sbanks, pbB, ptB, ptsB = geom(hd)
                    PTSv = abf2(20)[:, sbo:sbo + W]
                    ob_ = 6
                    oo = par * 256
                    for kt in range(nkt):
                        mm(psb(ob_, 128, oo), VCt[:, kt, hd * 128:(hd + 1) * 128], PTSv[:, kt * 128:(kt + 1) * 128], kt == 0, kt == nkt - 1,
                           [VCB] + ptsB, [PSB[ob_]], False)
                    for kt in range(nkt):
                        mm(psb(ob_, 128, oo + 128), ONEB[:], PTSv[:, kt * 128:(kt + 1) * 128], kt == 0, kt == nkt - 1,
                           [ONESB, VCB] + ptsB, [PSB[ob_]], kt == nkt - 1)
                    fw.op("dve", lambda h: h.reciprocal(out=RINV[:, par, :], in_=psb(ob_, 128, oo + 128)), [PSB[ob_]], [RINVB[par]])
                    fw.op("dve", lambda h: h.tensor_tensor(out=o_ap(2, hd)[:, qs], in0=psb(ob_, 128, oo), in1=RINV[:, par, :], op=ALU.mult),
                          [PSB[ob_], RINVB[par]], [o_buf(2, hd)])

                if two:
                    st_A(0)
                    st_B(0)
                    for hd in range(4):
                        if hd + 1 < 4:
                            st_A(hd + 1)
                            st_B(hd + 1)
                        st_C(hd)
                        st_D(hd)
                        st_E(hd)
                else:
                    st_A(0)
                    st_B(0)
                    st_C(0)
                    st_D(0)
                    for hd in range(4):
                        if hd + 1 < 4:
                            st_A(hd + 1)
                            st_B(hd + 1)
                        st_E(hd)
                        if hd + 1 < 4:
                            st_C(hd + 1)
                            st_D(hd + 1)

            if debug and (l, s, j) == (L - 1, 0, 0):
                for k in range(4):
                    for c in range(4):
                        fw.dma("pool", dbg_d[:, (k * 4 + c) * 512:(k * 4 + c + 1) * 512], o_ap(k, c), o_buf(k, c), Buf("dbg"), sembuf=o_buf(k, c))
            for mg in range(8):
                cv_issue(upto_layer=l + 1, count=1)
                bsl, bsB = WBS2[mg % 2], WBSB2[mg % 2]
                for k in range(4):
                    fw.dma("sp", bsl[:, k * 4:k * 4 + 4, :], WBR[l, k, :, mg * 256:(mg + 1) * 256].rearrange("(kc p) c -> p kc c", p=128), CVW["w_branch"][l], bsB)
                for k in range(4):
                    gsl, gsB = load_w(wpieces(WG[l, k, :, mg * 256:(mg + 1) * 256], KC), CVW["w_gate"][l])
                    for mi in range(2):
                        m = mg * 2 + mi
                        bgk = bank()
                        for kc in range(KC):
                            mm(psb(bgk), gsl[:, kc, mi * 128:(mi + 1) * 128], XN[:, kc, :], kc == 0, kc == KC - 1, [gsB] + xnr(kc, kc == KC - 1), [PSB[bgk]], kc == KC - 1)
                        bbk = bank()
                        for kc in range(4):
                            mm(psb(bbk), bsl[:, k * 4 + kc, mi * 128:(mi + 1) * 128], o_ap(k, kc), kc == 0, kc == 3, [bsB, o_buf(k, kc)], [PSB[bbk]], kc == 3)
                        sig = tmp((k * 2 + mi) % 4)
                        yacc = tmp(4 + mi)
                        fw.op("act", lambda h: h.activation(out=a32(sig), in_=psb(bgk), func=AF.Sigmoid, bias=ppc("bgate", (l * 4 + k) * 16 + m)),
                              [PSB[bgk], PPB], [ARB[sig]])
                        if k == 0:
                            fw.op("dve", lambda h: h.tensor_tensor(out=a32(yacc), in0=psb(bbk), in1=a32(sig), op=ALU.mult), [PSB[bbk], ARB[sig]], [ARB[yacc]])
                        else:
                            fw.op("dve", lambda h: h.tensor_tensor(out=a32(sig), in0=psb(bbk), in1=a32(sig), op=ALU.mult), [PSB[bbk], ARB[sig]], [ARB[sig]])
                            if k < 3:
                                fw.op("dve", lambda h: h.tensor_tensor(out=a32(yacc), in0=a32(yacc), in1=a32(sig), op=ALU.add), [ARB[yacc], ARB[sig]], [ARB[yacc]])
                            else:
                                ysl = 8 + m // 2
                                yv = abf(ysl)[:, (m % 2) * 512:(m % 2) * 512 + 512]
                                fw.op("dve", lambda h: h.tensor_tensor(out=yv, in0=a32(yacc), in1=a32(sig), op=ALU.add), [ARB[yacc], ARB[sig]], [ARB[ysl]])

            def y_rhs(kc):
                return abf(8 + kc // 2)[:, (kc % 2) * 512:(kc % 2) * 512 + 512]
            ybufs = [ARB[8 + i] for i in range(8)]
            for mg in range(8):
                bs = dense_fm(WO[l, :, mg * 256:(mg + 1) * 256], KC, y_rhs, ybufs, CVW["w_out"][l])
                for mi in range(2):
                    m = mg * 2 + mi
                    b = bs[mi]
                    fw.op("dve", lambda h: h.tensor_tensor(out=X[:, xp(m), :], in0=psb(b), in1=X[:, xp(m), :], op=ALU.add), [PSB[b], XB[xp(m)]], [XB[xp(m)]])

            norm_to_xn("gffn", l)

            def h_ap(fc):
                return abf(fc // 2)[:, (fc % 2) * 512:(fc % 2) * 512 + 512]

            def h_buf(fc):
                return ARB[fc // 2]
            nxt = tiles[ti + 1] if ti + 1 < len(tiles) else None
            for hf in range(2):
                for fg in range(11):
                    if hf == 0 and fg in (0, 5):
                        cv_issue(upto_layer=l + 1, count=1)
                    col0 = (hf * 11 + fg) * 256
                    s1, s1B = load_w(wpieces(W1[l, :, col0:col0 + 256], KC), CVW["w_ffn1"][l])
                    s3, s3B = load_w(wpieces(W3[l, :, col0:col0 + 256], KC), CVW["w_ffn3"][l])
                    for mi in range(2):
                        fc = fg * 2 + mi
                        b1 = bank()
                        for kc in range(KC):
                            mm(psb(b1), s1[:, kc, mi * 128:(mi + 1) * 128], XN[:, kc, :], kc == 0, kc == KC - 1, [s1B] + xnr(kc, kc == KC - 1), [PSB[b1]], kc == KC - 1)
                        b3 = bank()
                        for kc in range(KC):
                            mm(psb(b3), s3[:, kc, mi * 128:(mi + 1) * 128], XN[:, kc, :], kc == 0, kc == KC - 1, [s3B] + xnr(kc, kc == KC - 1), [PSB[b3]], kc == KC - 1)
                        sl = tmp(fc % 4)
                        fw.op("act", lambda h: h.activation(out=a32(sl), in_=psb(b1), func=AF.Silu), [PSB[b1]], [ARB[sl]])
                        fw.op("dve", lambda h: h.tensor_tensor(out=h_ap(fc), in0=psb(b3), in1=a32(sl), op=ALU.mult), [PSB[b3], ARB[sl]], [h_buf(fc)])
                hbufs = [ARB[i] for i in range(11)]
                early = nxt is not None and (nxt[1], nxt[2]) != (s, j)
                if hf == 1 and early:
                    for kc2 in (KC - 2, KC - 1):
                        load_x_chunk(nxt[0], nxt[1], nxt[2], kc2, next_phys(kc2))
                for mg in range(8):
                    r0 = hf * 22 * 128
                    sa, saB = load_w(wpieces(W2[l, r0:r0 + 11 * 128, mg * 256:(mg + 1) * 256], 11), CVW["w_ffn2"][l])
                    sb_, sbB = load_w(wpieces(W2[l, r0 + 11 * 128:r0 + 22 * 128, mg * 256:(mg + 1) * 256], 11), CVW["w_ffn2"][l])
                    bks = [bank(), bank()]
                    for si, (sl_, slB) in enumerate(((sa, saB), (sb_, sbB))):
                        for mi in range(2):
                            for kc in range(11):
                                fc = si * 11 + kc
                                last = (si == 1 and kc == 10)
                                mm(psb(bks[mi]), sl_[:, kc, mi * 128:(mi + 1) * 128], h_ap(fc), si == 0 and kc == 0, last,
                                   [slB] + hbufs, [PSB[bks[mi]]] if (last or (si == 0 and kc == 0)) else [], last or kc == 10)
                    for mi in range(2):
                        m = mg * 2 + mi
                        b = bks[mi]
                        fw.op("dve", lambda h: h.tensor_tensor(out=X[:, xp(m), :], in0=psb(b), in1=X[:, xp(m), :], op=ALU.add), [PSB[b], XB[xp(m)]], [XB[xp(m)]])
                        if hf == 1 and l < L - 1:
                            store_x_chunk(l, s, j, m, xp(m))
                            if nxt is not None and (m < KC - 2 or not early):
                                load_x_chunk(nxt[0], nxt[1], nxt[2], m, next_phys(m))
            if l == L - 1:
                rs = tmp(2)
                rms_stats(rs)
                for kc in range(KC):
                    fw.op("dve", lambda h: h.scalar_tensor_tensor(out=X[:, xp(kc), :], in0=X[:, xp(kc), :], scalar=ppc("gfin", kc), in1=a32(rs),
                                                                    op0=ALU.mult, op1=ALU.mult), [XB[xp(kc)], PPB, ARB[rs]], [XB[xp(kc)]])
                    store_x_chunk(l, s, j, kc, xp(kc))
                    if nxt is not None and (kc < KC - 2 or not early):
                        load_x_chunk(nxt[0], nxt[1], nxt[2], kc, next_phys(kc))

        for kc in range(KC):
            load_x_chunk(0, 0, 0, kc, xp(kc))
        ti = 0
        for l in range(L):
            layer_prep(l)
            for s in range(NSEQ):
                seq_reset()
                for j in range(NT):
                    tile(ti)
                    rotate_x()
                    ti += 1
            cv_issue(upto_layer=l + 1)
        outb = [ORG[s][j][kc] for s in range(NSEQ) for j in range(NT) for kc in range(KC)]
        fw.finish("pool", outb)
        fw.finish("sp", outb)
        build.stats = (fw.nins, fw.nwait)
        build.sbuf_left = nc.sbuf_bytes_remaining
    return nc


def make_in_maps(inputs, L, nseq, ncores):
    rope, cst = host_consts()
    pp, mats, rows = host_params(inputs, L)
    x = np.asarray(inputs["x"], np.float32)
    common = {"pp": pp, "mats": mats, "rows": rows, "rope": rope, "cst": cst}
    for n in WNAMES:
        common[n] = np.ascontiguousarray(np.asarray(inputs[n], np.float32)[:L]).reshape(L, 128, -1)
    maps = []
    for c in range(ncores):
        m = dict(common)
        m["xT"] = np.ascontiguousarray(x[c * nseq:(c + 1) * nseq].transpose(0, 2, 1))
        maps.append(m)
    return maps


def kernel(**inputs):
    nc = build(DEPTH, SEQ_PER_CORE, S // T)
    maps = make_in_maps(inputs, DEPTH, SEQ_PER_CORE, NCORES)
    res = run_bass_kernel_spmd(nc, maps, core_ids=list(range(NCORES)))
    outs = [np.asarray(r["oT"], np.float32).transpose(0, 2, 1) for r in res.results]
    return np.ascontiguousarray(np.concatenate(outs, axis=0))
```

```python
import numpy as np
from contextlib import ExitStack
import concourse.bass as bass
import concourse.mybir as mybir
from concourse.bass_utils import run_bass_kernel_spmd

F32 = mybir.dt.float32
BF16 = mybir.dt.bfloat16
AF = mybir.ActivationFunctionType
ALU = mybir.AluOpType
AX = mybir.AxisListType

D = 2048
S = 2048
T = 512
KC = 16
INW = 5120
DFF = 5632
BW = 512
NEG = -1e30
BIG = 1e30
EPS = 1e-6
NCORES = 8
SEQ_PER_CORE = 2
DEPTH = 4
SCALE = 128 ** -0.5


class Buf:
    __slots__ = ("name", "w", "r", "dsem", "dcnt")

    def __init__(self, name):
        self.name = name
        self.w = None
        self.r = {}
        self.dsem = None
        self.dcnt = 0


class Eng:
    def __init__(self, name, h, sem):
        self.name = name
        self.h = h
        self.sem = sem
        self.cnt = 0
        self.seen = {}


class FW:
    def __init__(self, nc, stack):
        self.nc = nc
        self.stack = stack
        self.engs = {}
        for n in ("pe", "act", "dve", "pool", "sp"):
            h = {"pe": nc.tensor, "act": nc.scalar, "dve": nc.vector, "pool": nc.gpsimd, "sp": nc.sync}[n]
            sem = stack.enter_context(nc.semaphore("sem_" + n))
            self.engs[n] = Eng(n, h, sem)
        self.nwait = 0
        self.nins = 0

    def _waits(self, E, reads, writes):
        need = {}

        def add(tok):
            sem, val = tok
            k = id(sem)
            if k not in need or need[k][1] < val:
                need[k] = (sem, val)
        for b in reads:
            if b.w is not None:
                add(b.w)
        for b in writes:
            if b.w is not None and b.w[0] is not E.sem:
                add(b.w)
            for tok in b.r.values():
                if tok[0] is not E.sem:
                    add(tok)
        for k, (sem, val) in need.items():
            if E.seen.get(k, 0) >= val:
                continue
            E.h.wait_ge(sem, val)
            E.seen[k] = val
            self.nwait += 1

    def _record(self, tok, reads, writes):
        k = id(tok[0])
        for b in reads:
            b.r[k] = tok
        for b in writes:
            b.w = tok
            b.r = {}

    def op(self, eng, fn, reads=(), writes=(), mark=True):
        E = self.engs[eng]
        self._waits(E, reads, writes)
        ins = fn(E.h)
        self.nins += 1
        if mark:
            E.cnt += 1
            ins.then_inc(E.sem, 1)
            self._record((E.sem, E.cnt), reads, writes)
        return ins

    def dma(self, q, out_ap, in_ap, src, dst, sembuf=None):
        E = self.engs[q]
        sb = sembuf if sembuf is not None else dst
        if sb.dsem is None:
            sb.dsem = self.stack.enter_context(self.nc.semaphore("d_" + sb.name))
        same_fill = (dst.w is not None and dst.w[0] is sb.dsem and not dst.r)
        if same_fill:
            saved = dst.w
            dst.w = None
            self._waits(E, [src], [dst])
            dst.w = saved
        else:
            self._waits(E, [src], [dst])
        sb.dcnt += 16
        ins = E.h.dma_start(out=out_ap, in_=in_ap)
        ins.then_inc(sb.dsem, 16)
        self.nins += 1
        self._record((sb.dsem, sb.dcnt), [src], [dst])
        return ins

    def finish(self, eng, bufs):
        E = self.engs[eng]
        self._waits(E, list(bufs), list(bufs))


def pp_layout(L):
    off = {}
    n = 0
    for name, w in [("gmix", L * 16), ("gffn", L * 16), ("gfin", 16), ("bgate", L * 64), ("wsconv", L * 12),
                    ("wlconv", L * 16), ("blconv", L * 4), ("bla", L * 4), ("blx", L * 4), ("lam", L * 4)]:
        off[name] = n
        n += w
    return off, n


def _cols(a):
    return np.ascontiguousarray(np.asarray(a, np.float32).reshape(-1, 128).T)


def host_consts():
    half = 64
    inv = (10000.0 ** (-(np.arange(half, dtype=np.float32)) / np.float32(half))).astype(np.float32)
    ang = np.arange(S, dtype=np.float32)[None, :] * inv[:, None]
    cos = np.cos(ang).astype(np.float32)
    sin = np.sin(ang).astype(np.float32)
    rope = np.concatenate([np.concatenate([cos, cos], 0), np.concatenate([-sin, sin], 0)], axis=1)
    perm = np.zeros((128, 128), np.float32)
    for m in range(128):
        perm[(m + 64) % 128, m] = 1.0
    triu = (np.arange(128)[:, None] <= np.arange(128)[None, :]).astype(np.float32)
    q = np.arange(128)[:, None]
    key = np.arange(256)[None, :]
    causal = np.concatenate([np.where(key <= q, 0.0, NEG), np.where(key <= 128 + q, 0.0, NEG)], axis=1).astype(np.float32)
    pastneg = np.zeros((8, 8), np.float32)
    for ob in range(8):
        for n in range(8):
            pastneg[ob, n] = 0.0 if n < ob else NEG
    pastneg = np.broadcast_to(pastneg.reshape(1, 64), (128, 64)).astype(np.float32)
    ident = np.eye(128, dtype=np.float32)
    cst = np.concatenate([perm, causal, pastneg, triu, ident], axis=1)
    return np.ascontiguousarray(rope.astype(np.float32)), np.ascontiguousarray(cst)


def host_params(inp, L):
    off, npp = pp_layout(L)
    pp = np.zeros((128, npp), np.float32)

    def put(name, a):
        c = _cols(a)
        pp[:, off[name]:off[name] + c.shape[1]] = c
    put("gmix", inp["g_mix"][:L])
    put("gffn", inp["g_ffn"][:L])
    put("gfin", inp["g_final"])
    put("bgate", inp["b_gate"][:L])
    put("wsconv", inp["w_sconv"][:L])
    put("wlconv", inp["w_lru_conv"][:L])
    put("blconv", inp["b_lru_conv"][:L])
    put("bla", inp["b_lru_a"][:L])
    put("blx", inp["b_lru_x"][:L])
    put("lam", inp["lru_lambda"][:L])
    wla = np.asarray(inp["w_lru_a"][:L], np.float32).transpose(2, 0, 1, 3).reshape(128, -1)
    wlx = np.asarray(inp["w_lru_x"][:L], np.float32).transpose(2, 0, 1, 3).reshape(128, -1)
    wsg = np.asarray(inp["w_sgu"][:L], np.float32).transpose(3, 0, 1, 2).reshape(128, -1)
    mats = np.ascontiguousarray(np.concatenate([wla, wlx, wsg], axis=1))
    bsgu = np.asarray(inp["b_sgu"][:L], np.float32).transpose(0, 2, 1).reshape(1, -1)
    gsgu = np.asarray(inp["g_sgu"][:L], np.float32).reshape(1, -1)
    rows = np.ascontiguousarray(np.concatenate([bsgu, gsgu], axis=1))
    return pp, mats, rows


WNAMES = ["w_in", "w_gate", "w_branch", "w_out", "w_ffn1", "w_ffn3", "w_ffn2"]


def wshape(name, L):
    return {"w_in": [L, D, INW], "w_gate": [L, 4, D, D], "w_branch": [L, 4, BW, D], "w_out": [L, D, D],
            "w_ffn1": [L, D, DFF], "w_ffn3": [L, D, DFF], "w_ffn2": [L, DFF, D]}[name]


def build(L=DEPTH, NSEQ=SEQ_PER_CORE, NT=S // T, debug=False):
    nc = bass.Bass("TRN2", target_bir_lowering=False)
    off, npp = pp_layout(L)
    CHK = 2048

    xT_d = nc.dram_tensor("xT", [NSEQ, D, S], F32, kind="ExternalInput").ap()
    oT_d = nc.dram_tensor("oT", [NSEQ, D, S], F32, kind="ExternalOutput").ap()
    pp_d = nc.dram_tensor("pp", [128, npp], F32, kind="ExternalInput").ap()
    mats_d = nc.dram_tensor("mats", [128, 3 * L * 512], F32, kind="ExternalInput").ap()
    rows_d = nc.dram_tensor("rows", [1, 2 * L * 512], F32, kind="ExternalInput").ap()
    rope_d = nc.dram_tensor("rope", [128, 2 * S], F32, kind="ExternalInput").ap()
    cst_d = nc.dram_tensor("cst", [128, 960], F32, kind="ExternalInput").ap()
    wsrc = {}
    wbf = {}
    for n in WNAMES:
        shp = wshape(n, L)
        E = int(np.prod(shp))
        wsrc[n] = nc.dram_tensor(n, [L, 128, E // L // 128], F32, kind="ExternalInput").ap()
        wbf[n] = nc.dram_tensor(n + "_bf", [E], BF16).ap()
    xscr = (nc.dram_tensor("xscr", [NSEQ, D, S], F32, kind="ExternalOutput") if debug else nc.dram_tensor("xscr", [NSEQ, D, S], F32)).ap()
    dbg_d = nc.dram_tensor("dbg", [128, 16 * 512], BF16, kind="ExternalOutput").ap() if debug else None

    def wview(n):
        shp = wshape(n, L)
        if len(shp) == 3:
            return wbf[n].rearrange("(l r c) -> l r c", l=shp[0], r=shp[1])
        return wbf[n].rearrange("(l k r c) -> l k r c", l=shp[0], k=shp[1], r=shp[2])

    with ExitStack() as st:
        fw = FW(nc, st)

        def sb(name, shape, dt=F32):
            return st.enter_context(nc.sbuf_tensor(name, shape, dt))

        X = sb("X", [128, KC + 2, T])
        XB = [Buf("X%d" % i) for i in range(KC + 2)]
        xst = {"map": list(range(KC)), "spare": [KC, KC + 1]}

        def xp(kc):
            return xst["map"][kc]
        XN = sb("XN", [128, KC, T], BF16)
        XNBs = [Buf("XN%d" % i) for i in range(KC)]

        def xnr(kc, last=False):
            return list(XNBs) if last else [XNBs[kc]]
        NSLOT = 4
        WS = [sb("WS%d" % i, [128, 16, 256], BF16) for i in range(NSLOT)]
        WSB = [Buf("WS%d" % i) for i in range(NSLOT)]
        WBS2 = [sb("WBS%d" % i, [128, 16, 256], BF16) for i in range(2)]
        WBSB2 = [Buf("WBS%d" % i) for i in range(2)]
        KCt = sb("KC", [128, 4, S], BF16)
        KCB = Buf("KC")
        VCt = sb("VC", [128, 16, 512], BF16)
        VCB = Buf("VC")
        PP = sb("PP", [128, npp])
        PPB = Buf("PP")
        CST = sb("CST", [128, 704])
        CSTB = Buf("CST")
        IDB = sb("IDB", [128, 128], BF16)
        IDBB = Buf("IDB")
        ONEB = sb("ONEB", [128, 128], BF16)
        ONE32 = sb("ONE32", [128, 128])
        ONESB = Buf("ONES")
        WLA = sb("WLA", [128, 512], BF16)
        WLX = sb("WLX", [128, 512], BF16)
        WSG = sb("WSG", [128, 512], BF16)
        LWB = Buf("LW")
        BSG = sb("BSG", [1, 1024], BF16)
        GSG = sb("GSG", [128, 512])
        LRB = Buf("LR")
        SPt = sb("SPt", [128, 16])
        SPB = Buf("SP")
        CS = sb("CS", [128, 2, T])
        CSB = Buf("CS")
        ZH = sb("ZH", [128, 4, 2])
        ZHB = [Buf("ZH%d" % i) for i in range(4)]
        XRH = sb("XRH", [128, 4, 3])
        XRHB = [Buf("XRH%d" % i) for i in range(4)]
        HC = sb("HC", [128, 4])
        HCB = [Buf("HC%d" % i) for i in range(4)]
        KM = sb("KM", [128, 4, 8])
        KMB = Buf("KM")
        SM = sb("SM", [128, 248])
        MBQB = [Buf("MBQ%d" % i) for i in range(4)]
        GMB, TOPB, SELB, MBB = Buf("GM"), Buf("TOP"), Buf("SEL"), Buf("MBm")
        MXB = [Buf("MX%d" % i) for i in range(4)]
        RSB = [Buf("RS%d" % i) for i in range(4)]
        RINV = sb("RINV", [128, 2, 128])
        RINVB = [Buf("RINV0"), Buf("RINV1")]

        NAR = 28
        AR = sb("AR", [128, NAR, 516])
        ARB = [Buf("AR%d" % i) for i in range(NAR)]

        def a32(i, n=512, o=0):
            return AR[:, i, o:o + n]

        def abf(i):
            return AR[:, i, :].bitcast(BF16)

        def abf2(i):
            return AR[:, i:i + 2, :].rearrange("p a b -> p (a b)").bitcast(BF16)

        TM0 = 22

        def tmp(k):
            return TM0 + k

        def o_ap(k, c):
            return abf(k * 2 + c // 2)[:, (c % 2) * 512:(c % 2) * 512 + 512]

        def o_buf(k, c):
            return ARB[k * 2 + c // 2]

        PS = st.enter_context(nc.psum_tensor("PS", [128, 4096], F32))
        PSB = [Buf("PS%d" % i) for i in range(8)]
        rot = {"i": 0}

        def bank(allowed=(0, 1, 2, 3, 4, 5, 6, 7)):
            while True:
                b = rot["i"] % 8
                rot["i"] += 1
                if b in allowed:
                    return b

        def psb(b, n=512, o=0):
            return PS[:, b * 512 + o:b * 512 + o + n]

        wrot = {"i": 0}
        DW = Buf("DW")

        def load_w(src_fn_list, dep):
            i = wrot["i"] % NSLOT
            wrot["i"] += 1
            for ap, kc0, n in src_fn_list:
                fw.dma("sp", WS[i][:, kc0:kc0 + n, :], ap, dep, WSB[i])
            return WS[i], WSB[i]

        def wpieces(ap2d, nkc):
            v = ap2d.rearrange("(kc p) c -> p kc c", p=128)
            out = []
            k = 0
            while k < nkc:
                n = min(8, nkc - k)
                out.append((v[:, k:k + n, :], k, n))
                k += n
            return out

        def mm(out, lhsT, rhs, start, stop, reads, writes, mark):
            return fw.op("pe", lambda h: h.matmul(out, lhsT=lhsT, rhs=rhs, start=start, stop=stop), reads, writes, mark)

        DIN = Buf("DIN")
        fw.dma("pool", PP[:], pp_d[:, :], DIN, PPB)
        fw.dma("pool", CST[:], cst_d[:, 0:704], DIN, CSTB)
        fw.dma("pool", a32(0, 256), cst_d[:, 704:960], DIN, ARB[0])
        fw.op("dve", lambda h: h.tensor_copy(out=IDB[:], in_=a32(0, 128, 128)), [ARB[0]], [IDBB])
        fw.op("dve", lambda h: h.memset(ONEB[:], 1.0), [], [ONESB])
        fw.op("dve", lambda h: h.memset(ONE32[:], 1.0), [], [ONESB])

        CVS = {n: Buf("cvs_" + n) for n in WNAMES}
        CVW = {n: [Buf("cvw_%s_%d" % (n, l)) for l in range(L)] for n in WNAMES}
        CVF = 8192
        cvjobs = []
        for l in range(L):
            for n in ["w_in", "w_branch", "w_gate", "w_out", "w_ffn1", "w_ffn3", "w_ffn2"]:
                El = int(np.prod(wshape(n, L))) // L
                per = El // 128
                assert per % CVF == 0
                for c in range(per // CVF):
                    cvjobs.append((l, n, c, El))
        cvpos = {"i": 0}

        def cv_issue(upto_layer=None, count=None):
            done = 0
            while cvpos["i"] < len(cvjobs):
                l, n, c, El = cvjobs[cvpos["i"]]
                if upto_layer is not None and l > upto_layer:
                    break
                if count is not None and done >= count:
                    break
                dst = wbf[n][l * El:(l + 1) * El].rearrange("(p f) -> p f", p=128)[:, c * CVF:(c + 1) * CVF]
                fw.dma("pool", dst, wsrc[n][l, :, c * CVF:(c + 1) * CVF], DIN, CVW[n][l], sembuf=CVS[n])
                cvpos["i"] += 1
                done += 1
        cv_issue(upto_layer=0, count=10)

        def ppc(name, idx):
            return PP[:, off[name] + idx:off[name] + idx + 1]

        def rms_stats(rs_slot):
            b = bank()
            for kc in range(KC):
                sq = tmp(kc % 2)
                sqv = abf(sq)[:, (kc // 2 % 2) * 512:(kc // 2 % 2) * 512 + 512]
                fw.op("act", lambda h: h.activation(out=sqv, in_=X[:, xp(kc), :], func=AF.Square), [XB[xp(kc)]], [ARB[sq]])
                mm(psb(b), ONEB[:], sqv, kc == 0, kc == KC - 1, [ONESB, ARB[sq]], [PSB[b]], True)
            fw.op("act", lambda h: h.activation(out=a32(rs_slot), in_=psb(b), func=AF.Sqrt, scale=1.0 / D, bias=EPSC[:, 0:1]),
                  [PSB[b], EPSB], [ARB[rs_slot]])
            fw.op("dve", lambda h: h.reciprocal(out=a32(rs_slot), in_=a32(rs_slot)), [ARB[rs_slot]], [ARB[rs_slot]])

        EPSC = sb("EPSC", [128, 2])
        EPSB = Buf("EPS")
        fw.op("dve", lambda h: h.memset(EPSC[:, 0:1], EPS), [], [EPSB])
        fw.op("dve", lambda h: h.memset(EPSC[:, 1:2], 1.0), [], [EPSB])

        def norm_to_xn(gname, l):
            rs = tmp(2)
            rms_stats(rs)
            for kc in range(KC):
                fw.op("dve", lambda h: h.scalar_tensor_tensor(out=XN[:, kc, :], in0=X[:, xp(kc), :], scalar=ppc(gname, l * 16 + kc),
                                                                in1=a32(rs), op0=ALU.mult, op1=ALU.mult),
                      [XB[xp(kc)], PPB, ARB[rs]], [XNBs[kc]])

        def dense_fm(src2d, nkc, rhs_fn, rhs_bufs, dep, allowed=(0, 1, 2, 3, 4, 5, 6, 7)):
            slot, sB = load_w(wpieces(src2d, nkc), dep)
            outs = []
            for mi in range(2):
                b = bank(allowed)
                for kc in range(nkc):
                    rb = rhs_bufs(kc, kc == nkc - 1) if callable(rhs_bufs) else rhs_bufs
                    mm(psb(b), slot[:, kc, mi * 128:(mi + 1) * 128], rhs_fn(kc), kc == 0, kc == nkc - 1,
                       [sB] + rb, [PSB[b]], kc == nkc - 1)
                outs.append(b)
            return outs

        def xn_rhs(kc):
            return XN[:, kc, :]

        XRG = [[[Buf("XRG%d_%d_%d" % (s, j, kc)) for kc in range(KC)] for j in range(NT)] for s in range(NSEQ)]
        ORG = [[[Buf("ORG%d_%d_%d" % (s, j, kc)) for kc in range(KC)] for j in range(NT)] for s in range(NSEQ)]
        tiles = [(l, s, j) for l in range(L) for s in range(NSEQ) for j in range(NT)]

        def load_x_chunk(l, s, j, kc, ph):
            if l == 0:
                fw.dma("act", X[:, ph, :], xT_d[s, kc * 128:(kc + 1) * 128, j * T:(j + 1) * T], DIN, XB[ph])
            else:
                fw.dma("act", X[:, ph, :], xscr[s, kc * 128:(kc + 1) * 128, j * T:(j + 1) * T], XRG[s][j][kc], XB[ph])

        def store_x_chunk(l, s, j, kc, ph):
            if l == L - 1:
                fw.dma("act", oT_d[s, kc * 128:(kc + 1) * 128, j * T:(j + 1) * T], X[:, ph, :], XB[ph], ORG[s][j][kc], sembuf=XB[ph])
            else:
                fw.dma("act", xscr[s, kc * 128:(kc + 1) * 128, j * T:(j + 1) * T], X[:, ph, :], XB[ph], XRG[s][j][kc], sembuf=XB[ph])

        def next_phys(kc):
            return xst["map"][kc] if kc < KC - 2 else xst["spare"][kc - (KC - 2)]

        def rotate_x():
            m = xst["map"]
            sp = xst["spare"]
            xst["map"] = m[:KC - 2] + sp
            xst["spare"] = m[KC - 2:]

        def layer_prep(l):
            s0 = tmp(3)
            for idx, dstt in enumerate((WLA, WLX)):
                fw.dma("pool", a32(s0), mats_d[:, (idx * L + l) * 512:(idx * L + l + 1) * 512], DIN, ARB[s0])
                fw.op("dve", lambda h: h.tensor_copy(out=dstt[:], in_=a32(s0)), [ARB[s0]], [LWB])
            fw.dma("pool", a32(s0), mats_d[:, (2 * L + l) * 512:(2 * L + l + 1) * 512], DIN, ARB[s0])
            st_ = tmp(5)
            fw.dma("pool", a32(st_, 128), cst_d[:, 704:832], DIN, ARB[st_])
            fw.op("dve", lambda h: h.tensor_tensor(out=WSG[:].rearrange("p (g t) -> p g t", g=4),
                                                    in0=a32(s0).rearrange("p (g t) -> p g t", g=4),
                                                    in1=a32(st_, 128).unsqueeze(1).to_broadcast([128, 4, 128]), op=ALU.mult),
                  [ARB[s0], ARB[st_]], [LWB])
            s1 = tmp(4)
            fw.dma("pool", AR[0:1, s1, 0:512], rows_d[0:1, l * 512:(l + 1) * 512], DIN, ARB[s1])
            fw.op("dve", lambda h: h.tensor_copy(out=BSG[0:1, 0:512], in_=AR[0:1, s1, 0:512]), [ARB[s1]], [LRB])
            s2 = tmp(5)
            fw.op("dve", lambda h: h.tensor_copy(out=AR[0:1, s2, 0:512], in_=BSG[0:1, 0:512]), [LRB], [ARB[s2]])
            fw.op("dve", lambda h: h.tensor_tensor(out=BSG[0:1, 512:1024], in0=AR[0:1, s1, 0:512], in1=AR[0:1, s2, 0:512], op=ALU.subtract),
                  [ARB[s1], ARB[s2]], [LRB])
            fw.dma("pool", GSG[:], rows_d[0:1, (L + l) * 512:(L + l + 1) * 512].partition_broadcast(128), DIN, LRB)
            lam = PP[:, off["lam"] + l * 4:off["lam"] + l * 4 + 4]
            z = SPt[:, 0:4]
            sp = SPt[:, 4:8]
            t8 = SPt[:, 8:12]
            t16 = SPt[:, 12:16]
            fw.op("act", lambda h: h.activation(out=z, in_=lam, func=AF.Exp, scale=-1.0), [PPB], [SPB])
            fw.op("dve", lambda h: h.tensor_scalar(out=sp, in0=z, scalar1=-0.2, scalar2=0.25, op0=ALU.mult, op1=ALU.add), [SPB], [SPB])
            for cst in (1.0 / 3.0, 0.5, 1.0):
                fw.op("dve", lambda h: h.tensor_tensor(out=sp, in0=sp, in1=z, op=ALU.mult), [SPB], [SPB])
                fw.op("dve", lambda h: h.tensor_scalar(out=sp, in0=sp, scalar1=-1.0, scalar2=cst, op0=ALU.mult, op1=ALU.add), [SPB], [SPB])
            fw.op("dve", lambda h: h.tensor_tensor(out=sp, in0=sp, in1=z, op=ALU.mult), [SPB], [SPB])
            fw.op("act", lambda h: h.activation(out=t8, in_=z, func=AF.Ln, bias=EPSC[:, 1:2], scale=1.0), [SPB, EPSB], [SPB])
            fw.op("dve", lambda h: h.tensor_scalar(out=t16, in0=z, scalar1=0.05, scalar2=None, op0=ALU.is_ge), [SPB], [SPB])
            fw.op("dve", lambda h: h.tensor_tensor(out=t8, in0=t8, in1=sp, op=ALU.subtract), [SPB], [SPB])
            fw.op("dve", lambda h: h.tensor_tensor(out=t8, in0=t8, in1=t16, op=ALU.mult), [SPB], [SPB])
            fw.op("dve", lambda h: h.tensor_tensor(out=sp, in0=sp, in1=t8, op=ALU.add), [SPB], [SPB])
            fw.op("dve", lambda h: h.tensor_scalar(out=t8, in0=sp, scalar1=-8.0, scalar2=None, op0=ALU.mult), [SPB], [SPB])
            fw.op("dve", lambda h: h.tensor_scalar(out=t16, in0=sp, scalar1=-16.0, scalar2=None, op0=ALU.mult), [SPB], [SPB])

        def seq_reset():
            for c in range(4):
                fw.op("pool", lambda h: h.memset(ZH[:, c, :], 0.0), [], [ZHB[c]])
                fw.op("pool", lambda h: h.memset(XRH[:, c, :], 0.0), [], [XRHB[c]])
                fw.op("pool", lambda h: h.memset(HC[:, c:c + 1], 0.0), [], [HCB[c]])
            fw.op("pool", lambda h: h.memset(KM[:], 0.0), [], [KMB])

        def tile(ti):
            l, s, j = tiles[ti]
            WIN = wview("w_in")
            WG = wview("w_gate")
            WBR = wview("w_branch")
            WO = wview("w_out")
            W1 = wview("w_ffn1")
            W3 = wview("w_ffn3")
            W2 = wview("w_ffn2")
            tok0 = j * T

            fw.dma("pool", CS[:, 0, :], rope_d[:, tok0:tok0 + T], DIN, CSB)
            fw.dma("pool", CS[:, 1, :], rope_d[:, S + tok0:S + tok0 + T], DIN, CSB)

            norm_to_xn("gmix", l)
            if ti == 0:
                cv_issue(upto_layer=0, count=20)

            def win_slot(col0, allowed=(0, 1, 2, 3, 4, 5, 6, 7)):
                return dense_fm(WIN[l, :, col0:col0 + 256], KC, xn_rhs, xnr, CVW["w_in"][l], allowed)

            def d_cv(c):
                return 8 + c

            def d_cvb(c):
                return abf(12 + c // 2)[:, (c % 2) * 512:(c % 2) * 512 + 512], ARB[12 + c // 2]
            for i2 in range(2):
                bx = win_slot(4096 + i2 * 256)
                for mi in range(2):
                    c = i2 * 2 + mi
                    cv = d_cv(c)
                    cvb_ap, cvbB = d_cvb(c)
                    xin = 14 + (c % 2)
                    b = bx[mi]
                    fw.op("pool", lambda h: h.tensor_copy(out=AR[:, xin, 0:3], in_=XRH[:, c, :]), [XRHB[c]], [ARB[xin]])
                    fw.op("act", lambda h: h.copy(out=AR[:, xin, 3:515], in_=psb(b)), [PSB[b]], [ARB[xin]])
                    wl = lambda tap: ppc("wlconv", (l * 4 + tap) * 4 + c)
                    fw.op("dve", lambda h: h.tensor_scalar(out=a32(cv), in0=AR[:, xin, 0:512], scalar1=wl(0), scalar2=ppc("blconv", l * 4 + c),
                                                            op0=ALU.mult, op1=ALU.add), [ARB[xin], PPB], [ARB[cv]])
                    for tap in (1, 2, 3):
                        fw.op("dve", lambda h: h.scalar_tensor_tensor(out=a32(cv), in0=AR[:, xin, tap:tap + 512], scalar=wl(tap), in1=a32(cv),
                                                                       op0=ALU.mult, op1=ALU.add), [ARB[xin], PPB, ARB[cv]], [ARB[cv]])
                    fw.op("pool", lambda h: h.tensor_copy(out=XRH[:, c, :], in_=AR[:, xin, 512:515]), [ARB[xin]], [XRHB[c]])
                    fw.op("pool", lambda h: h.tensor_copy(out=cvb_ap, in_=a32(cv)), [ARB[cv]], [cvbB])

            for i2 in range(2):
                bxc = win_slot(2048 + i2 * 256)
                bcg = win_slot(1536 + i2 * 256)
                bbg = win_slot(1024 + i2 * 256)
                for mi in range(2):
                    c = i2 * 2 + mi
                    xc32 = tmp(0 + 3 * (c % 2))
                    acc = tmp(1 + 3 * (c % 2))
                    zz = tmp(2 + 3 * (c % 2))
                    fw.op("pool", lambda h: h.tensor_copy(out=AR[:, zz, 0:2], in_=ZH[:, c, :]), [ZHB[c]], [ARB[zz]])
                    b = bxc[mi]
                    fw.op("act", lambda h: h.copy(out=a32(xc32), in_=psb(b)), [PSB[b]], [ARB[xc32]])
                    b = bcg[mi]
                    fw.op("dve", lambda h: h.tensor_tensor(out=AR[:, zz, 2:514], in0=psb(b), in1=a32(xc32), op=ALU.mult), [PSB[b], ARB[xc32]], [ARB[zz]])
                    ws = lambda tap: ppc("wsconv", (l * 3 + tap) * 4 + c)
                    fw.op("dve", lambda h: h.tensor_scalar(out=a32(acc), in0=AR[:, zz, 0:512], scalar1=ws(0), scalar2=None, op0=ALU.mult), [ARB[zz], PPB], [ARB[acc]])
                    for tap in (1, 2):
                        fw.op("dve", lambda h: h.scalar_tensor_tensor(out=a32(acc), in0=AR[:, zz, tap:tap + 512], scalar=ws(tap), in1=a32(acc),
                                                                       op0=ALU.mult, op1=ALU.add), [ARB[zz], PPB, ARB[acc]], [ARB[acc]])
                    fw.op("pool", lambda h: h.tensor_copy(out=ZH[:, c, :], in_=AR[:, zz, 512:514]), [ARB[zz]], [ZHB[c]])
                    b = bbg[mi]
                    fw.op("dve", lambda h: h.tensor_tensor(out=o_ap(1, c), in0=psb(b), in1=a32(acc), op=ALU.mult), [PSB[b], ARB[acc]], [o_buf(1, c)])

            if ti == 0:
                cv_issue(upto_layer=0, count=26)
            d_ra = lambda c: 14 + c
            d_a2 = lambda c: 18 + c
            d_gg = lambda c: 22 + c
            d_ib = lambda c: (26, 27, 12, 13)[c]
            bgs = []
            for i2 in range(2):
                bgs += win_slot(4608 + i2 * 256, (4, 5, 6, 7))
            for c in range(4):
                gg = d_gg(c)
                b2 = bgs[c]
                fw.op("act", lambda h: h.activation(out=a32(gg), in_=psb(b2), func=AF.Gelu_apprx_tanh), [PSB[b2]], [ARB[gg]])
            brs, bis = [], []
            for c in range(4):
                cvb_ap, cvbB = d_cvb(c)
                br = bank((0, 1, 2, 3))
                mm(psb(br), WLA[:, c * 128:(c + 1) * 128], cvb_ap, True, True, [LWB, cvbB], [PSB[br]], True)
                brs.append(br)
            for c in range(4):
                cvb_ap, cvbB = d_cvb(c)
                bi = bank((4, 5, 6, 7))
                mm(psb(bi), WLX[:, c * 128:(c + 1) * 128], cvb_ap, True, True, [LWB, cvbB], [PSB[bi]], True)
                bis.append(bi)
            for c in range(4):
                ra = d_ra(c)
                fw.op("act", lambda h: h.activation(out=a32(ra), in_=psb(brs[c]), func=AF.Sigmoid, bias=ppc("bla", l * 4 + c)), [PSB[brs[c]], PPB], [ARB[ra]])
            for c in range(4):
                ibt = d_ib(c)
                fw.op("act", lambda h: h.activation(out=a32(ibt), in_=psb(bis[c]), func=AF.Sigmoid, bias=ppc("blx", l * 4 + c)), [PSB[bis[c]], PPB], [ARB[ibt]])
            for c in range(4):
                ra, a2s = d_ra(c), d_a2(c)
                fw.op("act", lambda h: h.activation(out=a32(a2s), in_=a32(ra), func=AF.Exp, scale=SPt[:, 12 + c:13 + c]), [ARB[ra], SPB], [ARB[a2s]])
                fw.op("act", lambda h: h.activation(out=a32(ra), in_=a32(ra), func=AF.Exp, scale=SPt[:, 8 + c:9 + c]), [ARB[ra], SPB], [ARB[ra]])
                fw.op("dve", lambda h: h.tensor_tensor(out=a32(d_ib(c)), in0=a32(d_ib(c)), in1=a32(d_cv(c)), op=ALU.mult), [ARB[d_ib(c)], ARB[d_cv(c)]], [ARB[d_ib(c)]])
            for c in range(4):
                a2s, ibt = d_a2(c), d_ib(c)
                fw.op("act", lambda h: h.activation(out=a32(a2s), in_=a32(a2s), func=AF.Sqrt, scale=-1.0, bias=EPSC[:, 1:2]), [ARB[a2s], EPSB], [ARB[a2s]])
                fw.op("dve", lambda h: h.tensor_tensor(out=a32(ibt), in0=a32(ibt), in1=a32(a2s), op=ALU.mult), [ARB[ibt], ARB[a2s]], [ARB[ibt]])
            for c in range(4):
                ra, hh, ibt = d_ra(c), d_a2(c), d_ib(c)
                fw.op("dve", lambda h: h.tensor_tensor_scan(out=a32(hh), data0=a32(ra), data1=a32(ibt), initial=HC[:, c:c + 1],
                                                             op0=ALU.mult, op1=ALU.add), [ARB[ra], ARB[ibt], HCB[c]], [ARB[hh]])
                fw.op("pool", lambda h: h.tensor_copy(out=HC[:, c:c + 1], in_=a32(hh, 1, 511)), [ARB[hh]], [HCB[c]])
                fw.op("dve", lambda h: h.tensor_tensor(out=o_ap(3, c), in0=a32(d_gg(c)), in1=a32(hh), op=ALU.mult), [ARB[d_gg(c)], ARB[hh]], [o_buf(3, c)])

            if ti == 0:
                cv_issue(upto_layer=0)
            for i2 in range(2):
                bu = win_slot(0 + i2 * 256)
                for mi in range(2):
                    c = i2 * 2 + mi
                    b = bu[mi]
                    fw.op("act", lambda h: h.activation(out=a32(8 + c), in_=psb(b), func=AF.Gelu_apprx_tanh), [PSB[b]], [ARB[8 + c]])
            sv0, sv0B = load_w(wpieces(WIN[l, :, 512:768], KC), CVW["w_in"][l])
            sv1, sv1B = load_w(wpieces(WIN[l, :, 768:1024], KC), CVW["w_in"][l])
            def sgu_chain(tt):
                b = bank((4, 5, 6, 7))
                for hf, (sv, svB) in enumerate(((sv0, sv0B), (sv1, sv1B))):
                    for kc in range(KC):
                        mm(psb(b, 256, hf * 256), XN[:, kc, tt * 128:(tt + 1) * 128], sv[:, kc, :], kc == 0, kc == KC - 1,
                           xnr(kc, kc == KC - 1) + [svB], [PSB[b]], kc == KC - 1)
                vg = tmp(0 + 3 * (tt % 2))
                junk = tmp(1 + 3 * (tt % 2))
                vn = tmp(2 + 3 * (tt % 2))
                rsb = RSB[tt]
                rcol = SM[:, 232 + tt:233 + tt]
                fw.op("act", lambda h: h.activation(out=a32(vg), in_=psb(b), func=AF.Gelu_apprx_tanh), [PSB[b]], [ARB[vg]])
                fw.op("act", lambda h: h.activation(out=a32(junk), in_=a32(vg), func=AF.Square, accum_out=rcol), [ARB[vg]], [ARB[junk], rsb])
                fw.op("act", lambda h: h.activation(out=rcol, in_=rcol, func=AF.Sqrt, scale=1.0 / BW, bias=EPSC[:, 0:1]), [rsb, EPSB], [rsb])
                fw.op("dve", lambda h: h.reciprocal(out=rcol, in_=rcol), [rsb], [rsb])
                fw.op("dve", lambda h: h.scalar_tensor_tensor(out=abf(vn)[:, 0:512], in0=a32(vg), scalar=rcol, in1=GSG[:], op0=ALU.mult, op1=ALU.mult),
                      [ARB[vg], rsb, LRB], [ARB[vn]])
                return vn

            def sgu_mix(tt, vn):
                for g in range(4):
                    o = psb(g, 128, tt * 128)
                    mm(o, abf(vn)[:, g * 128:(g + 1) * 128], WSG[:, g * 128:(g + 1) * 128], True, False, [ARB[vn], LWB], [PSB[g]], False)
                    mm(o, ONEB[0:1, 0:128], BSG[0:1, g * 128:(g + 1) * 128], False, False, [ONESB, LRB], [PSB[g]], False)
                    mm(o, ONEB[0:1, 0:128], BSG[0:1, 512 + g * 128:512 + (g + 1) * 128], False, True, [ONESB, LRB, ARB[vn]], [PSB[g]], True)
            prev = None
            for tt in range(4):
                vn = sgu_chain(tt)
                if prev is not None:
                    sgu_mix(*prev)
                prev = (tt, vn)
            sgu_mix(*prev)
            for g in range(4):
                fw.op("dve", lambda h: h.tensor_tensor(out=o_ap(0, g), in0=psb(g), in1=a32(8 + g), op=ALU.mult), [PSB[g], ARB[8 + g]], [o_buf(0, g)])

            for which in range(2):
                for i2 in range(2):
                    bq = win_slot((2560 if which == 0 else 3072) + i2 * 256)
                    for mi in range(2):
                        hd = i2 * 2 + mi
                        b = bq[mi]
                        q32 = (12 + hd) if which == 0 else tmp(0 + 2 * (hd % 2))
                        t2 = tmp(1 + 2 * (hd % 2)) if which == 1 else tmp(4 + (hd % 2))
                        fw.op("act", lambda h: h.copy(out=a32(q32), in_=psb(b)), [PSB[b]], [ARB[q32]])
                        b2 = bank()
                        mm(psb(b2), CST[:, 0:128], a32(q32), True, True, [CSTB, ARB[q32]], [PSB[b2]], True)
                        fw.op("dve", lambda h: h.tensor_tensor(out=a32(t2), in0=psb(b2), in1=CS[:, 1, :], op=ALU.mult), [PSB[b2], CSB], [ARB[t2]])
                        fw.op("dve", lambda h: h.tensor_tensor(out=a32(q32), in0=a32(q32), in1=CS[:, 0, :], op=ALU.mult), [ARB[q32], CSB], [ARB[q32]])
                        fw.op("dve", lambda h: h.tensor_tensor(out=a32(q32), in0=a32(q32), in1=a32(t2), op=ALU.add), [ARB[q32], ARB[t2]], [ARB[q32]])
                        if which == 0:
                            qb = abf(16 + hd // 2)[:, (hd % 2) * 512:(hd % 2) * 512 + 512]
                            fw.op("pool", lambda h: h.tensor_copy(out=qb, in_=a32(q32)), [ARB[q32]], [ARB[16 + hd // 2]])
                        else:
                            fw.op("pool", lambda h: h.tensor_copy(out=KCt[:, hd, tok0:tok0 + T], in_=a32(q32)), [ARB[q32]], [KCB])
                            kmt = SM[:, 240 + 2 * hd:242 + 2 * hd]
                            fw.op("dve", lambda h: h.tensor_reduce(out=kmt, in_=a32(q32).rearrange("p (b k) -> p b k", b=2), axis=AX.X, op=ALU.add),
                                  [ARB[q32]], [MXB[hd]])
                            fw.op("dve", lambda h: h.tensor_scalar(out=KM[:, hd, 2 * j:2 * j + 2], in0=kmt, scalar1=1.0 / 256.0, scalar2=None, op0=ALU.mult),
                                  [MXB[hd]], [KMB])
            GM = SM[:, 0:32]
            TOP = SM[:, 32:64]
            SEL = SM[:, 64:96]
            g3 = lambda ap: ap.rearrange("p (h n) -> p h n", h=4)
            for qt in range(4):
                ob = (4 * j + qt) // 2
                if ob == 0:
                    continue
                qs = slice(qt * 128, (qt + 1) * 128)
                MBq = SM[:, 96 + qt * 32:96 + qt * 32 + 32]
                for hd in range(4):
                    mm(psb(7, 8, qt * 32 + hd * 8), a32(12 + hd)[:, qs], KM[:, hd, :], True, True, [ARB[12 + hd], KMB], [PSB[7]], hd == 3)
                pn = CST[:, 640 + ob * 8:640 + ob * 8 + 8].unsqueeze(1).to_broadcast([128, 4, 8])
                fw.op("dve", lambda h: h.tensor_tensor(out=g3(GM), in0=g3(psb(7, 32, qt * 32)), in1=pn, op=ALU.add), [PSB[7], CSTB], [GMB])
                for hd in range(4):
                    fw.op("dve", lambda h: h.max(out=TOP[:, hd * 8:(hd + 1) * 8], in_=GM[:, hd * 8:(hd + 1) * 8]), [GMB], [TOPB])
                fw.op("dve", lambda h: h.tensor_tensor(out=g3(SEL), in0=g3(GM), in1=g3(TOP)[:, :, 2:3].to_broadcast([128, 4, 8]), op=ALU.is_ge),
                      [GMB, TOPB], [SELB])
                fw.op("dve", lambda h: h.tensor_scalar(out=MBq, in0=SEL, scalar1=-1.0, scalar2=BIG, op0=ALU.add, op1=ALU.mult), [SELB], [MBQB[qt]])
                fw.op("dve", lambda h: h.tensor_tensor(out=g3(MBq), in0=g3(MBq), in1=pn, op=ALU.add), [MBQB[qt], CSTB], [MBQB[qt]])
            sv0, sv0B = load_w(wpieces(WIN[l, :, 3584:3840], KC), CVW["w_in"][l])
            sv1, sv1B = load_w(wpieces(WIN[l, :, 3840:4096], KC), CVW["w_in"][l])
            for tt in range(4):
                b = bank()
                for hf, (sv, svB) in enumerate(((sv0, sv0B), (sv1, sv1B))):
                    for kc in range(KC):
                        mm(psb(b, 256, hf * 256), XN[:, kc, tt * 128:(tt + 1) * 128], sv[:, kc, :], kc == 0, kc == KC - 1,
                           xnr(kc, kc == KC - 1) + [svB], [PSB[b]], kc == KC - 1)
                fw.op("act", lambda h: h.copy(out=VCt[:, j * 4 + tt, :], in_=psb(b)), [PSB[b]], [VCB])

            inter = j < 2
            PT = PS[:, 2048:3072].bitcast(BF16)
            ptB_all = [PSB[4]] if inter else [PSB[4], PSB[5]]
            PCH = 8 if inter else 16
            OBANK = 5 if inter else 6

            def make_qt(qt):
                QT = 4 * j + qt
                ob = QT // 2
                half = QT % 2
                W = (ob + 1) * 256
                qs = slice(qt * 128, (qt + 1) * 128)
                two = W <= 1024
                nb = (W + 511) // 512
                nkt = W // 128

                def geom(hd):
                    par = hd % 2
                    so = 0 if inter else ((par * 1024) if two else 0)
                    sbo = (par * 1032) if two else 0
                    sbanks = [PSB[(so + c * 512) // 512] for c in range(nb)]
                    pbB = [ARB[18 + par]] if two else [ARB[18], ARB[19]]
                    ptsB = [ARB[20 + par]] if two else [ARB[20], ARB[21]]
                    return par, so, sbo, sbanks, pbB, ptsB

                def st_A(hd):
                    par, so, sbo, sbanks, pbB, ptsB = geom(hd)
                    for c in range(nb):
                        n = min(512, W - c * 512)
                        mm(PS[:, so + c * 512:so + c * 512 + n], abf(16 + hd // 2)[:, (hd % 2) * 512 + qt * 128:(hd % 2) * 512 + (qt + 1) * 128],
                           KCt[:, hd, c * 512:c * 512 + n], True, True, [ARB[16 + hd // 2], KCB], [sbanks[c]], True)

                def st_B(hd):
                    par, so, sbo, sbanks, pbB, ptsB = geom(hd)
                    Sps = PS[:, so:so + W]
                    if ob > 0:
                        v3 = PS[:, so:so + ob * 256].rearrange("p (n k) -> p n k", k=256)
                        fw.op("dve", lambda h: h.scalar_tensor_tensor(out=v3, in0=v3, scalar=SCALE,
                                                                       in1=SM[:, 96 + qt * 32 + hd * 8:96 + qt * 32 + hd * 8 + ob].unsqueeze(2).to_broadcast([128, ob, 256]),
                                                                       op0=ALU.mult, op1=ALU.add), sbanks + [MBQB[qt]], sbanks)
                    vo = PS[:, so + ob * 256:so + W]
                    fw.op("dve", lambda h: h.scalar_tensor_tensor(out=vo, in0=vo, scalar=SCALE, in1=CST[:, 128 + half * 256:128 + (half + 1) * 256],
                                                                   op0=ALU.mult, op1=ALU.add), sbanks + [CSTB], sbanks)
                    mx = SM[:, 224 + par:225 + par]
                    fw.op("dve", lambda h: h.reduce_max(out=mx, in_=Sps, axis=AX.X), sbanks, [MXB[par]])
                    fw.op("dve", lambda h: h.tensor_scalar(out=mx, in0=mx, scalar1=-1.0, scalar2=None, op0=ALU.mult), [MXB[par]], [MXB[par]])
                    PBv = abf2(18)[:, sbo:sbo + W]
                    fw.op("act", lambda h: h.activation(out=PBv, in_=Sps, func=AF.Exp, bias=mx, scale=1.0), sbanks + [MXB[par]], pbB)

                def st_CD(hd, part):
                    par, so, sbo, sbanks, pbB, ptsB = geom(hd)
                    PBv = abf2(18)[:, sbo:sbo + W]
                    PTSv = abf2(20)[:, sbo:sbo + W]
                    k0 = part * PCH
                    k1 = min(nkt, k0 + PCH)
                    for kt in range(k0, k1):
                        fw.op("pe", lambda h: h.transpose(out=PT[:, (kt - k0) * 128:(kt - k0 + 1) * 128], in_=PBv[:, kt * 128:(kt + 1) * 128], identity=IDB[:]),
                              pbB + [IDBB], ptB_all, kt == k1 - 1)
                    fw.op("act", lambda h: h.copy(out=PTSv[:, k0 * 128:k1 * 128], in_=PT[:, 0:(k1 - k0) * 128]), ptB_all, ptsB)

                def st_E(hd):
                    par, so, sbo, sbanks, pbB, ptsB = geom(hd)
                    PTSv = abf2(20)[:, sbo:sbo + W]
                    ob_ = OBANK
                    oo = par * 256
                    for kt in range(nkt):
                        mm(psb(ob_, 128, oo), VCt[:, kt, hd * 128:(hd + 1) * 128], PTSv[:, kt * 128:(kt + 1) * 128], kt == 0, kt == nkt - 1,
                           [VCB] + ptsB, [PSB[ob_]], False)
                    for kt in range(nkt):
                        mm(psb(ob_, 128, oo + 128), ONEB[:], PTSv[:, kt * 128:(kt + 1) * 128], kt == 0, kt == nkt - 1,
                           [ONESB, VCB] + ptsB, [PSB[ob_]], kt == nkt - 1)
                    fw.op("dve", lambda h: h.reciprocal(out=RINV[:, par, :], in_=psb(ob_, 128, oo + 128)), [PSB[ob_]], [RINVB[par]])
                    fw.op("dve", lambda h: h.tensor_tensor(out=o_ap(2, hd)[:, qs], in0=psb(ob_, 128, oo), in1=RINV[:, par, :], op=ALU.mult),
                          [PSB[ob_], RINVB[par]], [o_buf(2, hd)])

                nparts = (nkt + PCH - 1) // PCH
                items = []

                def A(hd):
                    items.append(lambda: st_A(hd))

                def B(hd):
                    items.append(lambda: st_B(hd))

                def CD(hd):
                    for part in range(nparts):
                        items.append(lambda part=part: st_CD(hd, part))

                def E(hd):
                    items.append(lambda: st_E(hd))
                if two:
                    A(0)
                    B(0)
                    for hd in range(4):
                        if hd + 1 < 4:
                            A(hd + 1)
                            B(hd + 1)
                        CD(hd)
                        E(hd)
                else:
                    A(0)
                    B(0)
                    CD(0)
                    for hd in range(4):
                        if hd + 1 < 4:
                            A(hd + 1)
                            B(hd + 1)
                        E(hd)
                        if hd + 1 < 4:
                            CD(hd + 1)
                return items

            att_items = []
            for qt in range(4):
                att_items += make_qt(qt)

            def y_view(m):
                ysl = 8 + m // 2
                return abf(ysl)[:, (m % 2) * 512:(m % 2) * 512 + 512], ARB[ysl]

            def merge_pair(k, m, mi, gsl, gsB, bsl, bsB, bkc0, bgk, bbk, first, last, sigslot):
                for kc in range(KC):
                    mm(psb(bgk), gsl[:, kc, mi * 128:(mi + 1) * 128], XN[:, kc, :], kc == 0, kc == KC - 1, [gsB] + xnr(kc, kc == KC - 1), [PSB[bgk]], kc == KC - 1)
                for kc in range(4):
                    mm(psb(bbk), bsl[:, bkc0 + kc, mi * 128:(mi + 1) * 128], o_ap(k, kc), kc == 0, kc == 3, [bsB, o_buf(k, kc)], [PSB[bbk]], kc == 3)
                sig = sigslot
                yacc = tmp(4 + mi)
                yv, yB = y_view(m)
                fw.op("act", lambda h: h.activation(out=a32(sig), in_=psb(bgk), func=AF.Sigmoid, bias=ppc("bgate", (l * 4 + k) * 16 + m)),
                      [PSB[bgk], PPB], [ARB[sig]])
                if first:
                    fw.op("dve", lambda h: h.tensor_tensor(out=a32(yacc), in0=psb(bbk), in1=a32(sig), op=ALU.mult), [PSB[bbk], ARB[sig]], [ARB[yacc]])
                else:
                    fw.op("dve", lambda h: h.tensor_tensor(out=a32(sig), in0=psb(bbk), in1=a32(sig), op=ALU.mult), [PSB[bbk], ARB[sig]], [ARB[sig]])
                    if not last:
                        fw.op("dve", lambda h: h.tensor_tensor(out=a32(yacc), in0=a32(yacc), in1=a32(sig), op=ALU.add), [ARB[yacc], ARB[sig]], [ARB[yacc]])
                    else:
                        fw.op("dve", lambda h: h.tensor_tensor(out=yv, in0=a32(yacc), in1=a32(sig), op=ALU.add), [ARB[yacc], ARB[sig]], [yB])

            st1 = {"inter": inter}

            def m1_item(mg, ki, k):
                bsl, bsB = WBS2[mg % 2], WBSB2[mg % 2]
                if ki == 0:
                    cv_issue(upto_layer=l + 1, count=1)
                    for kk in (0, 1, 3):
                        fw.dma("sp", bsl[:, kk * 4:kk * 4 + 4, :], WBR[l, kk, :, mg * 256:(mg + 1) * 256].rearrange("(kc p) c -> p kc c", p=128),
                               CVW["w_branch"][l], bsB)
                gsl, gsB = load_w(wpieces(WG[l, k, :, mg * 256:(mg + 1) * 256], KC), CVW["w_gate"][l])
                for mi in range(2):
                    if st1["inter"]:
                        st1["flip"] = 1 - st1.get("flip", 0)
                        bgk, bbk = ((2, 3), (6, 7))[st1["flip"]]
                    else:
                        bgk, bbk = bank(), bank()
                    merge_pair(k, mg * 2 + mi, mi, gsl, gsB, bsl, bsB, k * 4, bgk, bbk, ki == 0, ki == 2, tmp((ki * 2 + mi) % 4))

            m1_items = [(mg, ki, k) for mg in range(8) for ki, k in enumerate((0, 1, 3))]
            RATIO = 3
            cnt = 0
            for it in att_items:
                it()
                cnt += 1
                if inter and cnt % RATIO == 0 and m1_items:
                    m1_item(*m1_items.pop(0))
            st1["inter"] = False
            while m1_items:
                m1_item(*m1_items.pop(0))

            if debug and (l, s, j) == (L - 1, 0, 0):
                for k in range(4):
                    for c in range(4):
                        fw.dma("pool", dbg_d[:, (k * 4 + c) * 512:(k * 4 + c + 1) * 512], o_ap(k, c), o_buf(k, c), Buf("dbg"), sembuf=o_buf(k, c))

            for mg in range(8):
                gsl, gsB = load_w(wpieces(WG[l, 2, :, mg * 256:(mg + 1) * 256], KC), CVW["w_gate"][l])
                bsl, bsB = load_w([(WBR[l, 2, :, mg * 256:(mg + 1) * 256].rearrange("(kc p) c -> p kc c", p=128), 0, 4)], CVW["w_branch"][l])
                for mi in range(2):
                    m = mg * 2 + mi
                    bgk, bbk = bank(), bank()
                    for kc in range(KC):
                        mm(psb(bgk), gsl[:, kc, mi * 128:(mi + 1) * 128], XN[:, kc, :], kc == 0, kc == KC - 1, [gsB] + xnr(kc, kc == KC - 1), [PSB[bgk]], kc == KC - 1)
                    for kc in range(4):
                        mm(psb(bbk), bsl[:, kc, mi * 128:(mi + 1) * 128], o_ap(2, kc), kc == 0, kc == 3, [bsB, o_buf(2, kc)], [PSB[bbk]], kc == 3)
                    sig = tmp((mg * 2 + mi) % 4)
                    yv, yB = y_view(m)
                    fw.op("act", lambda h: h.activation(out=a32(sig), in_=psb(bgk), func=AF.Sigmoid, bias=ppc("bgate", (l * 4 + 2) * 16 + m)),
                          [PSB[bgk], PPB], [ARB[sig]])
                    fw.op("dve", lambda h: h.tensor_tensor(out=a32(sig), in0=psb(bbk), in1=a32(sig), op=ALU.mult), [PSB[bbk], ARB[sig]], [ARB[sig]])
                    fw.op("dve", lambda h: h.tensor_tensor(out=yv, in0=yv, in1=a32(sig), op=ALU.add), [yB, ARB[sig]], [yB])

            def y_rhs(kc):
                return abf(8 + kc // 2)[:, (kc % 2) * 512:(kc % 2) * 512 + 512]
            ybufs = [ARB[8 + i] for i in range(8)]
            for mg in range(8):
                bs = dense_fm(WO[l, :, mg * 256:(mg + 1) * 256], KC, y_rhs, ybufs, CVW["w_out"][l])
                for mi in range(2):
                    m = mg * 2 + mi
                    b = bs[mi]
                    fw.op("dve", lambda h: h.tensor_tensor(out=X[:, xp(m), :], in0=psb(b), in1=X[:, xp(m), :], op=ALU.add), [PSB[b], XB[xp(m)]], [XB[xp(m)]])

            norm_to_xn("gffn", l)

            def h_ap(fc):
                return abf(fc // 2)[:, (fc % 2) * 512:(fc % 2) * 512 + 512]

            def h_buf(fc):
                return ARB[fc // 2]
            nxt = tiles[ti + 1] if ti + 1 < len(tiles) else None
            for hf in range(2):
                for fg in range(11):
                    if hf == 0 and fg in (0, 5):
                        cv_issue(upto_layer=l + 1, count=1)
                    col0 = (hf * 11 + fg) * 256
                    s1, s1B = load_w(wpieces(W1[l, :, col0:col0 + 256], KC), CVW["w_ffn1"][l])
                    s3, s3B = load_w(wpieces(W3[l, :, col0:col0 + 256], KC), CVW["w_ffn3"][l])
                    for mi in range(2):
                        fc = fg * 2 + mi
                        b1 = bank()
                        for kc in range(KC):
                            mm(psb(b1), s1[:, kc, mi * 128:(mi + 1) * 128], XN[:, kc, :], kc == 0, kc == KC - 1, [s1B] + xnr(kc, kc == KC - 1), [PSB[b1]], kc == KC - 1)
                        b3 = bank()
                        for kc in range(KC):
                            mm(psb(b3), s3[:, kc, mi * 128:(mi + 1) * 128], XN[:, kc, :], kc == 0, kc == KC - 1, [s3B] + xnr(kc, kc == KC - 1), [PSB[b3]], kc == KC - 1)
                        sl = tmp(fc % 4)
                        fw.op("act", lambda h: h.activation(out=a32(sl), in_=psb(b1), func=AF.Silu), [PSB[b1]], [ARB[sl]])
                        fw.op("dve", lambda h: h.tensor_tensor(out=h_ap(fc), in0=psb(b3), in1=a32(sl), op=ALU.mult), [PSB[b3], ARB[sl]], [h_buf(fc)])
                hbufs = [ARB[i] for i in range(11)]
                early = nxt is not None and (nxt[1], nxt[2]) != (s, j)
                if hf == 1 and early:
                    for kc2 in (KC - 2, KC - 1):
                        load_x_chunk(nxt[0], nxt[1], nxt[2], kc2, next_phys(kc2))
                for mg in range(8):
                    r0 = hf * 22 * 128
                    sa, saB = load_w(wpieces(W2[l, r0:r0 + 11 * 128, mg * 256:(mg + 1) * 256], 11), CVW["w_ffn2"][l])
                    sb_, sbB = load_w(wpieces(W2[l, r0 + 11 * 128:r0 + 22 * 128, mg * 256:(mg + 1) * 256], 11), CVW["w_ffn2"][l])
                    bks = [bank(), bank()]
                    for si, (sl_, slB) in enumerate(((sa, saB), (sb_, sbB))):
                        for mi in range(2):
                            for kc in range(11):
                                fc = si * 11 + kc
                                last = (si == 1 and kc == 10)
                                mm(psb(bks[mi]), sl_[:, kc, mi * 128:(mi + 1) * 128], h_ap(fc), si == 0 and kc == 0, last,
                                   [slB] + hbufs, [PSB[bks[mi]]] if (last or (si == 0 and kc == 0)) else [], last or kc == 10)
                    for mi in range(2):
                        m = mg * 2 + mi
                        b = bks[mi]
                        fw.op("dve", lambda h: h.tensor_tensor(out=X[:, xp(m), :], in0=psb(b), in1=X[:, xp(m), :], op=ALU.add), [PSB[b], XB[xp(m)]], [XB[xp(m)]])
                        if hf == 1 and l < L - 1:
                            store_x_chunk(l, s, j, m, xp(m))
                            if nxt is not None and (m < KC - 2 or not early):
                                load_x_chunk(nxt[0], nxt[1], nxt[2], m, next_phys(m))
            if l == L - 1:
                rs = tmp(2)
                rms_stats(rs)
                for kc in range(KC):
                    fw.op("dve", lambda h: h.scalar_tensor_tensor(out=X[:, xp(kc), :], in0=X[:, xp(kc), :], scalar=ppc("gfin", kc), in1=a32(rs),
                                                                    op0=ALU.mult, op1=ALU.mult), [XB[xp(kc)], PPB, ARB[rs]], [XB[xp(kc)]])
                    store_x_chunk(l, s, j, kc, xp(kc))
                    if nxt is not None and (kc < KC - 2 or not early):
                        load_x_chunk(nxt[0], nxt[1], nxt[2], kc, next_phys(kc))

        for kc in range(KC):
            load_x_chunk(0, 0, 0, kc, xp(kc))
        ti = 0
        for l in range(L):
            layer_prep(l)
            for s in range(NSEQ):
                seq_reset()
                for j in range(NT):
                    tile(ti)
                    rotate_x()
                    ti += 1
            cv_issue(upto_layer=l + 1)
        outb = [ORG[s][j][kc] for s in range(NSEQ) for j in range(NT) for kc in range(KC)]
        fw.finish("pool", outb)
        fw.finish("sp", outb)
        build.stats = (fw.nins, fw.nwait)
        build.sbuf_left = nc.sbuf_bytes_remaining
    return nc


def make_in_maps(inputs, L, nseq, ncores):
    rope, cst = host_consts()
    pp, mats, rows = host_params(inputs, L)
    x = np.asarray(inputs["x"], np.float32)
    common = {"pp": pp, "mats": mats, "rows": rows, "rope": rope, "cst": cst}
    for n in WNAMES:
        common[n] = np.ascontiguousarray(np.asarray(inputs[n], np.float32)[:L]).reshape(L, 128, -1)
    maps = []
    for c in range(ncores):
        m = dict(common)
        m["xT"] = np.ascontiguousarray(x[c * nseq:(c + 1) * nseq].transpose(0, 2, 1))
        maps.append(m)
    return maps


def kernel(**inputs):
    nc = build(DEPTH, SEQ_PER_CORE, S // T)
    maps = make_in_maps(inputs, DEPTH, SEQ_PER_CORE, NCORES)
    res = run_bass_kernel_spmd(nc, maps, core_ids=list(range(NCORES)))
    outs = [np.asarray(r["oT"], np.float32).transpose(0, 2, 1) for r in res.results]
    return np.ascontiguousarray(np.concatenate(outs, axis=0))
```

```python
import numpy as np
from contextlib import ExitStack
import concourse.bass as bass
import concourse.mybir as mybir
from concourse.bass_utils import run_bass_kernel_spmd

F32 = mybir.dt.float32
BF16 = mybir.dt.bfloat16
AF = mybir.ActivationFunctionType
ALU = mybir.AluOpType
AX = mybir.AxisListType

D = 2048
S = 2048
T = 512
KC = 16
INW = 5120
DFF = 5632
BW = 512
NEG = -1e30
BIG = 1e30
EPS = 1e-6
NCORES = 8
SEQ_PER_CORE = 2
DEPTH = 4
SCALE = 128 ** -0.5


class Buf:
    __slots__ = ("name", "w", "r", "dsem", "dcnt")

    def __init__(self, name):
        self.name = name
        self.w = None
        self.r = {}
        self.dsem = None
        self.dcnt = 0


class Eng:
    def __init__(self, name, h, sem):
        self.name = name
        self.h = h
        self.sem = sem
        self.cnt = 0
        self.seen = {}


class FW:
    def __init__(self, nc, stack):
        self.nc = nc
        self.stack = stack
        self.engs = {}
        for n in ("pe", "act", "dve", "pool", "sp"):
            h = {"pe": nc.tensor, "act": nc.scalar, "dve": nc.vector, "pool": nc.gpsimd, "sp": nc.sync}[n]
            sem = stack.enter_context(nc.semaphore("sem_" + n))
            self.engs[n] = Eng(n, h, sem)
        self.nwait = 0
        self.nins = 0

    def _waits(self, E, reads, writes):
        need = {}

        def add(tok):
            sem, val = tok
            k = id(sem)
            if k not in need or need[k][1] < val:
                need[k] = (sem, val)
        for b in reads:
            if b.w is not None:
                add(b.w)
        for b in writes:
            if b.w is not None and b.w[0] is not E.sem:
                add(b.w)
            for tok in b.r.values():
                if tok[0] is not E.sem:
                    add(tok)
        for k, (sem, val) in need.items():
            if E.seen.get(k, 0) >= val:
                continue
            E.h.wait_ge(sem, val)
            E.seen[k] = val
            self.nwait += 1

    def _record(self, tok, reads, writes):
        k = id(tok[0])
        for b in reads:
            b.r[k] = tok
        for b in writes:
            b.w = tok
            b.r = {}

    def op(self, eng, fn, reads=(), writes=(), mark=True):
        E = self.engs[eng]
        self._waits(E, reads, writes)
        ins = fn(E.h)
        self.nins += 1
        if mark:
            E.cnt += 1
            ins.then_inc(E.sem, 1)
            self._record((E.sem, E.cnt), reads, writes)
        return ins

    def dma(self, q, out_ap, in_ap, src, dst, sembuf=None):
        E = self.engs[q]
        sb = sembuf if sembuf is not None else dst
        if sb.dsem is None:
            sb.dsem = self.stack.enter_context(self.nc.semaphore("d_" + sb.name))
        same_fill = (dst.w is not None and dst.w[0] is sb.dsem and not dst.r)
        if same_fill:
            saved = dst.w
            dst.w = None
            self._waits(E, [src], [dst])
            dst.w = saved
        else:
            self._waits(E, [src], [dst])
        sb.dcnt += 16
        ins = E.h.dma_start(out=out_ap, in_=in_ap)
        ins.then_inc(sb.dsem, 16)
        self.nins += 1
        self._record((sb.dsem, sb.dcnt), [src], [dst])
        return ins

    def finish(self, eng, bufs):
        E = self.engs[eng]
        self._waits(E, list(bufs), list(bufs))


def pp_layout(L):
    off = {}
    n = 0
    for name, w in [("gmix", L * 16), ("gffn", L * 16), ("gfin", 16), ("bgate", L * 64), ("wsconv", L * 12),
                    ("wlconv", L * 16), ("blconv", L * 4), ("bla", L * 4), ("blx", L * 4), ("lam", L * 4)]:
        off[name] = n
        n += w
    return off, n


def _cols(a):
    return np.ascontiguousarray(np.asarray(a, np.float32).reshape(-1, 128).T)


def host_consts():
    half = 64
    inv = (10000.0 ** (-(np.arange(half, dtype=np.float32)) / np.float32(half))).astype(np.float32)
    ang = np.arange(S, dtype=np.float32)[None, :] * inv[:, None]
    cos = np.cos(ang).astype(np.float32)
    sin = np.sin(ang).astype(np.float32)
    rope = np.concatenate([np.concatenate([cos, cos], 0), np.concatenate([-sin, sin], 0)], axis=1)
    perm = np.zeros((128, 128), np.float32)
    for m in range(128):
        perm[(m + 64) % 128, m] = 1.0
    triu = (np.arange(128)[:, None] <= np.arange(128)[None, :]).astype(np.float32)
    q = np.arange(128)[:, None]
    key = np.arange(256)[None, :]
    causal = np.concatenate([np.where(key <= q, 0.0, NEG), np.where(key <= 128 + q, 0.0, NEG)], axis=1).astype(np.float32)
    pastneg = np.zeros((8, 8), np.float32)
    for ob in range(8):
        for n in range(8):
            pastneg[ob, n] = 0.0 if n < ob else NEG
    pastneg = np.broadcast_to(pastneg.reshape(1, 64), (128, 64)).astype(np.float32)
    ident = np.eye(128, dtype=np.float32)
    cst = np.concatenate([perm, causal, pastneg, triu, ident], axis=1)
    return np.ascontiguousarray(rope.astype(np.float32)), np.ascontiguousarray(cst)


def host_params(inp, L):
    off, npp = pp_layout(L)
    pp = np.zeros((128, npp), np.float32)

    def put(name, a):
        c = _cols(a)
        pp[:, off[name]:off[name] + c.shape[1]] = c
    put("gmix", inp["g_mix"][:L])
    put("gffn", inp["g_ffn"][:L])
    put("gfin", inp["g_final"])
    put("bgate", inp["b_gate"][:L])
    put("wsconv", inp["w_sconv"][:L])
    put("wlconv", inp["w_lru_conv"][:L])
    put("blconv", inp["b_lru_conv"][:L])
    put("bla", inp["b_lru_a"][:L])
    put("blx", inp["b_lru_x"][:L])
    put("lam", inp["lru_lambda"][:L])
    wla = np.asarray(inp["w_lru_a"][:L], np.float32).transpose(2, 0, 1, 3).reshape(128, -1)
    wlx = np.asarray(inp["w_lru_x"][:L], np.float32).transpose(2, 0, 1, 3).reshape(128, -1)
    wsg = np.asarray(inp["w_sgu"][:L], np.float32).transpose(3, 0, 1, 2).reshape(128, -1)
    mats = np.ascontiguousarray(np.concatenate([wla, wlx, wsg], axis=1))
    bsgu = np.asarray(inp["b_sgu"][:L], np.float32).transpose(0, 2, 1).reshape(1, -1)
    gsgu = np.asarray(inp["g_sgu"][:L], np.float32).reshape(1, -1)
    rows = np.ascontiguousarray(np.concatenate([bsgu, gsgu], axis=1))
    return pp, mats, rows


WNAMES = ["w_in", "w_gate", "w_branch", "w_out", "w_ffn1", "w_ffn3", "w_ffn2"]


def wshape(name, L):
    return {"w_in": [L, D, INW], "w_gate": [L, 4, D, D], "w_branch": [L, 4, BW, D], "w_out": [L, D, D],
            "w_ffn1": [L, D, DFF], "w_ffn3": [L, D, DFF], "w_ffn2": [L, DFF, D]}[name]


def build(L=DEPTH, NSEQ=SEQ_PER_CORE, NT=S // T, debug=False):
    nc = bass.Bass("TRN2", target_bir_lowering=False)
    off, npp = pp_layout(L)
    CHK = 2048

    xT_d = nc.dram_tensor("xT", [NSEQ, D, S], F32, kind="ExternalInput").ap()
    oT_d = nc.dram_tensor("oT", [NSEQ, D, S], F32, kind="ExternalOutput").ap()
    pp_d = nc.dram_tensor("pp", [128, npp], F32, kind="ExternalInput").ap()
    mats_d = nc.dram_tensor("mats", [128, 3 * L * 512], F32, kind="ExternalInput").ap()
    rows_d = nc.dram_tensor("rows", [1, 2 * L * 512], F32, kind="ExternalInput").ap()
    rope_d = nc.dram_tensor("rope", [128, 2 * S], F32, kind="ExternalInput").ap()
    cst_d = nc.dram_tensor("cst", [128, 960], F32, kind="ExternalInput").ap()
    wsrc = {}
    wbf = {}
    for n in WNAMES:
        shp = wshape(n, L)
        E = int(np.prod(shp))
        wsrc[n] = nc.dram_tensor(n, [L, 128, E // L // 128], F32, kind="ExternalInput").ap()
        wbf[n] = nc.dram_tensor(n + "_bf", [E], BF16).ap()
    xscr = (nc.dram_tensor("xscr", [NSEQ, D, S], F32, kind="ExternalOutput") if debug else nc.dram_tensor("xscr", [NSEQ, D, S], F32)).ap()
    dbg_d = nc.dram_tensor("dbg", [128, 16 * 512], BF16, kind="ExternalOutput").ap() if debug else None

    def wview(n):
        shp = wshape(n, L)
        if len(shp) == 3:
            return wbf[n].rearrange("(l r c) -> l r c", l=shp[0], r=shp[1])
        return wbf[n].rearrange("(l k r c) -> l k r c", l=shp[0], k=shp[1], r=shp[2])

    with ExitStack() as st:
        fw = FW(nc, st)

        def sb(name, shape, dt=F32):
            return st.enter_context(nc.sbuf_tensor(name, shape, dt))

        X = sb("X", [128, KC + 2, T])
        XB = [Buf("X%d" % i) for i in range(KC + 2)]
        xst = {"map": list(range(KC)), "spare": [KC, KC + 1]}

        def xp(kc):
            return xst["map"][kc]
        XN = sb("XN", [128, KC, T], BF16)
        XNBs = [Buf("XN%d" % i) for i in range(KC)]

        def xnr(kc, last=False):
            return list(XNBs) if last else [XNBs[kc]]
        NSLOT = 4
        WS = [sb("WS%d" % i, [128, 16, 256], BF16) for i in range(NSLOT)]
        WSB = [Buf("WS%d" % i) for i in range(NSLOT)]
        WBS2 = [sb("WBS%d" % i, [128, 16, 256], BF16) for i in range(2)]
        WBSB2 = [Buf("WBS%d" % i) for i in range(2)]
        KCt = sb("KC", [128, 4, S], BF16)
        KCB = Buf("KC")
        VCt = sb("VC", [128, 16, 512], BF16)
        VCB = Buf("VC")
        PP = sb("PP", [128, npp])
        PPB = Buf("PP")
        CST = sb("CST", [128, 704])
        CSTB = Buf("CST")
        IDB = sb("IDB", [128, 128], BF16)
        IDBB = Buf("IDB")
        ONEB = sb("ONEB", [128, 128], BF16)
        ONE32 = sb("ONE32", [128, 128])
        ONESB = Buf("ONES")
        WLA = sb("WLA", [128, 512], BF16)
        WLX = sb("WLX", [128, 512], BF16)
        WSG = sb("WSG", [128, 512], BF16)
        LWB = Buf("LW")
        BSG = sb("BSG", [1, 1024], BF16)
        GSG = sb("GSG", [128, 512])
        LRB = Buf("LR")
        SPt = sb("SPt", [128, 16])
        SPB = Buf("SP")
        CS = sb("CS", [128, 2, T])
        CSB = Buf("CS")
        ZH = sb("ZH", [128, 4, 2])
        ZHB = [Buf("ZH%d" % i) for i in range(4)]
        XRH = sb("XRH", [128, 4, 3])
        XRHB = [Buf("XRH%d" % i) for i in range(4)]
        HC = sb("HC", [128, 4])
        HCB = [Buf("HC%d" % i) for i in range(4)]
        KM = sb("KM", [128, 4, 8])
        KMB = Buf("KM")
        SM = sb("SM", [128, 248])
        MBQB = [Buf("MBQ%d" % i) for i in range(4)]
        GMB, TOPB, SELB, MBB = Buf("GM"), Buf("TOP"), Buf("SEL"), Buf("MBm")
        MXB = [Buf("MX%d" % i) for i in range(4)]
        RSB = [Buf("RS%d" % i) for i in range(4)]
        RINV = sb("RINV", [128, 2, 128])
        RINVB = [Buf("RINV0"), Buf("RINV1")]

        NAR = 28
        AR = sb("AR", [128, NAR, 516])
        ARB = [Buf("AR%d" % i) for i in range(NAR)]

        def a32(i, n=512, o=0):
            return AR[:, i, o:o + n]

        def abf(i):
            return AR[:, i, :].bitcast(BF16)

        def abf2(i):
            return AR[:, i:i + 2, :].rearrange("p a b -> p (a b)").bitcast(BF16)

        TM0 = 22

        def tmp(k):
            return TM0 + k

        def o_ap(k, c):
            return abf(k * 2 + c // 2)[:, (c % 2) * 512:(c % 2) * 512 + 512]

        def o_buf(k, c):
            return ARB[k * 2 + c // 2]

        PS = st.enter_context(nc.psum_tensor("PS", [128, 4096], F32))
        PSB = [Buf("PS%d" % i) for i in range(8)]
        rot = {"i": 0}

        def bank(allowed=(0, 1, 2, 3, 4, 5, 6, 7)):
            while True:
                b = rot["i"] % 8
                rot["i"] += 1
                if b in allowed:
                    return b

        def psb(b, n=512, o=0):
            return PS[:, b * 512 + o:b * 512 + o + n]

        wrot = {"i": 0}
        DW = Buf("DW")

        def load_w(src_fn_list, dep):
            i = wrot["i"] % NSLOT
            wrot["i"] += 1
            for ap, kc0, n in src_fn_list:
                fw.dma("sp", WS[i][:, kc0:kc0 + n, :], ap, dep, WSB[i])
            return WS[i], WSB[i]

        def wpieces(ap2d, nkc):
            v = ap2d.rearrange("(kc p) c -> p kc c", p=128)
            out = []
            k = 0
            while k < nkc:
                n = min(8, nkc - k)
                out.append((v[:, k:k + n, :], k, n))
                k += n
            return out

        def mm(out, lhsT, rhs, start, stop, reads, writes, mark):
            return fw.op("pe", lambda h: h.matmul(out, lhsT=lhsT, rhs=rhs, start=start, stop=stop), reads, writes, mark)

        DIN = Buf("DIN")
        fw.dma("pool", PP[:], pp_d[:, :], DIN, PPB)
        fw.dma("pool", CST[:], cst_d[:, 0:704], DIN, CSTB)
        fw.dma("pool", a32(0, 256), cst_d[:, 704:960], DIN, ARB[0])
        fw.op("dve", lambda h: h.tensor_copy(out=IDB[:], in_=a32(0, 128, 128)), [ARB[0]], [IDBB])
        fw.op("dve", lambda h: h.memset(ONEB[:], 1.0), [], [ONESB])
        fw.op("dve", lambda h: h.memset(ONE32[:], 1.0), [], [ONESB])

        CVS = {n: Buf("cvs_" + n) for n in WNAMES}
        CVW = {n: [Buf("cvw_%s_%d" % (n, l)) for l in range(L)] for n in WNAMES}
        CVF = 8192
        cvjobs = []
        for l in range(L):
            for n in ["w_in", "w_branch", "w_gate", "w_out", "w_ffn1", "w_ffn3", "w_ffn2"]:
                El = int(np.prod(wshape(n, L))) // L
                per = El // 128
                assert per % CVF == 0
                for c in range(per // CVF):
                    cvjobs.append((l, n, c, El))
        cvpos = {"i": 0}

        def cv_issue(upto_layer=None, count=None):
            done = 0
            while cvpos["i"] < len(cvjobs):
                l, n, c, El = cvjobs[cvpos["i"]]
                if upto_layer is not None and l > upto_layer:
                    break
                if count is not None and done >= count:
                    break
                dst = wbf[n][l * El:(l + 1) * El].rearrange("(p f) -> p f", p=128)[:, c * CVF:(c + 1) * CVF]
                fw.dma("pool", dst, wsrc[n][l, :, c * CVF:(c + 1) * CVF], DIN, CVW[n][l], sembuf=CVS[n])
                cvpos["i"] += 1
                done += 1
        cv_issue(upto_layer=0, count=10)

        def ppc(name, idx):
            return PP[:, off[name] + idx:off[name] + idx + 1]

        def rms_stats(rs_slot):
            b = bank()
            for kc in range(KC):
                sq = tmp(kc % 2)
                sqv = abf(sq)[:, (kc // 2 % 2) * 512:(kc // 2 % 2) * 512 + 512]
                fw.op("act", lambda h: h.activation(out=sqv, in_=X[:, xp(kc), :], func=AF.Square), [XB[xp(kc)]], [ARB[sq]])
                mm(psb(b), ONEB[:], sqv, kc == 0, kc == KC - 1, [ONESB, ARB[sq]], [PSB[b]], True)
            fw.op("act", lambda h: h.activation(out=a32(rs_slot), in_=psb(b), func=AF.Sqrt, scale=1.0 / D, bias=EPSC[:, 0:1]),
                  [PSB[b], EPSB], [ARB[rs_slot]])
            fw.op("dve", lambda h: h.reciprocal(out=a32(rs_slot), in_=a32(rs_slot)), [ARB[rs_slot]], [ARB[rs_slot]])

        EPSC = sb("EPSC", [128, 2])
        EPSB = Buf("EPS")
        fw.op("dve", lambda h: h.memset(EPSC[:, 0:1], EPS), [], [EPSB])
        fw.op("dve", lambda h: h.memset(EPSC[:, 1:2], 1.0), [], [EPSB])

        def norm_to_xn(gname, l):
            rs = tmp(2)
            rms_stats(rs)
            for kc in range(KC):
                fw.op("dve", lambda h: h.scalar_tensor_tensor(out=XN[:, kc, :], in0=X[:, xp(kc), :], scalar=ppc(gname, l * 16 + kc),
                                                                in1=a32(rs), op0=ALU.mult, op1=ALU.mult),
                      [XB[xp(kc)], PPB, ARB[rs]], [XNBs[kc]])

        def dense_fm(src2d, nkc, rhs_fn, rhs_bufs, dep, allowed=(0, 1, 2, 3, 4, 5, 6, 7)):
            slot, sB = load_w(wpieces(src2d, nkc), dep)
            outs = []
            for mi in range(2):
                b = bank(allowed)
                for kc in range(nkc):
                    rb = rhs_bufs(kc, kc == nkc - 1) if callable(rhs_bufs) else rhs_bufs
                    mm(psb(b), slot[:, kc, mi * 128:(mi + 1) * 128], rhs_fn(kc), kc == 0, kc == nkc - 1,
                       [sB] + rb, [PSB[b]], kc == nkc - 1)
                outs.append(b)
            return outs

        def xn_rhs(kc):
            return XN[:, kc, :]

        XRG = [[[Buf("XRG%d_%d_%d" % (s, j, kc)) for kc in range(KC)] for j in range(NT)] for s in range(NSEQ)]
        ORG = [[[Buf("ORG%d_%d_%d" % (s, j, kc)) for kc in range(KC)] for j in range(NT)] for s in range(NSEQ)]
        tiles = [(l, s, j) for l in range(L) for s in range(NSEQ) for j in range(NT)]

        def load_x_chunk(l, s, j, kc, ph):
            if l == 0:
                fw.dma("act", X[:, ph, :], xT_d[s, kc * 128:(kc + 1) * 128, j * T:(j + 1) * T], DIN, XB[ph])
            else:
                fw.dma("act", X[:, ph, :], xscr[s, kc * 128:(kc + 1) * 128, j * T:(j + 1) * T], XRG[s][j][kc], XB[ph])

        def store_x_chunk(l, s, j, kc, ph):
            if l == L - 1:
                fw.dma("act", oT_d[s, kc * 128:(kc + 1) * 128, j * T:(j + 1) * T], X[:, ph, :], XB[ph], ORG[s][j][kc], sembuf=XB[ph])
            else:
                fw.dma("act", xscr[s, kc * 128:(kc + 1) * 128, j * T:(j + 1) * T], X[:, ph, :], XB[ph], XRG[s][j][kc], sembuf=XB[ph])

        def next_phys(kc):
            return xst["map"][kc] if kc < KC - 2 else xst["spare"][kc - (KC - 2)]

        def rotate_x():
            m = xst["map"]
            sp = xst["spare"]
            xst["map"] = m[:KC - 2] + sp
            xst["spare"] = m[KC - 2:]

        def layer_prep(l):
            s0 = tmp(3)
            for idx, dstt in enumerate((WLA, WLX)):
                fw.dma("pool", a32(s0), mats_d[:, (idx * L + l) * 512:(idx * L + l + 1) * 512], DIN, ARB[s0])
                fw.op("dve", lambda h: h.tensor_copy(out=dstt[:], in_=a32(s0)), [ARB[s0]], [LWB])
            fw.dma("pool", a32(s0), mats_d[:, (2 * L + l) * 512:(2 * L + l + 1) * 512], DIN, ARB[s0])
            st_ = tmp(5)
            fw.dma("pool", a32(st_, 128), cst_d[:, 704:832], DIN, ARB[st_])
            fw.op("dve", lambda h: h.tensor_tensor(out=WSG[:].rearrange("p (g t) -> p g t", g=4),
                                                    in0=a32(s0).rearrange("p (g t) -> p g t", g=4),
                                                    in1=a32(st_, 128).unsqueeze(1).to_broadcast([128, 4, 128]), op=ALU.mult),
                  [ARB[s0], ARB[st_]], [LWB])
            s1 = tmp(4)
            fw.dma("pool", AR[0:1, s1, 0:512], rows_d[0:1, l * 512:(l + 1) * 512], DIN, ARB[s1])
            fw.op("dve", lambda h: h.tensor_copy(out=BSG[0:1, 0:512], in_=AR[0:1, s1, 0:512]), [ARB[s1]], [LRB])
            s2 = tmp(5)
            fw.op("dve", lambda h: h.tensor_copy(out=AR[0:1, s2, 0:512], in_=BSG[0:1, 0:512]), [LRB], [ARB[s2]])
            fw.op("dve", lambda h: h.tensor_tensor(out=BSG[0:1, 512:1024], in0=AR[0:1, s1, 0:512], in1=AR[0:1, s2, 0:512], op=ALU.subtract),
                  [ARB[s1], ARB[s2]], [LRB])
            fw.dma("pool", GSG[:], rows_d[0:1, (L + l) * 512:(L + l + 1) * 512].partition_broadcast(128), DIN, LRB)
            lam = PP[:, off["lam"] + l * 4:off["lam"] + l * 4 + 4]
            z = SPt[:, 0:4]
            sp = SPt[:, 4:8]
            t8 = SPt[:, 8:12]
            t16 = SPt[:, 12:16]
            fw.op("act", lambda h: h.activation(out=z, in_=lam, func=AF.Exp, scale=-1.0), [PPB], [SPB])
            fw.op("dve", lambda h: h.tensor_scalar(out=sp, in0=z, scalar1=-0.2, scalar2=0.25, op0=ALU.mult, op1=ALU.add), [SPB], [SPB])
            for cst in (1.0 / 3.0, 0.5, 1.0):
                fw.op("dve", lambda h: h.tensor_tensor(out=sp, in0=sp, in1=z, op=ALU.mult), [SPB], [SPB])
                fw.op("dve", lambda h: h.tensor_scalar(out=sp, in0=sp, scalar1=-1.0, scalar2=cst, op0=ALU.mult, op1=ALU.add), [SPB], [SPB])
            fw.op("dve", lambda h: h.tensor_tensor(out=sp, in0=sp, in1=z, op=ALU.mult), [SPB], [SPB])
            fw.op("act", lambda h: h.activation(out=t8, in_=z, func=AF.Ln, bias=EPSC[:, 1:2], scale=1.0), [SPB, EPSB], [SPB])
            fw.op("dve", lambda h: h.tensor_scalar(out=t16, in0=z, scalar1=0.05, scalar2=None, op0=ALU.is_ge), [SPB], [SPB])
            fw.op("dve", lambda h: h.tensor_tensor(out=t8, in0=t8, in1=sp, op=ALU.subtract), [SPB], [SPB])
            fw.op("dve", lambda h: h.tensor_tensor(out=t8, in0=t8, in1=t16, op=ALU.mult), [SPB], [SPB])
            fw.op("dve", lambda h: h.tensor_tensor(out=sp, in0=sp, in1=t8, op=ALU.add), [SPB], [SPB])
            fw.op("dve", lambda h: h.tensor_scalar(out=t8, in0=sp, scalar1=-8.0, scalar2=None, op0=ALU.mult), [SPB], [SPB])
            fw.op("dve", lambda h: h.tensor_scalar(out=t16, in0=sp, scalar1=-16.0, scalar2=None, op0=ALU.mult), [SPB], [SPB])

        def seq_reset():
            for c in range(4):
                fw.op("pool", lambda h: h.memset(ZH[:, c, :], 0.0), [], [ZHB[c]])
                fw.op("pool", lambda h: h.memset(XRH[:, c, :], 0.0), [], [XRHB[c]])
                fw.op("pool", lambda h: h.memset(HC[:, c:c + 1], 0.0), [], [HCB[c]])
            fw.op("pool", lambda h: h.memset(KM[:], 0.0), [], [KMB])

        def tile(ti):
            l, s, j = tiles[ti]
            WIN = wview("w_in")
            WG = wview("w_gate")
            WBR = wview("w_branch")
            WO = wview("w_out")
            W1 = wview("w_ffn1")
            W3 = wview("w_ffn3")
            W2 = wview("w_ffn2")
            tok0 = j * T

            fw.dma("pool", CS[:, 0, :], rope_d[:, tok0:tok0 + T], DIN, CSB)
            fw.dma("pool", CS[:, 1, :], rope_d[:, S + tok0:S + tok0 + T], DIN, CSB)

            norm_to_xn("gmix", l)
            if ti == 0:
                cv_issue(upto_layer=0, count=20)

            def win_slot(col0, allowed=(0, 1, 2, 3, 4, 5, 6, 7)):
                return dense_fm(WIN[l, :, col0:col0 + 256], KC, xn_rhs, xnr, CVW["w_in"][l], allowed)

            def d_cv(c):
                return 8 + c

            def d_cvb(c):
                return abf(12 + c // 2)[:, (c % 2) * 512:(c % 2) * 512 + 512], ARB[12 + c // 2]
            for i2 in range(2):
                bx = win_slot(4096 + i2 * 256)
                for mi in range(2):
                    c = i2 * 2 + mi
                    cv = d_cv(c)
                    cvb_ap, cvbB = d_cvb(c)
                    xin = 14 + (c % 2)
                    b = bx[mi]
                    fw.op("pool", lambda h: h.tensor_copy(out=AR[:, xin, 0:3], in_=XRH[:, c, :]), [XRHB[c]], [ARB[xin]])
                    fw.op("act", lambda h: h.copy(out=AR[:, xin, 3:515], in_=psb(b)), [PSB[b]], [ARB[xin]])
                    wl = lambda tap: ppc("wlconv", (l * 4 + tap) * 4 + c)
                    fw.op("dve", lambda h: h.tensor_scalar(out=a32(cv), in0=AR[:, xin, 0:512], scalar1=wl(0), scalar2=ppc("blconv", l * 4 + c),
                                                            op0=ALU.mult, op1=ALU.add), [ARB[xin], PPB], [ARB[cv]])
                    for tap in (1, 2, 3):
                        fw.op("dve", lambda h: h.scalar_tensor_tensor(out=a32(cv), in0=AR[:, xin, tap:tap + 512], scalar=wl(tap), in1=a32(cv),
                                                                       op0=ALU.mult, op1=ALU.add), [ARB[xin], PPB, ARB[cv]], [ARB[cv]])
                    fw.op("pool", lambda h: h.tensor_copy(out=XRH[:, c, :], in_=AR[:, xin, 512:515]), [ARB[xin]], [XRHB[c]])
                    fw.op("pool", lambda h: h.tensor_copy(out=cvb_ap, in_=a32(cv)), [ARB[cv]], [cvbB])

            for i2 in range(2):
                bxc = win_slot(2048 + i2 * 256)
                bcg = win_slot(1536 + i2 * 256)
                bbg = win_slot(1024 + i2 * 256)
                for mi in range(2):
                    c = i2 * 2 + mi
                    xc32 = tmp(0 + 3 * (c % 2))
                    acc = tmp(1 + 3 * (c % 2))
                    zz = tmp(2 + 3 * (c % 2))
                    fw.op("pool", lambda h: h.tensor_copy(out=AR[:, zz, 0:2], in_=ZH[:, c, :]), [ZHB[c]], [ARB[zz]])
                    b = bxc[mi]
                    fw.op("act", lambda h: h.copy(out=a32(xc32), in_=psb(b)), [PSB[b]], [ARB[xc32]])
                    b = bcg[mi]
                    fw.op("dve", lambda h: h.tensor_tensor(out=AR[:, zz, 2:514], in0=psb(b), in1=a32(xc32), op=ALU.mult), [PSB[b], ARB[xc32]], [ARB[zz]])
                    ws = lambda tap: ppc("wsconv", (l * 3 + tap) * 4 + c)
                    fw.op("dve", lambda h: h.tensor_scalar(out=a32(acc), in0=AR[:, zz, 0:512], scalar1=ws(0), scalar2=None, op0=ALU.mult), [ARB[zz], PPB], [ARB[acc]])
                    for tap in (1, 2):
                        fw.op("dve", lambda h: h.scalar_tensor_tensor(out=a32(acc), in0=AR[:, zz, tap:tap + 512], scalar=ws(tap), in1=a32(acc),
                                                                       op0=ALU.mult, op1=ALU.add), [ARB[zz], PPB, ARB[acc]], [ARB[acc]])
                    fw.op("pool", lambda h: h.tensor_copy(out=ZH[:, c, :], in_=AR[:, zz, 512:514]), [ARB[zz]], [ZHB[c]])
                    b = bbg[mi]
                    fw.op("dve", lambda h: h.tensor_tensor(out=o_ap(1, c), in0=psb(b), in1=a32(acc), op=ALU.mult), [PSB[b], ARB[acc]], [o_buf(1, c)])

            if ti == 0:
                cv_issue(upto_layer=0, count=26)
            d_ra = lambda c: 14 + c
            d_a2 = lambda c: 18 + c
            d_gg = lambda c: 22 + c
            d_ib = lambda c: (26, 27, 12, 13)[c]
            bgs = []
            for i2 in range(2):
                bgs += win_slot(4608 + i2 * 256, (4, 5, 6, 7))
            for c in range(4):
                gg = d_gg(c)
                b2 = bgs[c]
                fw.op("act", lambda h: h.activation(out=a32(gg), in_=psb(b2), func=AF.Gelu_apprx_tanh), [PSB[b2]], [ARB[gg]])
            brs, bis = [], []
            for c in range(4):
                cvb_ap, cvbB = d_cvb(c)
                br = bank((0, 1, 2, 3))
                mm(psb(br), WLA[:, c * 128:(c + 1) * 128], cvb_ap, True, True, [LWB, cvbB], [PSB[br]], True)
                brs.append(br)
            for c in range(4):
                cvb_ap, cvbB = d_cvb(c)
                bi = bank((4, 5, 6, 7))
                mm(psb(bi), WLX[:, c * 128:(c + 1) * 128], cvb_ap, True, True, [LWB, cvbB], [PSB[bi]], True)
                bis.append(bi)
            for c in range(4):
                ra = d_ra(c)
                fw.op("act", lambda h: h.activation(out=a32(ra), in_=psb(brs[c]), func=AF.Sigmoid, bias=ppc("bla", l * 4 + c)), [PSB[brs[c]], PPB], [ARB[ra]])
            for c in range(4):
                ibt = d_ib(c)
                fw.op("act", lambda h: h.activation(out=a32(ibt), in_=psb(bis[c]), func=AF.Sigmoid, bias=ppc("blx", l * 4 + c)), [PSB[bis[c]], PPB], [ARB[ibt]])
            for c in range(4):
                ra, a2s = d_ra(c), d_a2(c)
                fw.op("act", lambda h: h.activation(out=a32(a2s), in_=a32(ra), func=AF.Exp, scale=SPt[:, 12 + c:13 + c]), [ARB[ra], SPB], [ARB[a2s]])
                fw.op("act", lambda h: h.activation(out=a32(ra), in_=a32(ra), func=AF.Exp, scale=SPt[:, 8 + c:9 + c]), [ARB[ra], SPB], [ARB[ra]])
                fw.op("dve", lambda h: h.tensor_tensor(out=a32(d_ib(c)), in0=a32(d_ib(c)), in1=a32(d_cv(c)), op=ALU.mult), [ARB[d_ib(c)], ARB[d_cv(c)]], [ARB[d_ib(c)]])
            for c in range(4):
                a2s, ibt = d_a2(c), d_ib(c)
                fw.op("act", lambda h: h.activation(out=a32(a2s), in_=a32(a2s), func=AF.Sqrt, scale=-1.0, bias=EPSC[:, 1:2]), [ARB[a2s], EPSB], [ARB[a2s]])
                fw.op("dve", lambda h: h.tensor_tensor(out=a32(ibt), in0=a32(ibt), in1=a32(a2s), op=ALU.mult), [ARB[ibt], ARB[a2s]], [ARB[ibt]])
            for c in range(4):
                ra, hh, ibt = d_ra(c), d_a2(c), d_ib(c)
                fw.op("dve", lambda h: h.tensor_tensor_scan(out=a32(hh), data0=a32(ra), data1=a32(ibt), initial=HC[:, c:c + 1],
                                                             op0=ALU.mult, op1=ALU.add), [ARB[ra], ARB[ibt], HCB[c]], [ARB[hh]])
                fw.op("pool", lambda h: h.tensor_copy(out=HC[:, c:c + 1], in_=a32(hh, 1, 511)), [ARB[hh]], [HCB[c]])
                fw.op("dve", lambda h: h.tensor_tensor(out=o_ap(3, c), in0=a32(d_gg(c)), in1=a32(hh), op=ALU.mult), [ARB[d_gg(c)], ARB[hh]], [o_buf(3, c)])

            if ti == 0:
                cv_issue(upto_layer=0)
            for i2 in range(2):
                bu = win_slot(0 + i2 * 256)
                for mi in range(2):
                    c = i2 * 2 + mi
                    b = bu[mi]
                    fw.op("act", lambda h: h.activation(out=a32(8 + c), in_=psb(b), func=AF.Gelu_apprx_tanh), [PSB[b]], [ARB[8 + c]])
            sv0, sv0B = load_w(wpieces(WIN[l, :, 512:768], KC), CVW["w_in"][l])
            sv1, sv1B = load_w(wpieces(WIN[l, :, 768:1024], KC), CVW["w_in"][l])
            def sgu_chain(tt):
                b = bank((4, 5, 6, 7))
                for hf, (sv, svB) in enumerate(((sv0, sv0B), (sv1, sv1B))):
                    for kc in range(KC):
                        mm(psb(b, 256, hf * 256), XN[:, kc, tt * 128:(tt + 1) * 128], sv[:, kc, :], kc == 0, kc == KC - 1,
                           xnr(kc, kc == KC - 1) + [svB], [PSB[b]], kc == KC - 1)
                vg = tmp(0 + 3 * (tt % 2))
                junk = tmp(1 + 3 * (tt % 2))
                vn = tmp(2 + 3 * (tt % 2))
                rsb = RSB[tt]
                rcol = SM[:, 232 + tt:233 + tt]
                fw.op("act", lambda h: h.activation(out=a32(vg), in_=psb(b), func=AF.Gelu_apprx_tanh), [PSB[b]], [ARB[vg]])
                fw.op("act", lambda h: h.activation(out=a32(junk), in_=a32(vg), func=AF.Square, accum_out=rcol), [ARB[vg]], [ARB[junk], rsb])
                fw.op("act", lambda h: h.activation(out=rcol, in_=rcol, func=AF.Sqrt, scale=1.0 / BW, bias=EPSC[:, 0:1]), [rsb, EPSB], [rsb])
                fw.op("dve", lambda h: h.reciprocal(out=rcol, in_=rcol), [rsb], [rsb])
                fw.op("dve", lambda h: h.scalar_tensor_tensor(out=abf(vn)[:, 0:512], in0=a32(vg), scalar=rcol, in1=GSG[:], op0=ALU.mult, op1=ALU.mult),
                      [ARB[vg], rsb, LRB], [ARB[vn]])
                return vn

            def sgu_mix(tt, vn):
                for g in range(4):
                    o = psb(g, 128, tt * 128)
                    mm(o, abf(vn)[:, g * 128:(g + 1) * 128], WSG[:, g * 128:(g + 1) * 128], True, False, [ARB[vn], LWB], [PSB[g]], False)
                    mm(o, ONEB[0:1, 0:128], BSG[0:1, g * 128:(g + 1) * 128], False, False, [ONESB, LRB], [PSB[g]], False)
                    mm(o, ONEB[0:1, 0:128], BSG[0:1, 512 + g * 128:512 + (g + 1) * 128], False, True, [ONESB, LRB, ARB[vn]], [PSB[g]], True)
            prev = None
            for tt in range(4):
                vn = sgu_chain(tt)
                if prev is not None:
                    sgu_mix(*prev)
                prev = (tt, vn)
            sgu_mix(*prev)
            for g in range(4):
                fw.op("dve", lambda h: h.tensor_tensor(out=o_ap(0, g), in0=psb(g), in1=a32(8 + g), op=ALU.mult), [PSB[g], ARB[8 + g]], [o_buf(0, g)])

            for which in range(2):
                for i2 in range(2):
                    bq = win_slot((2560 if which == 0 else 3072) + i2 * 256)
                    for mi in range(2):
                        hd = i2 * 2 + mi
                        b = bq[mi]
                        q32 = (12 + hd) if which == 0 else tmp(0 + 2 * (hd % 2))
                        t2 = tmp(1 + 2 * (hd % 2)) if which == 1 else tmp(4 + (hd % 2))
                        fw.op("act", lambda h: h.copy(out=a32(q32), in_=psb(b)), [PSB[b]], [ARB[q32]])
                        b2 = bank()
                        mm(psb(b2), CST[:, 0:128], a32(q32), True, True, [CSTB, ARB[q32]], [PSB[b2]], True)
                        fw.op("dve", lambda h: h.tensor_tensor(out=a32(t2), in0=psb(b2), in1=CS[:, 1, :], op=ALU.mult), [PSB[b2], CSB], [ARB[t2]])
                        fw.op("dve", lambda h: h.tensor_tensor(out=a32(q32), in0=a32(q32), in1=CS[:, 0, :], op=ALU.mult), [ARB[q32], CSB], [ARB[q32]])
                        fw.op("dve", lambda h: h.tensor_tensor(out=a32(q32), in0=a32(q32), in1=a32(t2), op=ALU.add), [ARB[q32], ARB[t2]], [ARB[q32]])
                        if which == 0:
                            qb = abf(16 + hd // 2)[:, (hd % 2) * 512:(hd % 2) * 512 + 512]
                            fw.op("pool", lambda h: h.tensor_copy(out=qb, in_=a32(q32)), [ARB[q32]], [ARB[16 + hd // 2]])
                        else:
                            fw.op("pool", lambda h: h.tensor_copy(out=KCt[:, hd, tok0:tok0 + T], in_=a32(q32)), [ARB[q32]], [KCB])
                            kmt = SM[:, 240 + 2 * hd:242 + 2 * hd]
                            fw.op("dve", lambda h: h.tensor_reduce(out=kmt, in_=a32(q32).rearrange("p (b k) -> p b k", b=2), axis=AX.X, op=ALU.add),
                                  [ARB[q32]], [MXB[hd]])
                            fw.op("dve", lambda h: h.tensor_scalar(out=KM[:, hd, 2 * j:2 * j + 2], in0=kmt, scalar1=1.0 / 256.0, scalar2=None, op0=ALU.mult),
                                  [MXB[hd]], [KMB])
            GM = SM[:, 0:32]
            TOP = SM[:, 32:64]
            SEL = SM[:, 64:96]
            g3 = lambda ap: ap.rearrange("p (h n) -> p h n", h=4)
            for qt in range(4):
                ob = (4 * j + qt) // 2
                if ob == 0:
                    continue
                qs = slice(qt * 128, (qt + 1) * 128)
                MBq = SM[:, 96 + qt * 32:96 + qt * 32 + 32]
                for hd in range(4):
                    mm(psb(7, 8, qt * 32 + hd * 8), a32(12 + hd)[:, qs], KM[:, hd, :], True, True, [ARB[12 + hd], KMB], [PSB[7]], hd == 3)
                pn = CST[:, 640 + ob * 8:640 + ob * 8 + 8].unsqueeze(1).to_broadcast([128, 4, 8])
                fw.op("dve", lambda h: h.tensor_tensor(out=g3(GM), in0=g3(psb(7, 32, qt * 32)), in1=pn, op=ALU.add), [PSB[7], CSTB], [GMB])
                for hd in range(4):
                    fw.op("dve", lambda h: h.max(out=TOP[:, hd * 8:(hd + 1) * 8], in_=GM[:, hd * 8:(hd + 1) * 8]), [GMB], [TOPB])
                fw.op("dve", lambda h: h.tensor_tensor(out=g3(SEL), in0=g3(GM), in1=g3(TOP)[:, :, 2:3].to_broadcast([128, 4, 8]), op=ALU.is_ge),
                      [GMB, TOPB], [SELB])
                fw.op("dve", lambda h: h.tensor_scalar(out=MBq, in0=SEL, scalar1=-1.0, scalar2=BIG, op0=ALU.add, op1=ALU.mult), [SELB], [MBQB[qt]])
                fw.op("dve", lambda h: h.tensor_tensor(out=g3(MBq), in0=g3(MBq), in1=pn, op=ALU.add), [MBQB[qt], CSTB], [MBQB[qt]])
            sv0, sv0B = load_w(wpieces(WIN[l, :, 3584:3840], KC), CVW["w_in"][l])
            sv1, sv1B = load_w(wpieces(WIN[l, :, 3840:4096], KC), CVW["w_in"][l])
            for tt in range(4):
                b = bank()
                for hf, (sv, svB) in enumerate(((sv0, sv0B), (sv1, sv1B))):
                    for kc in range(KC):
                        mm(psb(b, 256, hf * 256), XN[:, kc, tt * 128:(tt + 1) * 128], sv[:, kc, :], kc == 0, kc == KC - 1,
                           xnr(kc, kc == KC - 1) + [svB], [PSB[b]], kc == KC - 1)
                fw.op("act", lambda h: h.copy(out=VCt[:, j * 4 + tt, :], in_=psb(b)), [PSB[b]], [VCB])

            PT = PS[:, 2048:3072].bitcast(BF16)
            ai = 0
            for qt in range(4):
                QT = 4 * j + qt
                ob = QT // 2
                half = QT % 2
                W = (ob + 1) * 256
                qs = slice(qt * 128, (qt + 1) * 128)
                two = W <= 1024
                nb = (W + 511) // 512
                nkt = W // 128
                ctx = {}

                def geom(hd):
                    par = hd % 2
                    so = (par * 1024) if two else 0
                    sbo = (par * 1032) if two else 0
                    sbanks = [PSB[(so + c * 512) // 512] for c in range(nb)]
                    pbB = [ARB[18 + par]] if two else [ARB[18], ARB[19]]
                    ptB = [PSB[4 + par]] if two else [PSB[4], PSB[5]]
                    ptsB = [ARB[20 + par]] if two else [ARB[20], ARB[21]]
                    return par, so, sbo, sbanks, pbB, ptB, ptsB

                def st_A(hd):
                    par, so, sbo, sbanks, pbB, ptB, ptsB = geom(hd)
                    for c in range(nb):
                        n = min(512, W - c * 512)
                        mm(PS[:, so + c * 512:so + c * 512 + n], abf(16 + hd // 2)[:, (hd % 2) * 512 + qt * 128:(hd % 2) * 512 + (qt + 1) * 128],
                           KCt[:, hd, c * 512:c * 512 + n], True, True, [ARB[16 + hd // 2], KCB], [sbanks[c]], True)

                def st_B(hd):
                    par, so, sbo, sbanks, pbB, ptB, ptsB = geom(hd)
                    Sps = PS[:, so:so + W]
                    if ob > 0:
                        v3 = PS[:, so:so + ob * 256].rearrange("p (n k) -> p n k", k=256)
                        fw.op("dve", lambda h: h.scalar_tensor_tensor(out=v3, in0=v3, scalar=SCALE,
                                                                       in1=SM[:, 96 + qt * 32 + hd * 8:96 + qt * 32 + hd * 8 + ob].unsqueeze(2).to_broadcast([128, ob, 256]),
                                                                       op0=ALU.mult, op1=ALU.add), sbanks + [MBQB[qt]], sbanks)
                    vo = PS[:, so + ob * 256:so + W]
                    fw.op("dve", lambda h: h.scalar_tensor_tensor(out=vo, in0=vo, scalar=SCALE, in1=CST[:, 128 + half * 256:128 + (half + 1) * 256],
                                                                   op0=ALU.mult, op1=ALU.add), sbanks + [CSTB], sbanks)
                    mx = SM[:, 224 + par:225 + par]
                    fw.op("dve", lambda h: h.reduce_max(out=mx, in_=Sps, axis=AX.X), sbanks, [MXB[par]])
                    fw.op("dve", lambda h: h.tensor_scalar(out=mx, in0=mx, scalar1=-1.0, scalar2=None, op0=ALU.mult), [MXB[par]], [MXB[par]])
                    PBv = abf2(18)[:, sbo:sbo + W]
                    fw.op("act", lambda h: h.activation(out=PBv, in_=Sps, func=AF.Exp, bias=mx, scale=1.0), sbanks + [MXB[par]], pbB)

                def st_C(hd):
                    par, so, sbo, sbanks, pbB, ptB, ptsB = geom(hd)
                    PBv = abf2(18)[:, sbo:sbo + W]
                    for kt in range(nkt):
                        fw.op("pe", lambda h: h.transpose(out=PT[:, so + kt * 128:so + (kt + 1) * 128], in_=PBv[:, kt * 128:(kt + 1) * 128], identity=IDB[:]),
                              pbB + [IDBB], ptB, kt == nkt - 1)

                def st_D(hd):
                    par, so, sbo, sbanks, pbB, ptB, ptsB = geom(hd)
                    PTSv = abf2(20)[:, sbo:sbo + W]
                    fw.op("act", lambda h: h.copy(out=PTSv, in_=PT[:, so:so + W]), ptB, ptsB)

                def st_E(hd):
                    par, so, sbo, sbanks, pbB, ptB, ptsB = geom(hd)
                    PTSv = abf2(20)[:, sbo:sbo + W]
                    ob_ = 6
                    oo = par * 256
                    for kt in range(nkt):
                        mm(psb(ob_, 128, oo), VCt[:, kt, hd * 128:(hd + 1) * 128], PTSv[:, kt * 128:(kt + 1) * 128], kt == 0, kt == nkt - 1,
                           [VCB] + ptsB, [PSB[ob_]], False)
                    for kt in range(nkt):
                        mm(psb(ob_, 128, oo + 128), ONEB[:], PTSv[:, kt * 128:(kt + 1) * 128], kt == 0, kt == nkt - 1,
                           [ONESB, VCB] + ptsB, [PSB[ob_]], kt == nkt - 1)
                    fw.op("dve", lambda h: h.reciprocal(out=RINV[:, par, :], in_=psb(ob_, 128, oo + 128)), [PSB[ob_]], [RINVB[par]])
                    fw.op("dve", lambda h: h.tensor_tensor(out=o_ap(2, hd)[:, qs], in0=psb(ob_, 128, oo), in1=RINV[:, par, :], op=ALU.mult),
                          [PSB[ob_], RINVB[par]], [o_buf(2, hd)])

                if two:
                    st_A(0)
                    st_B(0)
                    for hd in range(4):
                        if hd + 1 < 4:
                            st_A(hd + 1)
                            st_B(hd + 1)
                        st_C(hd)
                        st_D(hd)
                        st_E(hd)
                else:
                    st_A(0)
                    st_B(0)
                    st_C(0)
                    st_D(0)
                    for hd in range(4):
                        if hd + 1 < 4:
                            st_A(hd + 1)
                            st_B(hd + 1)
                        st_E(hd)
                        if hd + 1 < 4:
                            st_C(hd + 1)
                            st_D(hd + 1)

            if debug and (l, s, j) == (L - 1, 0, 0):
                for k in range(4):
                    for c in range(4):
                        fw.dma("pool", dbg_d[:, (k * 4 + c) * 512:(k * 4 + c + 1) * 512], o_ap(k, c), o_buf(k, c), Buf("dbg"), sembuf=o_buf(k, c))
            for mg in range(8):
                cv_issue(upto_layer=l + 1, count=1)
                bsl, bsB = WBS2[mg % 2], WBSB2[mg % 2]
                for k in range(4):
                    fw.dma("sp", bsl[:, k * 4:k * 4 + 4, :], WBR[l, k, :, mg * 256:(mg + 1) * 256].rearrange("(kc p) c -> p kc c", p=128), CVW["w_branch"][l], bsB)
                for k in range(4):
                    gsl, gsB = load_w(wpieces(WG[l, k, :, mg * 256:(mg + 1) * 256], KC), CVW["w_gate"][l])
                    for mi in range(2):
                        m = mg * 2 + mi
                        bgk = bank()
                        for kc in range(KC):
                            mm(psb(bgk), gsl[:, kc, mi * 128:(mi + 1) * 128], XN[:, kc, :], kc == 0, kc == KC - 1, [gsB] + xnr(kc, kc == KC - 1), [PSB[bgk]], kc == KC - 1)
                        bbk = bank()
                        for kc in range(4):
                            mm(psb(bbk), bsl[:, k * 4 + kc, mi * 128:(mi + 1) * 128], o_ap(k, kc), kc == 0, kc == 3, [bsB, o_buf(k, kc)], [PSB[bbk]], kc == 3)
                        sig = tmp((k * 2 + mi) % 4)
                        yacc = tmp(4 + mi)
                        fw.op("act", lambda h: h.activation(out=a32(sig), in_=psb(bgk), func=AF.Sigmoid, bias=ppc("bgate", (l * 4 + k) * 16 + m)),
                              [PSB[bgk], PPB], [ARB[sig]])
                        if k == 0:
                            fw.op("dve", lambda h: h.tensor_tensor(out=a32(yacc), in0=psb(bbk), in1=a32(sig), op=ALU.mult), [PSB[bbk], ARB[sig]], [ARB[yacc]])
                        else:
                            fw.op("dve", lambda h: h.tensor_tensor(out=a32(sig), in0=psb(bbk), in1=a32(sig), op=ALU.mult), [PSB[bbk], ARB[sig]], [ARB[sig]])
                            if k < 3:
                                fw.op("dve", lambda h: h.tensor_tensor(out=a32(yacc), in0=a32(yacc), in1=a32(sig), op=ALU.add), [ARB[yacc], ARB[sig]], [ARB[yacc]])
                            else:
                                ysl = 8 + m // 2
                                yv = abf(ysl)[:, (m % 2) * 512:(m % 2) * 512 + 512]
                                fw.op("dve", lambda h: h.tensor_tensor(out=yv, in0=a32(yacc), in1=a32(sig), op=ALU.add), [ARB[yacc], ARB[sig]], [ARB[ysl]])

            def y_rhs(kc):
                return abf(8 + kc // 2)[:, (kc % 2) * 512:(kc % 2) * 512 + 512]
            ybufs = [ARB[8 + i] for i in range(8)]
            for mg in range(8):
                bs = dense_fm(WO[l, :, mg * 256:(mg + 1) * 256], KC, y_rhs, ybufs, CVW["w_out"][l])
                for mi in range(2):
                    m = mg * 2 + mi
                    b = bs[mi]
                    fw.op("dve", lambda h: h.tensor_tensor(out=X[:, xp(m), :], in0=psb(b), in1=X[:, xp(m), :], op=ALU.add), [PSB[b], XB[xp(m)]], [XB[xp(m)]])

            norm_to_xn("gffn", l)

            def h_ap(fc):
                return abf(fc // 2)[:, (fc % 2) * 512:(fc % 2) * 512 + 512]

            def h_buf(fc):
                return ARB[fc // 2]
            nxt = tiles[ti + 1] if ti + 1 < len(tiles) else None
            for hf in range(2):
                for fg in range(11):
                    if hf == 0 and fg in (0, 5):
                        cv_issue(upto_layer=l + 1, count=1)
                    col0 = (hf * 11 + fg) * 256
                    s1, s1B = load_w(wpieces(W1[l, :, col0:col0 + 256], KC), CVW["w_ffn1"][l])
                    s3, s3B = load_w(wpieces(W3[l, :, col0:col0 + 256], KC), CVW["w_ffn3"][l])
                    for mi in range(2):
                        fc = fg * 2 + mi
                        b1 = bank()
                        for kc in range(KC):
                            mm(psb(b1), s1[:, kc, mi * 128:(mi + 1) * 128], XN[:, kc, :], kc == 0, kc == KC - 1, [s1B] + xnr(kc, kc == KC - 1), [PSB[b1]], kc == KC - 1)
                        b3 = bank()
                        for kc in range(KC):
                            mm(psb(b3), s3[:, kc, mi * 128:(mi + 1) * 128], XN[:, kc, :], kc == 0, kc == KC - 1, [s3B] + xnr(kc, kc == KC - 1), [PSB[b3]], kc == KC - 1)
                        sl = tmp(fc % 4)
                        fw.op("act", lambda h: h.activation(out=a32(sl), in_=psb(b1), func=AF.Silu), [PSB[b1]], [ARB[sl]])
                        fw.op("dve", lambda h: h.tensor_tensor(out=h_ap(fc), in0=psb(b3), in1=a32(sl), op=ALU.mult), [PSB[b3], ARB[sl]], [h_buf(fc)])
                hbufs = [ARB[i] for i in range(11)]
                early = nxt is not None and (nxt[1], nxt[2]) != (s, j)
                if hf == 1 and early:
                    for kc2 in (KC - 2, KC - 1):
                        load_x_chunk(nxt[0], nxt[1], nxt[2], kc2, next_phys(kc2))
                for mg in range(8):
                    r0 = hf * 22 * 128
                    sa, saB = load_w(wpieces(W2[l, r0:r0 + 11 * 128, mg * 256:(mg + 1) * 256], 11), CVW["w_ffn2"][l])
                    sb_, sbB = load_w(wpieces(W2[l, r0 + 11 * 128:r0 + 22 * 128, mg * 256:(mg + 1) * 256], 11), CVW["w_ffn2"][l])
                    bks = [bank(), bank()]
                    for si, (sl_, slB) in enumerate(((sa, saB), (sb_, sbB))):
                        for mi in range(2):
                            for kc in range(11):
                                fc = si * 11 + kc
                                last = (si == 1 and kc == 10)
                                mm(psb(bks[mi]), sl_[:, kc, mi * 128:(mi + 1) * 128], h_ap(fc), si == 0 and kc == 0, last,
                                   [slB] + hbufs, [PSB[bks[mi]]] if (last or (si == 0 and kc == 0)) else [], last or kc == 10)
                    for mi in range(2):
                        m = mg * 2 + mi
                        b = bks[mi]
                        fw.op("dve", lambda h: h.tensor_tensor(out=X[:, xp(m), :], in0=psb(b), in1=X[:, xp(m), :], op=ALU.add), [PSB[b], XB[xp(m)]], [XB[xp(m)]])
                        if hf == 1 and l < L - 1:
                            store_x_chunk(l, s, j, m, xp(m))
                            if nxt is not None and (m < KC - 2 or not early):
                                load_x_chunk(nxt[0], nxt[1], nxt[2], m, next_phys(m))
            if l == L - 1:
                rs = tmp(2)
                rms_stats(rs)
                for kc in range(KC):
                    fw.op("dve", lambda h: h.scalar_tensor_tensor(out=a32(kc), in0=X[:, xp(kc), :], scalar=ppc("gfin", kc), in1=a32(rs),
                                                                    op0=ALU.mult, op1=ALU.mult), [XB[xp(kc)], PPB, ARB[rs]], [ARB[kc]])
                    fw.dma("act", oT_d[s, kc * 128:(kc + 1) * 128, j * T:(j + 1) * T], a32(kc), ARB[kc], ORG[s][j][kc], sembuf=ARB[kc])
                    if nxt is not None and (kc < KC - 2 or not early):
                        load_x_chunk(nxt[0], nxt[1], nxt[2], kc, next_phys(kc))

        for kc in range(KC):
            load_x_chunk(0, 0, 0, kc, xp(kc))
        ti = 0
        for l in range(L):
            layer_prep(l)
            for s in range(NSEQ):
                seq_reset()
                for j in range(NT):
                    tile(ti)
                    rotate_x()
                    ti += 1
            cv_issue(upto_layer=l + 1)
        outb = [ORG[s][j][kc] for s in range(NSEQ) for j in range(NT) for kc in range(KC)]
        fw.finish("pool", outb)
        fw.finish("sp", outb)
        build.stats = (fw.nins, fw.nwait)
        build.sbuf_left = nc.sbuf_bytes_remaining
    return nc


def make_in_maps(inputs, L, nseq, ncores):
    rope, cst = host_consts()
    pp, mats, rows = host_params(inputs, L)
    x = np.asarray(inputs["x"], np.float32)
    common = {"pp": pp, "mats": mats, "rows": rows, "rope": rope, "cst": cst}
    for n in WNAMES:
        common[n] = np.ascontiguousarray(np.asarray(inputs[n], np.float32)[:L]).reshape(L, 128, -1)
    maps = []
    for c in range(ncores):
        m = dict(common)
        m["xT"] = np.ascontiguousarray(x[c * nseq:(c + 1) * nseq].transpose(0, 2, 1))
        maps.append(m)
    return maps


def kernel(**inputs):
    nc = build(DEPTH, SEQ_PER_CORE, S // T)
    maps = make_in_maps(inputs, DEPTH, SEQ_PER_CORE, NCORES)
    res = run_bass_kernel_spmd(nc, maps, core_ids=list(range(NCORES)))
    outs = [np.asarray(r["oT"], np.float32).transpose(0, 2, 1) for r in res.results]
    return np.ascontiguousarray(np.concatenate(outs, axis=0))
```
